# Optimizing a Trainium2 kernel written in Bass

```python
import math
import jax, jax.numpy as jnp
from jax import lax
import numpy as np

D_MODEL = 1024
BATCH = 32
SEQ = 2048
DEPTH = 4

N_MEM = 256
N_MIXERS = 3
N_CONV = (DEPTH + 2) // 3
N_SSD = (DEPTH + 1) // 3
N_DIFF = DEPTH // 3
DN_ALPHA = (2.0 * DEPTH) ** 0.25
DN_BETA = (8.0 * DEPTH) ** -0.25
LN_EPS = 1e-5
CONV_WIDTH = 31
MB_D_INNER = 2 * D_MODEL
MB_HEAD_DIM = 64
MB_HEADS = MB_D_INNER // MB_HEAD_DIM
MB_D_STATE = 128
MB_GROUPS = 8
MB_CONV = 4
MB_CHUNK = 128
MB_XBC = MB_D_INNER + 2 * MB_GROUPS * MB_D_STATE
MB_IN = MB_D_INNER + MB_XBC + MB_HEADS
DA_HEAD_DIM = 64
DA_HEADS = D_MODEL // (2 * DA_HEAD_DIM)
DA_V_DIM = 2 * DA_HEAD_DIM
ROPE_THETA = 500000.0
ROPE_DIM = DA_HEAD_DIM // 4
Q_BLOCK = 128
XA_HEADS = 4
XA_HEAD_DIM = D_MODEL // XA_HEADS
D_FF = ((8 * D_MODEL + 3 * 256 - 1) // (3 * 256)) * 256

kernel_name = "hybrid_conv_ssd_diffattn_deepnorm"


def layer_norm(x, g, b):
    xf = x.astype(jnp.float32)
    mu = jnp.mean(xf, -1, keepdims=True)
    var = jnp.mean(jnp.square(xf - mu), -1, keepdims=True)
    return ((xf - mu) * lax.rsqrt(var + LN_EPS) * g.astype(jnp.float32) + b.astype(jnp.float32)).astype(x.dtype)


def rms_norm(x, g, eps=1e-5):
    xf = x.astype(jnp.float32)
    y = xf * lax.rsqrt(jnp.mean(jnp.square(xf), -1, keepdims=True) + eps)
    return (y * g.astype(jnp.float32)).astype(x.dtype)


def post_norm(x, y, g, b):
    return layer_norm(DN_ALPHA * x + y.astype(x.dtype), g, b)


def causal_depthwise_conv(x, w, b):
    k, c = w.shape
    y = lax.conv_general_dilated(
        x, w[:, None, :].astype(x.dtype), window_strides=(1,), padding=[(k - 1, 0)],
        dimension_numbers=('NWC', 'WIO', 'NWC'), feature_group_count=c)
    return y + b.astype(x.dtype)


def conformer_conv(x, w_in, b_in, w_dw, b_dw, ln_g, ln_b, w_out, b_out):
    h = x @ w_in + b_in
    a, gate = jnp.split(h, 2, axis=-1)
    h = a * jax.nn.sigmoid(gate)
    h = causal_depthwise_conv(h, w_dw, b_dw)
    h = jax.nn.silu(layer_norm(h, ln_g, ln_b))
    return h @ w_out + b_out


def ssd_chunked_scan(xs, dt, a, bm, cm):
    b, s, g, hg, p = xs.shape
    n = bm.shape[-1]
    L = MB_CHUNK
    nc = s // L

    def to_chunks(t):
        return jnp.moveaxis(t.reshape((b, nc, L) + t.shape[2:]), 1, 0)

    causal = jnp.tril(jnp.ones((L, L), dtype=bool))

    def step(h, inp):
        xc, dtc, bc, cc = inp
        acum = jnp.cumsum(dtc * a, axis=1)
        seg = acum[:, :, None] - acum[:, None, :]
        decay = jnp.exp(jnp.where(causal[None, :, :, None, None], seg, -jnp.inf))
        cb = jnp.einsum('blgn,bsgn->blsg', cc, bc)
        wts = cb[..., None] * decay * dtc[:, None]
        y_diag = jnp.einsum('blsgh,bsghp->blghp', wts, xc)
        y_off = jnp.einsum('blgn,bghpn->blghp', cc, h) * jnp.exp(acum)[..., None]
        xw = xc * (jnp.exp(acum[:, -1:] - acum) * dtc)[..., None]
        h_new = h * jnp.exp(acum[:, -1])[..., None, None] + jnp.einsum('bsgn,bsghp->bghpn', bc, xw)
        return h_new, y_diag + y_off

    h0 = jnp.zeros((b, g, hg, p, n), jnp.float32)
    _, ys = lax.scan(step, h0, (to_chunks(xs), to_chunks(dt), to_chunks(bm), to_chunks(cm)))
    return jnp.moveaxis(ys, 0, 1).reshape(b, s, g, hg, p)


def ssd_mixer(x, w_in, w_conv, b_conv, dt_bias, a_log, d_skip, norm_g, w_out):
    b, s, _ = x.shape
    f32 = jnp.float32
    hpg = MB_HEADS // MB_GROUPS
    zxbcdt = x @ w_in
    z, xbc, dt = jnp.split(zxbcdt, [MB_D_INNER, MB_D_INNER + MB_XBC], axis=-1)
    xbc = jax.nn.silu(causal_depthwise_conv(xbc, w_conv, b_conv))
    xs, bm, cm = jnp.split(xbc, [MB_D_INNER, MB_D_INNER + MB_GROUPS * MB_D_STATE], axis=-1)
    xs = xs.reshape(b, s, MB_GROUPS, hpg, MB_HEAD_DIM).astype(f32)
    bm = bm.reshape(b, s, MB_GROUPS, MB_D_STATE).astype(f32)
    cm = cm.reshape(b, s, MB_GROUPS, MB_D_STATE).astype(f32)
    dt = jax.nn.softplus(dt.astype(f32) + dt_bias.astype(f32)).reshape(b, s, MB_GROUPS, hpg)
    a = -jnp.exp(a_log.astype(f32)).reshape(MB_GROUPS, hpg)
    y = ssd_chunked_scan(xs, dt, a, bm, cm)
    y = y + d_skip.astype(f32).reshape(MB_GROUPS, hpg)[..., None] * xs
    y = y.reshape(b, s, MB_D_INNER) * jax.nn.silu(z.astype(f32))
    y = y.reshape(b, s, MB_GROUPS, MB_D_INNER // MB_GROUPS)
    y = y * lax.rsqrt(jnp.mean(jnp.square(y), -1, keepdims=True) + 1e-5)
    y = y.reshape(b, s, MB_D_INNER) * norm_g.astype(f32)
    return y.astype(x.dtype) @ w_out


def rope_tables(positions):
    inv = ROPE_THETA ** (-jnp.arange(0, ROPE_DIM, 2, dtype=jnp.float32) / ROPE_DIM)
    ang = positions.astype(jnp.float32)[..., None] * inv
    return jnp.cos(ang)[:, :, None, :], jnp.sin(ang)[:, :, None, :]


def rope_partial(t, cos, sin):
    half = ROPE_DIM // 2
    t1, t2, tp = t[..., :half], t[..., half:ROPE_DIM], t[..., ROPE_DIM:]
    return jnp.concatenate([t1 * cos - t2 * sin, t2 * cos + t1 * sin, tp.astype(cos.dtype)], axis=-1).astype(t.dtype)


def diff_attention(x, cos, sin, w_qkv, lq1, lk1, lq2, lk2, subln_g, w_out, lambda_init):
    b, s, _ = x.shape
    f32 = jnp.float32
    q, k, v = jnp.split(x @ w_qkv, 3, axis=-1)
    q = rope_partial(q.reshape(b, s, 2 * DA_HEADS, DA_HEAD_DIM), cos, sin)
    k = rope_partial(k.reshape(b, s, 2 * DA_HEADS, DA_HEAD_DIM), cos, sin)
    q = q.reshape(b, s, DA_HEADS, 2, DA_HEAD_DIM) * (DA_HEAD_DIM ** -0.5)
    k = k.reshape(b, s, DA_HEADS, 2, DA_HEAD_DIM)
    v = v.reshape(b, s, DA_HEADS, DA_V_DIM)
    lam = (jnp.exp(jnp.sum(lq1.astype(f32) * lk1.astype(f32)))
           - jnp.exp(jnp.sum(lq2.astype(f32) * lk2.astype(f32))) + lambda_init)
    outs = []
    for i in range(s // Q_BLOCK):
        q0 = i * Q_BLOCK
        kend = q0 + Q_BLOCK
        sc = jnp.einsum('bqhcd,bkhcd->bhcqk', q[:, q0:kend], k[:, :kend]).astype(f32)
        mask = (q0 + jnp.arange(Q_BLOCK))[:, None] >= jnp.arange(kend)[None, :]
        p = jax.nn.softmax(jnp.where(mask, sc, -jnp.inf), axis=-1)
        pd = p[:, :, 0] - lam * p[:, :, 1]
        outs.append(jnp.einsum('bhqk,bkhe->bqhe', pd.astype(v.dtype), v[:, :kend]))
    o = jnp.concatenate(outs, axis=1)
    o = rms_norm(o, subln_g) * (1.0 - lambda_init)
    return o.reshape(b, s, D_MODEL) @ w_out


def memory_cross_attention(x, mem, w_q, w_kv, w_out):
    b, s, _ = x.shape
    m = mem.shape[1]
    q = (x @ w_q).reshape(b, s, XA_HEADS, XA_HEAD_DIM) * (XA_HEAD_DIM ** -0.5)
    k, v = jnp.split(mem @ w_kv, 2, axis=-1)
    k = k.reshape(b, m, XA_HEADS, XA_HEAD_DIM)
    v = v.reshape(b, m, XA_HEADS, XA_HEAD_DIM)
    sc = jnp.einsum('bshd,bmhd->bhsm', q, k).astype(jnp.float32)
    p = jax.nn.softmax(sc, axis=-1)
    o = jnp.einsum('bhsm,bmhd->bshd', p.astype(v.dtype), v)
    return o.reshape(b, s, D_MODEL) @ w_out


def swiglu(x, w_in, w_out):
    g, u = jnp.split(x @ w_in, 2, axis=-1)
    return (jax.nn.silu(g) * u) @ w_out


def setup_inputs(seed: int = 0) -> dict:
    key = jax.random.key(seed)
    f32 = jnp.float32
    counter = [0]

    def nxt():
        counter[0] += 1
        return jax.random.fold_in(key, counter[0])

    def nrm(shape, scale):
        return jax.random.normal(nxt(), shape, f32) * scale

    D = D_MODEL
    x = nrm((BATCH, SEQ, D), 1.0)
    mem = nrm((BATCH, N_MEM, D), 1.0)
    offs = jax.random.randint(nxt(), (BATCH, 1), 0, 1024, dtype=jnp.int32)
    positions = (jnp.arange(SEQ, dtype=jnp.int32)[None, :] + offs).astype(jnp.int32)
    cv_w_in = nrm((N_CONV, D, 2 * D), D ** -0.5)
    cv_b_in = nrm((N_CONV, 2 * D), 0.02)
    cv_w_dw = nrm((N_CONV, CONV_WIDTH, D), CONV_WIDTH ** -0.5)
    cv_b_dw = nrm((N_CONV, D), 0.02)
    cv_ln_g = 1.0 + nrm((N_CONV, D), 0.02)
    cv_ln_b = nrm((N_CONV, D), 0.02)
    cv_w_out = nrm((N_CONV, D, D), D ** -0.5 * DN_BETA)
    cv_b_out = nrm((N_CONV, D), 0.02)
    mb_w_in = nrm((N_SSD, D, MB_IN), D ** -0.5)
    mb_w_conv = nrm((N_SSD, MB_CONV, MB_XBC), MB_CONV ** -0.5)
    mb_b_conv = nrm((N_SSD, MB_XBC), 0.02)
    dt0 = jnp.exp(jax.random.uniform(nxt(), (N_SSD, MB_HEADS), f32, math.log(1e-3), math.log(1e-1)))
    mb_dt_bias = dt0 + jnp.log(-jnp.expm1(-dt0))
    mb_a_log = jnp.log(jax.random.uniform(nxt(), (N_SSD, MB_HEADS), f32, 1.0, 16.0))
    mb_d = 1.0 + nrm((N_SSD, MB_HEADS), 0.02)
    mb_norm_g = 1.0 + nrm((N_SSD, MB_D_INNER), 0.02)
    mb_w_out = nrm((N_SSD, MB_D_INNER, D), MB_D_INNER ** -0.5 * DN_BETA)
    da_w_qkv = nrm((N_DIFF, D, 3 * D), D ** -0.5)
    da_lq1 = nrm((N_DIFF, DA_HEAD_DIM), 0.1)
    da_lk1 = nrm((N_DIFF, DA_HEAD_DIM), 0.1)
    da_lq2 = nrm((N_DIFF, DA_HEAD_DIM), 0.1)
    da_lk2 = nrm((N_DIFF, DA_HEAD_DIM), 0.1)
    da_subln_g = 1.0 + nrm((N_DIFF, DA_V_DIM), 0.02)
    da_w_out = nrm((N_DIFF, D, D), D ** -0.5 * DN_BETA)
    xa_w_q = nrm((DEPTH, D, D), D ** -0.5)
    xa_w_kv = nrm((DEPTH, D, 2 * D), D ** -0.5)
    xa_w_out = nrm((DEPTH, D, D), D ** -0.5 * DN_BETA)
    ff_w_in = nrm((DEPTH, D, 2 * D_FF), D ** -0.5)
    ff_w_out = nrm((DEPTH, D_FF, D), D_FF ** -0.5 * DN_BETA)
    ln_g = 1.0 + nrm((DEPTH, 3, D), 0.02)
    ln_b = nrm((DEPTH, 3, D), 0.02)
    return {
        "x": x, "mem": mem, "positions": positions,
        "cv_w_in": cv_w_in, "cv_b_in": cv_b_in, "cv_w_dw": cv_w_dw, "cv_b_dw": cv_b_dw,
        "cv_ln_g": cv_ln_g, "cv_ln_b": cv_ln_b, "cv_w_out": cv_w_out, "cv_b_out": cv_b_out,
        "mb_w_in": mb_w_in, "mb_w_conv": mb_w_conv, "mb_b_conv": mb_b_conv,
        "mb_dt_bias": mb_dt_bias, "mb_a_log": mb_a_log, "mb_d": mb_d,
        "mb_norm_g": mb_norm_g, "mb_w_out": mb_w_out,
        "da_w_qkv": da_w_qkv, "da_lq1": da_lq1, "da_lk1": da_lk1, "da_lq2": da_lq2,
        "da_lk2": da_lk2, "da_subln_g": da_subln_g, "da_w_out": da_w_out,
        "xa_w_q": xa_w_q, "xa_w_kv": xa_w_kv, "xa_w_out": xa_w_out,
        "ff_w_in": ff_w_in, "ff_w_out": ff_w_out,
        "ln_g": ln_g, "ln_b": ln_b,
    }


def reference(x, mem, positions,
              cv_w_in, cv_b_in, cv_w_dw, cv_b_dw, cv_ln_g, cv_ln_b, cv_w_out, cv_b_out,
              mb_w_in, mb_w_conv, mb_b_conv, mb_dt_bias, mb_a_log, mb_d, mb_norm_g, mb_w_out,
              da_w_qkv, da_lq1, da_lk1, da_lq2, da_lk2, da_subln_g, da_w_out,
              xa_w_q, xa_w_kv, xa_w_out,
              ff_w_in, ff_w_out,
              ln_g, ln_b):
    cos, sin = rope_tables(positions)
    for i in range(DEPTH):
        mixer, j = i % N_MIXERS, i // N_MIXERS
        if mixer == 0:
            y = conformer_conv(x, cv_w_in[j], cv_b_in[j], cv_w_dw[j], cv_b_dw[j],
                               cv_ln_g[j], cv_ln_b[j], cv_w_out[j], cv_b_out[j])
        elif mixer == 1:
            y = ssd_mixer(x, mb_w_in[j], mb_w_conv[j], mb_b_conv[j], mb_dt_bias[j],
                          mb_a_log[j], mb_d[j], mb_norm_g[j], mb_w_out[j])
        else:
            lambda_init = 0.8 - 0.6 * math.exp(-0.3 * i)
            y = diff_attention(x, cos, sin, da_w_qkv[j], da_lq1[j], da_lk1[j], da_lq2[j],
                               da_lk2[j], da_subln_g[j], da_w_out[j], lambda_init)
        x = post_norm(x, y, ln_g[i, 0], ln_b[i, 0])
        x = post_norm(x, memory_cross_attention(x, mem, xa_w_q[i], xa_w_kv[i], xa_w_out[i]),
                      ln_g[i, 1], ln_b[i, 1])
        x = post_norm(x, swiglu(x, ff_w_in[i], ff_w_out[i]), ln_g[i, 2], ln_b[i, 2])
    return x
```

```python
import contextlib
import numpy as np
import concourse.bass as bass
import concourse.mybir as mybir
from concourse.bass_utils import run_bass_kernel_spmd

F32 = mybir.dt.float32
BF16 = mybir.dt.bfloat16
I32 = mybir.dt.int32
ALU = mybir.AluOpType
AF = mybir.ActivationFunctionType
AX = mybir.AxisListType

ENGS = ("pe", "act", "dve", "pool", "sp")
EPOCH = 20000


class Buf:
    __slots__ = ("name", "w", "r", "dsem", "excl")

    def __init__(self, name, alias=None):
        self.name = name
        self.excl = False
        self.w = None
        self.r = dict(alias or {})
        self.dsem = None


class Tile:
    __slots__ = ("t", "b")

    def __init__(self, t, b):
        self.t = t
        self.b = b


def pipeline(n, stages, lag=1):
    ns = len(stages)
    for step in range(n + (ns - 1) * lag):
        for si, fn in enumerate(stages):
            c = step - si * lag
            if 0 <= c < n:
                fn(c)


class Rot:
    def __init__(self, items):
        self.items = list(items)
        self.i = 0

    def next(self):
        x = self.items[self.i % len(self.items)]
        self.i += 1
        return x


class Sched:
    def __init__(self, nc):
        self.nc = nc
        self.stack = contextlib.ExitStack()
        self.scopes = []
        self.alias = {}
        self.prog = {e: [] for e in ENGS}
        self.cnt = {e: 0 for e in ENGS}
        self.epoch = {e: 0 for e in ENGS}
        self.seen = {e: {} for e in ENGS}
        self.semh = {}
        self.dval = {}
        self.finals = {}
        self.nsem = 0
        self.uid = 0
        self.free_dsems = []
        for e in ENGS:
            self._newsem((e, 0))

    def _newsem(self, key):
        self.nsem += 1
        self.semh[key] = self.stack.enter_context(self.nc.semaphore("s%d" % self.nsem))
        return key

    @contextlib.contextmanager
    def scope(self):
        st = contextlib.ExitStack()
        self.scopes.append((st, []))
        try:
            yield
        finally:
            st2, bufs = self.scopes.pop()
            for b in bufs:
                deps = dict(b.r)
                if b.w is not None:
                    deps[b.w[0]] = max(deps.get(b.w[0], 0), b.w[1])
                for k, v in deps.items():
                    if self.alias.get(k, 0) < v:
                        self.alias[k] = v
                if b.dsem is not None:
                    self.free_dsems.append(b.dsem)
                    b.dsem = None
            st2.close()

    def alloc(self, name, shape, dtype):
        self.uid += 1
        st = self.scopes[-1][0] if self.scopes else self.stack
        return st.enter_context(self.nc.sbuf_tensor("%s_%d" % (name, self.uid), list(shape), dtype))

    def buf(self, name):
        b = Buf(name, self.alias)
        if self.scopes:
            self.scopes[-1][1].append(b)
        return b

    def tile(self, name, shape, dtype):
        return Tile(self.alloc(name, shape, dtype), self.buf(name))

    def psum_banks(self):
        out = []
        for i in range(8):
            t = self.stack.enter_context(self.nc.psum_tensor("ps%d" % i, [128, 512], F32))
            out.append(Tile(t, Buf("ps%d" % i)))
            out[-1].b.excl = True
        return out

    def _waits(self, eng, reads, writes):
        waits = {}
        seen = self.seen[eng]

        def need(k, v):
            if seen.get(k, 0) >= v:
                return
            if waits.get(k, 0) < v:
                waits[k] = v
        for b in reads:
            if b.w is not None:
                need(*b.w)
            if b.excl:
                for k, v in b.r.items():
                    if k[0] != eng:
                        need(k, v)
        for b in writes:
            if b.w is not None:
                need(*b.w)
            for k, v in b.r.items():
                need(k, v)
        if eng == "pe":
            for k in [k for k in waits if k[0] == "pe"]:
                del waits[k]
        for k, v in waits.items():
            seen[k] = v
        return list(waits.items())

    def op(self, eng, fn, reads=(), writes=()):
        waits = self._waits(eng, reads, writes)
        if self.cnt[eng] >= EPOCH:
            self.epoch[eng] += 1
            self.cnt[eng] = 0
            self._newsem((eng, self.epoch[eng]))
        self.cnt[eng] += 1
        key = (eng, self.epoch[eng])
        c = self.cnt[eng]
        self.prog[eng].append((waits, fn, key, 1))
        for b in reads:
            b.r[key] = c
        for b in writes:
            b.w = (key, c)
            b.r = {}

    def dma(self, q, out_ap, in_ap, reads=(), writes=(), out_final=False, **kw):
        waits = self._waits(q, reads, writes)
        owner = writes[0] if writes else reads[0]
        if owner.dsem is None:
            if self.free_dsems:
                owner.dsem = self.free_dsems.pop()
            else:
                owner.dsem = self._newsem(("d", self.nsem))
                self.dval[owner.dsem] = 0
        key = owner.dsem
        self.dval[key] += 16
        v = self.dval[key]
        self.prog[q].append((waits, lambda e: e.dma_start(out=out_ap, in_=in_ap, **kw), key, 16))
        for b in reads:
            b.r[key] = v
        for b in writes:
            b.w = (key, v)
            b.r = {}
        if out_final:
            self.finals[key] = v

    def finish(self):
        fw = list(self.finals.items())
        nc = self.nc
        with nc.Block() as block:
            def replay(name, e, tail=()):
                for waits, fn, key, inc in self.prog[name]:
                    for k, v in waits:
                        e.wait_ge(self.semh[k], v)
                    fn(e).then_inc(self.semh[key], inc)
                for k, v in tail:
                    e.wait_ge(self.semh[k], v)

            @block.tensor
            def _(e):
                replay("pe", e)

            @block.scalar
            def _(e):
                replay("act", e)

            @block.vector
            def _(e):
                replay("dve", e)

            @block.gpsimd
            def _(e):
                replay("pool", e)

            @block.sync
            def _(e):
                replay("sp", e, fw)
        self.stack.close()


D = 1024
SEQ = 2048
NT = 16
NTB = 4
NMEM = 256
DFF = 2816
DEPTH = 4
ALPHA = (2.0 * DEPTH) ** 0.25
LN_EPS = 1e-5
CONVW = 31
FF_GROUPS = [(0, 6), (6, 12), (12, 17), (17, 22)]
DEBUG = None
DSTOP = 99
VARIANT = 0
NCST = 8
LAMBDA_INIT = {2: 0.8 - 0.6 * float(np.exp(-0.3 * 2))}
ROPE_THETA = 500000.0

CV_BIN, CV_WDW, CV_BDW, CV_LNG, CV_LNB, CV_N = 0, 16, 16 + 248, 16 + 256, 16 + 264, 16 + 272


class Builder:
    def __init__(self, nseq, stages):
        self.nseq = nseq
        self.stages = stages
        nc = self.nc = bass.Bass("TRN2", target_bir_lowering=False)
        S = self.S = Sched(nc)

        def din(name, shape, dt=F32):
            return nc.dram_tensor(name, list(shape), dt, kind="ExternalInput").ap()
        self.dx = din("x", [nseq, SEQ, D])
        self.dmem = din("mem", [nseq, NMEM, D])
        self.dpos = din("pos", [nseq, SEQ], I32)
        self.dcst = din("cst", [128, NCST * 128])
        self.dcstf = din("cstf", [128, 16])
        self.dcst32 = din("cst32", [128, 384])
        self.mb_w_in = din("mb_w_in", [1, D, 6176])
        self.mb_w_out = din("mb_w_out", [1, 2 * D, D])
        self.mb_cols = din("mb_cols", [1, 128, 160])
        self.mb_hv = din("mb_hv", [1, 96])
        self.mb_norm_g = din("mb_norm_g", [1, 2 * D])
        self.mb_ngcol = din("mb_ngcol", [1, 128, 16])
        self.da_w_qkv = din("da_w_qkv", [1, D, 3 * D])
        self.da_w_out = din("da_w_out", [1, D, D])
        self.da_lqk = din("da_lqk", [4, 64])
        self.da_subln_g = din("da_subln_g", [1, 128])
        self.cv_w_in = din("cv_w_in", [2, D, 2 * D])
        self.cv_w_out = din("cv_w_out", [2, D, D])
        self.cv_b_out = din("cv_b_out", [2, D])
        self.cv_cols = din("cv_cols", [2, 128, CV_N])
        self.xa_w_q = din("xa_w_q", [4, D, D])
        self.xa_w_kv = din("xa_w_kv", [4, D, 2 * D])
        self.xa_w_out = din("xa_w_out", [4, D, D])
        self.ff_w_in = din("ff_w_in", [4, D, 2 * DFF])
        self.ff_w_out = din("ff_w_out", [4, DFF, D])
        self.ln_g = din("ln_g", [4, 3, D])
        self.ln_b = din("ln_b", [4, 3, D])
        self.dout = nc.dram_tensor("out", [nseq, SEQ, D], F32, kind="ExternalOutput").ap()
        self.dbg = nc.dram_tensor("dbg", [128, 8, SEQ], F32, kind="ExternalOutput").ap() if DEBUG else None

        self.X = S.alloc("X", [128, NT, D], F32)
        self.Xb = [S.buf("X%d" % t) for t in range(NT)]
        self.XT = S.alloc("XT", [128, 8, SEQ], BF16)
        self.XTb = [S.buf("XT%d" % t) for t in range(NT)]
        self.G = S.tile("G", [128, D], F32)
        self.Bt = S.tile("Bt", [128, D], F32)
        self.cst = S.tile("cst", [128, NCST * 128], BF16)
        cb = lambda i: self.cst.t[:, i * 128:(i + 1) * 128]
        self.ident, self.ones, self.onesD, self.prot, self.sel0, self.sel1, self.negm = [cb(i) for i in range(7)]
        self.cstf = S.tile("cstf", [128, 16], F32)
        S.dma("sp", self.cstf.t[:], self.dcstf, writes=[self.cstf.b])
        self.rot_banks = list(range(8))
        self.cst32 = S.tile("cst32", [128, 384], F32)
        S.dma("sp", self.cst32.t[:], self.dcst32, writes=[self.cst32.b])
        self.tri32 = self.cst32.t[:, 0:128]
        self.ones32 = self.cst32.t[:, 128:256]
        self.ident32 = self.cst32.t[:, 256:384]
        self.banks = S.psum_banks()
        self.bank_i = 0
        self.xbR = Rot([S.tile("xb%d" % i, [128, D], BF16) for i in range(2)])
        self.stR = Rot([S.tile("st%d" % i, [128, 2, 6], F32) for i in range(3)])
        self.mvR = Rot([S.tile("mv%d" % i, [128, 4], F32) for i in range(3)])
        S.dma("pool", self.cst.t[:], self.dcst, writes=[self.cst.b])

    def ps(self):
        b = self.banks[self.rot_banks[self.bank_i % len(self.rot_banks)]]
        self.bank_i += 1
        return b

    def mm(self, out, lhsT, rhs, start, stop, reads, writes, sgc=False):
        self.S.op("pe", lambda e: e.matmul(out, lhsT, rhs, start=start, stop=stop, skip_group_check=sgc), reads=reads, writes=writes)

    def xt_reads(self, tb):
        return self.XTb[4 * tb:4 * tb + 4]

    def load_x(self, b):
        S = self.S
        for t in range(NT):
            S.dma("sp", self.X[:, t, :], self.dx[b, t * 128:(t + 1) * 128, :], writes=[self.Xb[t]])

    def store_x(self, b, t):
        self.S.dma("sp", self.dout[b, t * 128:(t + 1) * 128, :], self.X[:, t, :], reads=[self.Xb[t]], out_final=True)

    def make_xt(self, t):
        S = self.S
        xb = self.xbR.next()
        X = self.X
        S.op("act", lambda e: e.activation(xb.t[:], X[:, t, :], AF.Copy), reads=[self.Xb[t]], writes=[xb.b])
        self.xt_from(t, xb)

    def xt_from(self, t, xb):
        S = self.S
        p = self.ps()
        pv = p.t[:].bitcast(BF16)
        for k in range(8):
            S.op("pe", lambda e, k=k: e.transpose(pv[:, k * 128:(k + 1) * 128], xb.t[:, k * 128:(k + 1) * 128], self.ident),
                 reads=[xb.b, self.cst.b], writes=[p.b])
        XT = self.XT
        S.op("act", lambda e: e.activation(XT[:, :, t * 128:(t + 1) * 128], pv.rearrange("p (k n) -> p k n", k=8), AF.Copy),
             reads=[p.b], writes=[self.XTb[t]])

    def load_ln(self, li, sub):
        S = self.S
        S.dma("sp", self.G.t[:], self.ln_g[li, sub:sub + 1, :].broadcast_to([128, D]), writes=[self.G.b])
        S.dma("sp", self.Bt.t[:], self.ln_b[li, sub:sub + 1, :].broadcast_to([128, D]), writes=[self.Bt.b])

    def norm_tile_a(self, t):
        S = self.S
        X = self.X
        Xb = self.Xb[t]
        st = self.stR.next()
        mv = self.mvR.next()
        for c in range(2):
            S.op("dve", lambda e, c=c: e.bn_stats(st.t[:, c, :], X[:, t, c * 512:(c + 1) * 512]), reads=[Xb], writes=[st.b])
        S.op("dve", lambda e: e.bn_aggr(mv.t[:, 0:2], st.t[:]), reads=[st.b], writes=[mv.b])
        S.op("act", lambda e: e.activation(mv.t[:, 2:3], mv.t[:, 1:2], AF.Sqrt, bias=LN_EPS, scale=1.0), reads=[mv.b], writes=[mv.b])
        return mv

    def norm_tile_b(self, t, mv, store_b=None):
        S = self.S
        X = self.X
        Xb = self.Xb[t]
        S.op("dve", lambda e: e.reciprocal(mv.t[:, 2:3], mv.t[:, 2:3]), reads=[mv.b], writes=[mv.b])
        S.op("dve", lambda e: e.scalar_tensor_tensor(X[:, t, :], X[:, t, :], mv.t[:, 0:1], self.G.t[:], ALU.subtract, ALU.mult),
             reads=[Xb, mv.b, self.G.b], writes=[Xb])
        S.op("dve", lambda e: e.scalar_tensor_tensor(X[:, t, :], X[:, t, :], mv.t[:, 2:3], self.Bt.t[:], ALU.mult, ALU.add),
             reads=[Xb, mv.b, self.Bt.b], writes=[Xb])
        if store_b is not None:
            self.store_x(store_b, t)

    def outproj_norm(self, W, Wb, nk, lhs_fn, first, last, bias, want_xt, store_b, rowscale=None):
        S = self.S
        X = self.X
        xbs = {}

        def s_mm(t):
            pp = [self.ps(), self.ps()]
            for h in range(2):
                for k in range(nk):
                    lap, lb = lhs_fn(k, t)
                    self.mm(pp[h].t[:, :], lap, W[:, k, h * 512:(h + 1) * 512], k == 0, k == nk - 1,
                            reads=list(lb) + list(Wb), writes=[pp[h].b])
            if rowscale is not None and first:
                S.op("act", lambda e: e.activation(X[:, t, :], X[:, t, :], AF.Copy, scale=ALPHA), reads=[self.Xb[t]], writes=[self.Xb[t]])
            for h in range(2):
                xs = X[:, t, h * 512:(h + 1) * 512]
                if rowscale is not None:
                    S.op("dve", lambda e, xs=xs, p=pp[h]: e.scalar_tensor_tensor(xs, p.t[:, :], rowscale.t[:, t:t + 1], xs, ALU.mult, ALU.add),
                         reads=[self.Xb[t], pp[h].b, rowscale.b], writes=[self.Xb[t]])
                elif first:
                    S.op("dve", lambda e, xs=xs, p=pp[h]: e.scalar_tensor_tensor(xs, xs, ALPHA, p.t[:, :], ALU.mult, ALU.add),
                         reads=[self.Xb[t], pp[h].b], writes=[self.Xb[t]])
                else:
                    S.op("dve", lambda e, xs=xs, p=pp[h]: e.tensor_tensor(xs, xs, p.t[:, :], ALU.add),
                         reads=[self.Xb[t], pp[h].b], writes=[self.Xb[t]])
            if last and bias is not None:
                S.op("dve", lambda e, t=t: e.tensor_tensor(X[:, t, :], X[:, t, :], bias.t[:], ALU.add),
                     reads=[self.Xb[t], bias.b], writes=[self.Xb[t]])

        mvs = {}

        def s_ln(t):
            mvs[t] = self.norm_tile_a(t)

        def s_ln2(t):
            self.norm_tile_b(t, mvs.pop(t), store_b)
            if want_xt:
                xb = self.xbR.next()
                xbs[t] = xb
                S.op("act", lambda e: e.activation(xb.t[:], X[:, t, :], AF.Copy), reads=[self.Xb[t]], writes=[xb.b])

        def s_xt(t):
            self.xt_from(t, xbs.pop(t))

        if not last:
            pipeline(NT, [s_mm])
        elif want_xt:
            pipeline(NT, [s_mm, s_ln, s_ln2, s_xt])
        else:
            pipeline(NT, [s_mm, s_ln, s_ln2])

    def wload(self, dst_ap, src_ap, buf):
        self.S.dma("pool", dst_ap, src_ap, writes=[buf])

    def ffn(self, li, want_xt, store_b):
        S = self.S
        self.load_ln(li, 2)
        with S.scope():
            HT = S.alloc("HT", [128, 6, SEQ], BF16)
            HTb = [[S.buf("HT%d_%d" % (j, tb)) for tb in range(NTB)] for j in range(6)]
            WOr = Rot([S.tile("WO%d" % i, [128, 6, D], BF16) for i in range(2)])
            WIr = Rot([S.tile("WI%d" % i, [128, 8, 2, 128], BF16) for i in range(3)])
            sgR = Rot([S.tile("sg%d" % i, [128, 512], F32) for i in range(2)])
            win = self.ff_w_in[li].rearrange("(k p) n -> p k n", p=128)
            for gi, (c0, c1) in enumerate(FF_GROUPS):
                n = c1 - c0
                WO = WOr.next()
                self.wload(WO.t[:, 0:n, :], self.ff_w_out[li, c0 * 128:c1 * 128, :].rearrange("(k p) n -> p k n", p=128), WO.b)
                for jl in range(n):
                    j = c0 + jl
                    WI = WIr.next()
                    self.wload(WI.t[:, :, 0, :], win[:, :, j * 128:(j + 1) * 128], WI.b)
                    self.wload(WI.t[:, :, 1, :], win[:, :, DFF + j * 128:DFF + (j + 1) * 128], WI.b)
                    for tb in range(NTB):
                        pg, pu = self.ps(), self.ps()
                        for which, p in ((0, pg), (1, pu)):
                            for k in range(8):
                                self.mm(p.t[:, :], WI.t[:, k, which, :], self.XT[:, k, tb * 512:(tb + 1) * 512], k == 0, k == 7,
                                        reads=[WI.b] + self.xt_reads(tb), writes=[p.b])
                        sg = sgR.next()
                        S.op("act", lambda e, sg=sg, pg=pg: e.activation(sg.t[:], pg.t[:, :], AF.Silu), reads=[pg.b], writes=[sg.b])
                        S.op("dve", lambda e, sg=sg, pu=pu, jl=jl, tb=tb: e.tensor_tensor(HT[:, jl, tb * 512:(tb + 1) * 512], sg.t[:], pu.t[:, :], ALU.mult),
                             reads=[sg.b, pu.b], writes=[HTb[jl][tb]])
                last = gi == len(FF_GROUPS) - 1
                self.outproj_norm(WO.t, [WO.b], n, lambda k, t: (HT[:, k, t * 128:(t + 1) * 128], [HTb[k][t // 4]]),
                                  first=(gi == 0), last=last, bias=None, want_xt=want_xt, store_b=store_b)

    def prep_mem(self, b):
        S = self.S
        self.memT = S.tile("memT", [128, 8, NMEM], BF16)
        self.memf = [S.tile("memf%d" % mt, [128, D], F32) for mt in range(2)]
        if True:
            for mt in range(2):
                mf = self.memf[mt]
                S.dma("sp", mf.t[:], self.dmem[b, mt * 128:(mt + 1) * 128, :], writes=[mf.b])
                xb = self.xbR.next()
                S.op("act", lambda e, xb=xb, mf=mf: e.activation(xb.t[:], mf.t[:], AF.Copy), reads=[mf.b], writes=[xb.b])
                p = self.ps()
                pv = p.t[:].bitcast(BF16)
                for k in range(8):
                    S.op("pe", lambda e, k=k, pv=pv, xb=xb: e.transpose(pv[:, k * 128:(k + 1) * 128], xb.t[:, k * 128:(k + 1) * 128], self.ident),
                         reads=[xb.b, self.cst.b], writes=[p.b])
                S.op("dve", lambda e, pv=pv, mt=mt: e.tensor_copy(self.memT.t[:, :, mt * 128:(mt + 1) * 128], pv.rearrange("p (k n) -> p k n", k=8)),
                     reads=[p.b], writes=[self.memT.b])

    def colnorm_max(self, srcs, ncols, out_col, tmpR):
        S = self.S
        p = self.ps()
        n = len(srcs)
        for i, (ap, bufs) in enumerate(srcs):
            sq = tmpR.next()
            S.op("act", lambda e, sq=sq, ap=ap: e.activation(sq.t[:, 0:ncols], ap, AF.Square), reads=bufs, writes=[sq.b])
            self.mm(p.t[:, 0:ncols], self.ones, sq.t[:, 0:ncols], i == 0, i == n - 1, reads=[sq.b, self.cst.b], writes=[p.b])
        S.op("dve", lambda e: e.reduce_max(out_col[0], p.t[:, 0:ncols], AX.X), reads=[p.b], writes=[out_col[1]])

    def xattn(self, li, b):
        S = self.S
        self.load_ln(li, 1)
        XT = self.XT
        scale = 1.0 / 16.0
        with S.scope():
            memT = S.tile("memT", [128, 8, NMEM], BF16)
            KT = S.tile("KT", [128, 8, NMEM], BF16)
            V = S.tile("V", [128, 2, D], BF16)
            OT = S.alloc("OT", [128, 8, SEQ], BF16)
            OTb = [[S.buf("OT%d_%d" % (c, tb)) for tb in range(NTB)] for c in range(8)]
            sqR = Rot([S.tile("sq%d" % i, [128, 512], BF16) for i in range(2)])
            nrm = S.tile("nrm", [128, 16], F32)
            kmx = S.tile("kmx", [128, 4], F32)
            nbs = [S.tile("nbx%d" % i, [128, 1], F32) for i in range(4)]
            WQr = Rot([S.tile("WQ%d" % i, [128, 8, 256], BF16) for i in range(2)])
            QTs = [S.tile("QT%d" % i, [128, 2, SEQ], BF16) for i in range(2)]
            wq = self.xa_w_q[li].rearrange("(k p) n -> p k n", p=128)
            WQs = {}

            def load_wq(h):
                WQ = WQr.next()
                self.wload(WQ.t[:], wq[:, :, h * 256:(h + 1) * 256], WQ.b)
                WQs[h] = WQ

            def qproj(h):
                WQ = WQs.pop(h)
                QT = QTs[h % 2]
                if h + 1 < 4:
                    load_wq(h + 1)
                for tb in range(NTB):
                    for c in range(2):
                        p = self.ps()
                        for k in range(8):
                            self.mm(p.t[:, :], WQ.t[:, k, c * 128:(c + 1) * 128], XT[:, k, tb * 512:(tb + 1) * 512], k == 0, k == 7,
                                    reads=[WQ.b] + self.xt_reads(tb), writes=[p.b])
                        S.op("act", lambda e, p=p, c=c, tb=tb: e.activation(QT.t[:, c, tb * 512:(tb + 1) * 512], p.t[:, :], AF.Copy), reads=[p.b], writes=[QT.b])
                for tb in range(NTB):
                    self.colnorm_max([(QT.t[:, c, tb * 512:(tb + 1) * 512], [QT.b]) for c in range(2)], 512, (nrm.t[:, 4 + tb:5 + tb], nrm.b), sqR)

            def qbias(h):
                S.op("dve", lambda e: e.reduce_max(nrm.t[:, 12:13], nrm.t[:, 4:8], AX.X), reads=[nrm.b], writes=[nrm.b])
                S.op("dve", lambda e: e.tensor_tensor(nrm.t[:, 12:13], nrm.t[:, 12:13], kmx.t[:, h:h + 1], ALU.mult), reads=[nrm.b, kmx.b], writes=[nrm.b])
                S.op("act", lambda e: e.activation(nrm.t[:, 13:14], nrm.t[:, 12:13], AF.Sqrt), reads=[nrm.b], writes=[nrm.b])
                S.op("dve", lambda e: e.tensor_scalar(nbs[h].t[:, 0:1], nrm.t[:, 13:14], -scale, None, ALU.mult), reads=[nrm.b], writes=[nbs[h].b])

            load_wq(0)
            with S.scope():
                KVr = Rot([S.tile("KV%d" % i, [128, 8, 512], BF16) for i in range(2)])
                wkv = self.xa_w_kv[li].rearrange("(k p) n -> p k n", p=128)
                KVs = []
                for q4 in range(2):
                    W = KVr.next()
                    self.wload(W.t[:], wkv[:, :, q4 * 512:(q4 + 1) * 512], W.b)
                    KVs.append(W)
                memf = [S.tile("memf%d" % mt, [128, D], F32) for mt in range(2)]
                for mt in range(2):
                    mf = memf[mt]
                    S.dma("sp", mf.t[:], self.dmem[b, mt * 128:(mt + 1) * 128, :], writes=[mf.b])
                qproj(0)
                for mt in range(2):
                    mf = memf[mt]
                    xb = self.xbR.next()
                    S.op("act", lambda e, xb=xb, mf=mf: e.activation(xb.t[:], mf.t[:], AF.Copy), reads=[mf.b], writes=[xb.b])
                    p = self.ps()
                    pv = p.t[:].bitcast(BF16)
                    for k in range(8):
                        S.op("pe", lambda e, k=k, pv=pv, xb=xb: e.transpose(pv[:, k * 128:(k + 1) * 128], xb.t[:, k * 128:(k + 1) * 128], self.ident),
                             reads=[xb.b, self.cst.b], writes=[p.b])
                    S.op("dve", lambda e, pv=pv, mt=mt: e.tensor_copy(memT.t[:, :, mt * 128:(mt + 1) * 128], pv.rearrange("p (k n) -> p k n", k=8)),
                         reads=[p.b], writes=[memT.b])
                for q4 in range(4):
                    W = KVs[q4]
                    if q4 < 2:
                        for c in range(4):
                            p = self.ps()
                            for k in range(8):
                                self.mm(p.t[:, 0:NMEM], W.t[:, k, c * 128:(c + 1) * 128], memT.t[:, k, :], k == 0, k == 7,
                                        reads=[W.b, memT.b], writes=[p.b])
                            S.op("act", lambda e, p=p, cc=q4 * 4 + c: e.activation(KT.t[:, cc, :], p.t[:, 0:NMEM], AF.Copy), reads=[p.b], writes=[KT.b])
                    else:
                        for mt in range(2):
                            p = self.ps()
                            for k in range(8):
                                self.mm(p.t[:, :], memT.t[:, k, mt * 128:(mt + 1) * 128], W.t[:, k, :], k == 0, k == 7,
                                        reads=[W.b, memT.b], writes=[p.b])
                            S.op("act", lambda e, p=p, mt=mt, h2=q4 - 2: e.activation(V.t[:, mt, h2 * 512:(h2 + 1) * 512], p.t[:, :], AF.Copy), reads=[p.b], writes=[V.b])
                    if q4 + 2 < 4:
                        W2 = KVr.next()
                        self.wload(W2.t[:], wkv[:, :, (q4 + 2) * 512:(q4 + 3) * 512], W2.b)
                        KVs.append(W2)
            for h in range(4):
                self.colnorm_max([(KT.t[:, 2 * h + c, :], [KT.b]) for c in range(2)], NMEM, (kmx.t[:, h:h + 1], kmx.b), sqR)
            qbias(0)
            WO = S.tile("WOx", [128, 8, D], BF16)
            self.wload(WO.t[:], self.xa_w_out[li].rearrange("(k p) n -> p k n", p=128), WO.b)
            PTr = Rot([S.tile("PT%d" % i, [128, 2, 512], BF16) for i in range(2)])
            rvR = Rot([S.tile("rv%d" % i, [128, 512], F32) for i in range(2)])

            def attend(h):
                QT = QTs[h % 2]
                stt = {}

                def A0(tb):
                    PT = PTr.next()
                    stt[tb] = PT
                    for mt in range(2):
                        p = self.ps()
                        for c in range(2):
                            self.mm(p.t[:, :], KT.t[:, 2 * h + c, mt * 128:(mt + 1) * 128], QT.t[:, c, tb * 512:(tb + 1) * 512], c == 0, c == 1,
                                    reads=[KT.b, QT.b], writes=[p.b])
                        S.op("act", lambda e, p=p, mt=mt: e.activation(PT.t[:, mt, :], p.t[:, :], AF.Exp, bias=nbs[h].t[:, 0:1], scale=scale),
                             reads=[p.b, nbs[h].b], writes=[PT.b])

                def A1(tb):
                    PT = stt.pop(tb)
                    prs = self.ps()
                    for mt in range(2):
                        self.mm(prs.t[:, :], self.ones, PT.t[:, mt, :], mt == 0, mt == 1, reads=[PT.b, self.cst.b], writes=[prs.b])
                    rv = rvR.next()
                    S.op("dve", lambda e: e.reciprocal(rv.t[:], prs.t[:, :]), reads=[prs.b], writes=[rv.b])
                    for c in range(2):
                        p = self.ps()
                        for mt in range(2):
                            self.mm(p.t[:, :], V.t[:, mt, h * 256 + c * 128:h * 256 + (c + 1) * 128], PT.t[:, mt, :], mt == 0, mt == 1,
                                    reads=[V.b, PT.b], writes=[p.b])
                        S.op("dve", lambda e, p=p, cc=2 * h + c: e.tensor_tensor(OT[:, cc, tb * 512:(tb + 1) * 512], p.t[:, :], rv.t[:], ALU.mult),
                             reads=[p.b, rv.b], writes=[OTb[2 * h + c][tb]])
                pipeline(NTB, [A0, A1])

            for h in range(4):
                if h + 1 < 4:
                    qproj(h + 1)
                attend(h)
                if h + 1 < 4:
                    qbias(h + 1)
            self.outproj_norm(WO.t, [WO.b], 8, lambda k, t: (OT[:, k, t * 128:(t + 1) * 128], [OTb[k][t // 4]]),
                              first=True, last=True, bias=None, want_xt=True, store_b=None)

    def conformer(self, li, j):
        S = self.S
        self.load_ln(li, 0)
        with S.scope():
            cols = S.tile("cvcols", [128, CV_N], F32)
            S.dma("sp", cols.t[:], self.cv_cols[j], writes=[cols.b])
            bout = S.tile("bout", [128, D], F32)
            S.dma("sp", bout.t[:], self.cv_b_out[j:j + 1, :].broadcast_to([128, D]), writes=[bout.b])
            WO = S.tile("WOc", [128, 8, D], BF16)
            self.wload(WO.t[:], self.cv_w_out[j].rearrange("(k p) n -> p k n", p=128), WO.b)
            CV = S.alloc("CV", [128, 8, SEQ], BF16)
            CVb = [[S.buf("CV%d_%d" % (c, tb)) for tb in range(NTB)] for c in range(8)]
            with S.scope():
                WIr = Rot([S.tile("WIc%d" % i, [128, 8, 2, 128], BF16) for i in range(2)])
                GLr = Rot([S.tile("GLU%d" % i, [128, CONVW - 1 + SEQ], BF16) for i in range(2)])
                DGr = Rot([S.tile("DG%d" % i, [128, CONVW, 128], BF16) for i in range(2)])
                sgR = Rot([S.tile("sgc%d" % i, [128, 512], F32) for i in range(2)])
                win = self.cv_w_in[j].rearrange("(k p) n -> p k n", p=128)
                H0 = CONVW - 1
                for c in range(8):
                    WI = WIr.next()
                    self.wload(WI.t[:, :, 0, :], win[:, :, c * 128:(c + 1) * 128], WI.b)
                    self.wload(WI.t[:, :, 1, :], win[:, :, D + c * 128:D + (c + 1) * 128], WI.b)
                    GL = GLr.next()
                    DG = DGr.next()
                    S.op("dve", lambda e, GL=GL: e.memset(GL.t[:, 0:H0], 0.0), writes=[GL.b])
                    for k in range(CONVW):
                        S.op("dve", lambda e, DG=DG, k=k, c=c: e.tensor_scalar(DG.t[:, k, :], self.ident, cols.t[:, CV_WDW + k * 8 + c:CV_WDW + k * 8 + c + 1], None, ALU.mult),
                             reads=[self.cst.b, cols.b], writes=[DG.b])
                    for tb in range(NTB):
                        pa, pg = self.ps(), self.ps()
                        for which, p in ((0, pa), (1, pg)):
                            for k in range(8):
                                self.mm(p.t[:, :], WI.t[:, k, which, :], self.XT[:, k, tb * 512:(tb + 1) * 512], k == 0, k == 7,
                                        reads=[WI.b] + self.xt_reads(tb), writes=[p.b])
                        sg = sgR.next()
                        S.op("act", lambda e, sg=sg, pg=pg, c=c: e.activation(sg.t[:], pg.t[:, :], AF.Sigmoid, bias=cols.t[:, CV_BIN + 8 + c:CV_BIN + 9 + c], scale=1.0),
                             reads=[pg.b, cols.b], writes=[sg.b])
                        S.op("dve", lambda e, sg=sg, pa=pa, GL=GL, c=c, tb=tb: e.scalar_tensor_tensor(GL.t[:, H0 + tb * 512:H0 + (tb + 1) * 512], pa.t[:, :], cols.t[:, CV_BIN + c:CV_BIN + c + 1], sg.t[:], ALU.add, ALU.mult),
                             reads=[pa.b, sg.b, cols.b], writes=[GL.b])
                    for tb in range(NTB):
                        p = self.ps()
                        for k in range(CONVW):
                            self.mm(p.t[:, :], DG.t[:, k, :], GL.t[:, k + tb * 512:k + (tb + 1) * 512], k == 0, k == CONVW - 1,
                                    reads=[DG.b, GL.b], writes=[p.b])
                        S.op("act", lambda e, p=p, c=c, tb=tb: e.activation(CV[:, c, tb * 512:(tb + 1) * 512], p.t[:, :], AF.Identity, bias=cols.t[:, CV_BDW + c:CV_BDW + c + 1], scale=1.0),
                             reads=[p.b, cols.b], writes=[CVb[c][tb]])
            if DEBUG == "conv":
                S.dma("pool", self.dbg, CV[:], reads=[b_ for r_ in CVb for b_ in r_], out_final=True)
            sqR = Rot([S.tile("sqc%d" % i, [128, 512], BF16) for i in range(2)])
            mr = Rot([S.tile("mr%d" % i, [128, 3, 512], F32) for i in range(2)])
            t1R = Rot([S.tile("t1c%d" % i, [128, 512], F32) for i in range(2)])
            for tb in range(NTB):
                pm, pq = self.ps(), self.ps()
                for c in range(8):
                    self.mm(pm.t[:, :], self.onesD, CV[:, c, tb * 512:(tb + 1) * 512], c == 0, c == 7, reads=[CVb[c][tb], self.cst.b], writes=[pm.b])
                for c in range(8):
                    sq = sqR.next()
                    S.op("act", lambda e, sq=sq, c=c, tb=tb: e.activation(sq.t[:], CV[:, c, tb * 512:(tb + 1) * 512], AF.Square), reads=[CVb[c][tb]], writes=[sq.b])
                    self.mm(pq.t[:, :], self.onesD, sq.t[:], c == 0, c == 7, reads=[sq.b, self.cst.b], writes=[pq.b])
                m = mr.next()
                S.op("act", lambda e, m=m, pm=pm: e.activation(m.t[:, 0, :], pm.t[:, :], AF.Square), reads=[pm.b], writes=[m.b])
                S.op("dve", lambda e, m=m, pq=pq: e.tensor_tensor(m.t[:, 1, :], pq.t[:, :], m.t[:, 0, :], ALU.subtract), reads=[pq.b, m.b], writes=[m.b])
                S.op("act", lambda e, m=m: e.activation(m.t[:, 1, :], m.t[:, 1, :], AF.Sqrt, bias=LN_EPS, scale=1.0), reads=[m.b], writes=[m.b])
                S.op("dve", lambda e, m=m: e.reciprocal(m.t[:, 1, :], m.t[:, 1, :]), reads=[m.b], writes=[m.b])
                S.op("dve", lambda e, m=m, pm=pm: e.scalar_tensor_tensor(m.t[:, 2, :], pm.t[:, :], -1.0, m.t[:, 1, :], ALU.mult, ALU.mult), reads=[pm.b, m.b], writes=[m.b])
                for c in range(8):
                    t1 = t1R.next()
                    S.op("dve", lambda e, t1=t1, m=m, c=c, tb=tb: e.tensor_tensor(t1.t[:], CV[:, c, tb * 512:(tb + 1) * 512], m.t[:, 1, :], ALU.mult), reads=[CVb[c][tb], m.b], writes=[t1.b])
                    S.op("dve", lambda e, t1=t1, m=m: e.tensor_tensor(t1.t[:], t1.t[:], m.t[:, 2, :], ALU.add), reads=[t1.b, m.b], writes=[t1.b])
                    S.op("act", lambda e, t1=t1, c=c, tb=tb: e.activation(CV[:, c, tb * 512:(tb + 1) * 512], t1.t[:], AF.Silu, bias=cols.t[:, CV_LNB + c:CV_LNB + c + 1], scale=cols.t[:, CV_LNG + c:CV_LNG + c + 1]),
                         reads=[t1.b, cols.b], writes=[CVb[c][tb]])
            if DEBUG == "z":
                S.dma("pool", self.dbg, CV[:], reads=[b_ for r_ in CVb for b_ in r_], out_final=True)
            self.outproj_norm(WO.t, [WO.b], 8, lambda k, t: (CV[:, k, t * 128:(t + 1) * 128], [CVb[k][t // 4]]),
                              first=True, last=True, bias=bout, want_xt=True, store_b=None)

    def diffattn(self, li, j, b):
        S = self.S
        self.load_ln(li, 0)
        lam0 = LAMBDA_INIT[li]
        PI = float(np.pi)
        cf = self.cstf
        with S.scope():
            OT = S.alloc("OTd", [128, 8, SEQ], BF16)
            OTb = [[S.buf("OTd%d_%d" % (c, tb)) for tb in range(NTB)] for c in range(8)]
            WO = S.tile("WOd", [128, 8, D], BF16)
            self.wload(WO.t[:], self.da_w_out[j].rearrange("(k p) n -> p k n", p=128), WO.b)
            COS = S.tile("COS", [128, SEQ], BF16)
            SIN = S.tile("SIN", [128, SEQ], BF16)
            sm = S.tile("dsm", [128, 32], F32)
            gsc = S.tile("gsc", [128, 128], F32)
            with S.scope():
                lq = S.tile("lqk", [128, 4, 64], F32)
                S.dma("sp", lq.t[:].rearrange("p a d -> p (a d)"), self.da_lqk.rearrange("a d -> (a d)").partition_broadcast(128), writes=[lq.b])
                lpr = S.tile("lpr", [128, 2, 64], F32)
                S.op("dve", lambda e: e.tensor_tensor(lpr.t[:, 0, :], lq.t[:, 0, :], lq.t[:, 1, :], ALU.mult), reads=[lq.b], writes=[lpr.b])
                S.op("dve", lambda e: e.tensor_tensor(lpr.t[:, 1, :], lq.t[:, 2, :], lq.t[:, 3, :], ALU.mult), reads=[lq.b, lpr.b], writes=[lpr.b])
                S.op("dve", lambda e: e.reduce_sum(sm.t[:, 0:2], lpr.t[:], AX.X), reads=[lpr.b], writes=[sm.b])
                S.op("act", lambda e: e.activation(sm.t[:, 2:4], sm.t[:, 0:2], AF.Exp), reads=[sm.b], writes=[sm.b])
                S.op("dve", lambda e: e.tensor_tensor(sm.t[:, 4:5], sm.t[:, 3:4], sm.t[:, 2:3], ALU.subtract), reads=[sm.b], writes=[sm.b])
                S.op("dve", lambda e: e.tensor_scalar(sm.t[:, 5:6], sm.t[:, 4:5], -lam0, None, ALU.add), reads=[sm.b], writes=[sm.b])
                S.dma("sp", gsc.t[:], self.da_subln_g[0:1, :].broadcast_to([128, 128]), writes=[gsc.b])
                S.op("dve", lambda e: e.tensor_scalar(gsc.t[:], gsc.t[:], 1.0 - lam0, None, ALU.mult), reads=[gsc.b], writes=[gsc.b])
                HSEQ = SEQ // 2
                posi = S.tile("posi", [128, HSEQ], I32)
                ang = S.tile("ang", [128, HSEQ], F32)
                rr = S.tile("rr", [128, HSEQ], F32)
                r2 = S.tile("r2", [128, HSEQ], F32)
                TWO_PI = 2.0 * PI
                for hs in range(2):
                    hsl = slice(hs * HSEQ, (hs + 1) * HSEQ)
                    S.dma("sp", posi.t[:], self.dpos[b:b + 1, hsl].broadcast_to([128, HSEQ]), writes=[posi.b])
                    S.op("dve", lambda e: e.tensor_copy(ang.t[:], posi.t[:]), reads=[posi.b, ang.b], writes=[ang.b])
                    for shift, TAB, sc in ((0.25, COS, TWO_PI), (0.0, SIN, cf.t[:, 1:2])):
                        S.op("dve", lambda e, shift=shift: e.tensor_scalar(rr.t[:], ang.t[:], cf.t[:, 0:1], shift, ALU.mult, ALU.add), reads=[ang.b, cf.b, rr.b], writes=[rr.b])
                        S.op("dve", lambda e: e.tensor_copy(posi.t[:], rr.t[:]), reads=[rr.b, posi.b], writes=[posi.b])
                        S.op("dve", lambda e: e.tensor_copy(r2.t[:], posi.t[:]), reads=[posi.b, r2.b], writes=[r2.b])
                        S.op("dve", lambda e: e.tensor_tensor(rr.t[:], rr.t[:], r2.t[:], ALU.subtract), reads=[rr.b, r2.b], writes=[rr.b])
                        S.op("dve", lambda e: e.tensor_scalar(r2.t[:], rr.t[:], 0.5, None, ALU.is_gt), reads=[rr.b, r2.b], writes=[r2.b])
                        S.op("dve", lambda e: e.tensor_tensor(rr.t[:], rr.t[:], r2.t[:], ALU.subtract), reads=[rr.b, r2.b], writes=[rr.b])
                        S.op("act", lambda e, TAB=TAB, sc=sc, hsl=hsl: e.activation(TAB.t[:, hsl], rr.t[:], AF.Sin, scale=sc), reads=[rr.b, cf.b], writes=[TAB.b])
            ctx = dict(
                WXr=Rot([S.tile("Wd%d" % i, [128, 8, 128], BF16) for i in range(3)]),
                QK=[S.tile("QTd", [128, SEQ], BF16), S.tile("KTd", [128, SEQ], BF16)],
                Vh=S.tile("Vh", [128, NT, 130], BF16),
                qbR=Rot([S.tile("qb%d" % i, [128, 512], BF16) for i in range(2)]),
                t1R=Rot([S.tile("t1d%d" % i, [128, 512], F32) for i in range(2)]),
                t2R=Rot([S.tile("t2d%d" % i, [128, 512], F32) for i in range(1)]),
                sqR=Rot([S.tile("sqd%d" % i, [128, 512], BF16) for i in range(2)]),
                PTr=Rot([S.tile("PTd%d" % i, [128, 512], BF16) for i in range(4)]),
                O1=S.tile("O1", [128, 4, 128], F32),
                ONr=Rot([S.tile("ONb%d" % i, [128, 128], BF16) for i in range(4)]),
                rsR=Rot([S.tile("rsd%d" % i, [128, 16], F32) for i in range(2)]),
                OT=OT, OTb=OTb, COS=COS, SIN=SIN, sm=sm, gsc=gsc,
                wqkv=self.da_w_qkv[j].rearrange("(k p) n -> p k n", p=128))
            Vh = ctx["Vh"]
            S.op("dve", lambda e: e.memset(Vh.t[:, :, 128:130], 1.0), writes=[Vh.b])
            for h in range(8):
                self.diff_head(h, ctx)
            self.rot_banks = list(range(8))
            self.outproj_norm(WO.t, [WO.b], 8, lambda k, t: (OT[:, k, t * 128:(t + 1) * 128], [OTb[k][t // 4]]),
                              first=True, last=True, bias=None, want_xt=True, store_b=None)

    def diff_head(self, h, ctx):
        S = self.S
        XT = self.XT
        QK, Vh, OT, OTb, COS, SIN, sm, gsc, O1 = (ctx[k] for k in ("QK", "Vh", "OT", "OTb", "COS", "SIN", "sm", "gsc", "O1"))
        qbR, t1R, t2R, sqR, PTr, ONr, rsR, WXr, wqkv = (ctx[k] for k in ("qbR", "t1R", "t2R", "sqR", "PTr", "ONr", "rsR", "WXr", "wqkv"))
        scale = 0.125
        Ws = []
        for w3 in range(3):
            W = WXr.next()
            self.wload(W.t[:], wqkv[:, :, w3 * D + h * 128:w3 * D + (h + 1) * 128], W.b)
            Ws.append(W)
        self.rot_banks = list(range(8))
        st = {}

        def P0(i):
            which, tb = divmod(i, 4)
            W = Ws[which]
            sl = slice(tb * 512, (tb + 1) * 512)
            p = self.ps()
            for k in range(8):
                self.mm(p.t[:, :], W.t[:, k, :], XT[:, k, sl], k == 0, k == 7, reads=[W.b] + self.xt_reads(tb), writes=[p.b])
            qb = qbR.next()
            t1 = t1R.next()
            S.op("act", lambda e: e.activation(qb.t[:], p.t[:, :], AF.Copy), reads=[p.b], writes=[qb.b])
            S.op("dve", lambda e: e.tensor_tensor(t1.t[:], p.t[:, :], COS.t[:, sl], ALU.mult), reads=[p.b, COS.b], writes=[t1.b])
            st[i] = (qb, t1)

        def P1(i):
            which, tb = divmod(i, 4)
            T = QK[which]
            sl = slice(tb * 512, (tb + 1) * 512)
            qb, t1 = st.pop(i)
            pr = self.ps()
            self.mm(pr.t[:, :], self.prot, qb.t[:], True, True, reads=[qb.b, self.cst.b], writes=[pr.b])
            t2 = t2R.next()
            S.op("dve", lambda e: e.tensor_tensor(t2.t[:], pr.t[:, :], SIN.t[:, sl], ALU.mult), reads=[pr.b, SIN.b], writes=[t2.b])
            S.op("dve", lambda e: e.tensor_tensor(T.t[:, sl], t1.t[:], t2.t[:], ALU.add), reads=[t1.b, t2.b], writes=[T.b])
            sq = sqR.next()
            S.op("act", lambda e: e.activation(sq.t[:], T.t[:, sl], AF.Square), reads=[T.b], writes=[sq.b])
            st[("sq", i)] = sq

        def P2(i):
            which, tb = divmod(i, 4)
            sq = st.pop(("sq", i))
            for c in range(2):
                pn = self.ps()
                self.mm(pn.t[:, :], self.sel0 if c == 0 else self.sel1, sq.t[:], True, True, reads=[sq.b, self.cst.b], writes=[pn.b])
                col = 8 + which * 8 + c * 4 + tb
                S.op("dve", lambda e, pn=pn, col=col: e.reduce_max(sm.t[:, col:col + 1], pn.t[:, :], AX.X), reads=[pn.b], writes=[sm.b])
        pipeline(8, [P0, P1, P2])
        Wv = Ws[2]
        for t in range(NT):
            p = self.ps()
            for k in range(8):
                self.mm(p.t[:, 0:128], XT[:, k, t * 128:(t + 1) * 128], Wv.t[:, k, :], k == 0, k == 7, reads=[Wv.b, self.XTb[t]], writes=[p.b])
            S.op("act", lambda e, p=p, t=t: e.activation(Vh.t[:, t, 0:128], p.t[:, 0:128], AF.Copy), reads=[p.b], writes=[Vh.b])
        for c in range(2):
            S.op("dve", lambda e, c=c: e.reduce_max(sm.t[:, 6:7], sm.t[:, 8 + c * 4:12 + c * 4], AX.X), reads=[sm.b], writes=[sm.b])
            S.op("dve", lambda e, c=c: e.reduce_max(sm.t[:, 7:8], sm.t[:, 16 + c * 4:20 + c * 4], AX.X), reads=[sm.b], writes=[sm.b])
            S.op("dve", lambda e: e.tensor_tensor(sm.t[:, 6:7], sm.t[:, 6:7], sm.t[:, 7:8], ALU.mult), reads=[sm.b], writes=[sm.b])
            S.op("act", lambda e: e.activation(sm.t[:, 7:8], sm.t[:, 6:7], AF.Sqrt), reads=[sm.b], writes=[sm.b])
            S.op("dve", lambda e, c=c: e.tensor_scalar(sm.t[:, 24 + c:25 + c], sm.t[:, 7:8], -scale, None, ALU.mult), reads=[sm.b], writes=[sm.b])
        QT, KT = QK
        self.rot_banks = [0, 1, 2]
        ptb = self.banks[3]
        accs = [(self.banks[4], self.banks[5]), (self.banks[6], self.banks[7])]
        items = [(I, c, kb) for I in range(4) for c in range(2) for kb in range(4 * I + 4)]
        ast = {}

        def S0(n):
            I, c, kb = items[n]
            pb = slice(c * 64, (c + 1) * 64)
            r0 = max(0, kb - 4 * I)
            nq = (4 - r0) * 128
            q0 = I * 512 + r0 * 128
            p = self.ps()
            diag = kb >= 4 * I
            self.mm(p.t[:, 0:nq], KT.t[pb, kb * 128:(kb + 1) * 128], QT.t[pb, q0:q0 + nq], True, not diag, reads=[KT.b, QT.b], writes=[p.b])
            if diag:
                self.mm(p.t[:, 0:128], self.ident, self.negm, False, True, reads=[self.cst.b], writes=[p.b])
            ast[n] = p

        def S1(n):
            I, c, kb = items[n]
            r0 = max(0, kb - 4 * I)
            nq = (4 - r0) * 128
            p = ast.pop(n)
            PT = PTr.next()
            S.op("act", lambda e: e.activation(PT.t[:, 0:nq], p.t[:, 0:nq], AF.Exp, bias=sm.t[:, 24 + c:25 + c], scale=scale), reads=[p.b, sm.b], writes=[PT.b])
            ast[("pt", n)] = PT

        def S2(n):
            I, c, kb = items[n]
            r0 = max(0, kb - 4 * I)
            PT = ast.pop(("pt", n))
            aset = accs[(I * 2 + c) % 2]
            for r in range(r0, 4):
                acc = aset[r // 2]
                co = (r % 2) * 256
                self.mm(acc.t[:, co:co + 129], PT.t[:, (r - r0) * 128:(r - r0 + 1) * 128], Vh.t[:, kb, 0:129], kb == 0 and r % 2 == 0, kb == 4 * I + r,
                        reads=[PT.b, Vh.b], writes=[acc.b], sgc=True)

        def evac_all(I, c):
            aset = accs[(I * 2 + c) % 2]
            rs = rsR.next()
            for r in range(4):
                acc = aset[r // 2]
                co = (r % 2) * 256
                S.op("dve", lambda e, acc=acc, co=co, r=r: e.reciprocal(rs.t[:, r:r + 1], acc.t[:, co + 128:co + 129]), reads=[acc.b], writes=[rs.b])
                if c == 0:
                    S.op("dve", lambda e, acc=acc, co=co, r=r: e.tensor_scalar(O1.t[:, r, :], acc.t[:, co:co + 128], rs.t[:, r:r + 1], None, ALU.mult),
                         reads=[acc.b, rs.b, O1.b], writes=[O1.b])
                else:
                    S.op("dve", lambda e, r=r: e.tensor_tensor(rs.t[:, 4 + r:5 + r], rs.t[:, r:r + 1], sm.t[:, 5:6], ALU.mult), reads=[rs.b, sm.b], writes=[rs.b])
                    S.op("dve", lambda e, acc=acc, co=co, r=r: e.scalar_tensor_tensor(O1.t[:, r, :], acc.t[:, co:co + 128], rs.t[:, 4 + r:5 + r], O1.t[:, r, :], ALU.mult, ALU.add),
                         reads=[acc.b, rs.b, O1.b], writes=[O1.b])
            if c == 0:
                return
            ONs = [ONr.next() for r in range(4)]
            for r in range(4):
                S.op("act", lambda e, r=r: e.activation(ONs[r].t[:], O1.t[:, r, :], AF.Square, accum_out=rs.t[:, 8 + r:9 + r]), reads=[O1.b], writes=[ONs[r].b, rs.b])
            S.op("act", lambda e: e.activation(rs.t[:, 8:12], rs.t[:, 8:12], AF.Sqrt, bias=1e-5, scale=1.0 / 128.0), reads=[rs.b], writes=[rs.b])
            S.op("dve", lambda e: e.reciprocal(rs.t[:, 8:12], rs.t[:, 8:12]), reads=[rs.b], writes=[rs.b])
            for r in range(4):
                ON = ONs[r]
                S.op("dve", lambda e, r=r, ON=ON: e.scalar_tensor_tensor(ON.t[:], O1.t[:, r, :], rs.t[:, 8 + r:9 + r], gsc.t[:], ALU.mult, ALU.mult),
                     reads=[O1.b, rs.b, gsc.b, ON.b], writes=[ON.b])

            def transposes():
                ptv = ptb.t[:].bitcast(BF16)
                for r in range(4):
                    ON = ONs[r]
                    S.op("pe", lambda e, ON=ON, r=r: e.transpose(ptv[:, r * 128:(r + 1) * 128], ON.t[:], self.ident), reads=[ON.b, self.cst.b], writes=[ptb.b])
                S.op("act", lambda e: e.activation(OT[:, h, I * 512:(I + 1) * 512], ptv[:, 0:512], AF.Copy), reads=[ptb.b], writes=[OTb[h][I]])
            return transposes

        deferred = []

        def S3(n):
            I, c, kb = items[n]
            while deferred and deferred[0][0] <= n:
                deferred.pop(0)[1]()
            if kb == 4 * I + 3:
                fn = evac_all(I, c)
                if fn is not None:
                    deferred.append((n + 6, fn))
        pipeline(len(items), [S0, S1, S2, S3])
        while deferred:
            deferred.pop(0)[1]()
        self.rot_banks = list(range(8))

    def ssd(self, li, j):
        S = self.S
        self.load_ln(li, 0)
        XT = self.XT
        win = self.mb_w_in[j].rearrange("(k p) n -> p k n", p=128)
        with S.scope():
            cols = S.tile("mbcols", [128, 160], F32)
            S.dma("sp", cols.t[:], self.mb_cols[j], writes=[cols.b])
            hv = S.tile("mbhv", [128, 3, 32], F32)
            S.dma("sp", hv.t[:].rearrange("p a h -> p (a h)"), self.mb_hv[j].partition_broadcast(128), writes=[hv.b])
            DT = S.tile("DT", [128, NT, 32], F32)
            DTA = S.tile("DTA", [128, NT, 32], F32)
            ACU = S.tile("ACU", [128, NT, 32], F32)
            EA = S.tile("EA", [128, NT, 32], F32)
            W2 = S.tile("W2", [128, NT, 32], F32)
            ET = S.tile("ET", [128, NT, 32], F32)
            v3 = lambda ap: ap.rearrange("p (t h) -> p t h", t=NT)
            with S.scope():
                Wdt = S.tile("Wdt", [128, 8, 32], F32)
                S.dma("sp", Wdt.t[:], win[:, :, 6144:6176], writes=[Wdt.b])
                L1 = S.tile("L1", [128, NT, 32], F32)
                X32r = Rot([S.tile("X32_%d" % i, [128, 8, 128], F32) for i in range(2)])
                self.rot_banks = list(range(7))
                p = self.banks[7]
                for t in range(NT):
                    X32 = X32r.next()
                    for hf in range(2):
                        pq = self.ps()
                        for kk in range(4):
                            k = hf * 4 + kk
                            self.mm(pq.t[:, kk * 128:(kk + 1) * 128], self.X[:, t, k * 128:(k + 1) * 128], self.ident32, True, True,
                                    reads=[self.Xb[t], self.cst32.b], writes=[pq.b], sgc=True)
                        S.op("act" if hf == 0 else "dve", (lambda e, pq=pq, X32=X32, hf=hf: e.activation(X32.t[:, hf * 4:hf * 4 + 4, :], pq.t[:, :].rearrange("p (k n) -> p k n", k=4), AF.Copy)) if hf == 0 else
                             (lambda e, pq=pq, X32=X32, hf=hf: e.tensor_copy(X32.t[:, hf * 4:hf * 4 + 4, :], pq.t[:, :].rearrange("p (k n) -> p k n", k=4))),
                             reads=[pq.b], writes=[X32.b])
                    for k in range(8):
                        self.mm(p.t[:, t * 32:(t + 1) * 32], X32.t[:, k, :], Wdt.t[:, k, :], t == 0 and k == 0, k == 7,
                                reads=[Wdt.b, X32.b], writes=[p.b], sgc=True)
                self.rot_banks = list(range(8))
                S.op("dve", lambda e, p=p: e.tensor_tensor(DT.t[:], v3(p.t[:, :]), hv.t[:, 0:1, :].to_broadcast([128, NT, 32]), ALU.add), reads=[p.b, hv.b], writes=[DT.b])
                S.op("act", lambda e: e.activation(L1.t[:], DT.t[:], AF.Abs), reads=[DT.b], writes=[L1.b])
                S.op("act", lambda e: e.activation(L1.t[:], L1.t[:], AF.Exp, scale=-1.0), reads=[L1.b], writes=[L1.b])
                S.op("act", lambda e: e.activation(L1.t[:], L1.t[:], AF.Ln, bias=1.0, scale=1.0), reads=[L1.b], writes=[L1.b])
                S.op("dve", lambda e: e.scalar_tensor_tensor(DT.t[:], DT.t[:], 0.0, L1.t[:], ALU.max, ALU.add), reads=[DT.b, L1.b], writes=[DT.b])
                S.op("act", lambda e: e.activation(hv.t[:, 1, :], hv.t[:, 1, :], AF.Exp), reads=[hv.b], writes=[hv.b])
                S.op("dve", lambda e: e.scalar_tensor_tensor(DTA.t[:], DT.t[:], -1.0, hv.t[:, 1:2, :].to_broadcast([128, NT, 32]), ALU.mult, ALU.mult), reads=[DT.b, hv.b], writes=[DTA.b])
                pc, pt = self.ps(), self.ps()
                for ch in range(NT):
                    self.mm(pc.t[:, ch * 32:(ch + 1) * 32], self.tri32, DTA.t[:, ch, :], ch == 0, True, reads=[self.cst32.b, DTA.b], writes=[pc.b], sgc=True)
                for ch in range(NT):
                    self.mm(pt.t[:, ch * 32:(ch + 1) * 32], self.ones32, DTA.t[:, ch, :], ch == 0, True, reads=[self.cst32.b, DTA.b], writes=[pt.b], sgc=True)
                S.op("act", lambda e, pc=pc: e.activation(ACU.t[:], v3(pc.t[:, :]), AF.Copy), reads=[pc.b], writes=[ACU.b])
                S.op("act", lambda e: e.activation(EA.t[:], ACU.t[:], AF.Exp), reads=[ACU.b], writes=[EA.b])
                S.op("act", lambda e, pt=pt: e.activation(ET.t[:], v3(pt.t[:, :]), AF.Exp), reads=[pt.b], writes=[ET.b])
                S.op("dve", lambda e, pt=pt: e.tensor_tensor(W2.t[:], v3(pt.t[:, :]), ACU.t[:], ALU.subtract), reads=[pt.b, ACU.b], writes=[W2.b])
                S.op("act", lambda e: e.activation(W2.t[:], W2.t[:], AF.Exp), reads=[W2.b], writes=[W2.b])
                S.op("dve", lambda e: e.tensor_tensor(W2.t[:], W2.t[:], DT.t[:], ALU.mult), reads=[W2.b, DT.b], writes=[W2.b])
            if DEBUG == "ssd":
                for i_, T_ in enumerate((DT, DTA, ACU, EA, W2, ET)):
                    S.dma("sp", self.dbg[:, i_, 0:512], T_.t[:].rearrange("p t h -> p (t h)"), reads=[T_.b], out_final=True)
            WXr = Rot([S.tile("WX%d" % i, [128, 8, 128], BF16) for i in range(3)])
            WZs = [S.tile("WZ%d" % i, [128, 8, 256], BF16) for i in range(2)]
            WOs = [S.tile("WOg%d" % i, [128, 2, D], BF16) for i in range(2)]
            CIs = [S.tile("CI%d" % i, [128, 3 + SEQ], BF16) for i in range(2)]
            for CI in CIs:
                S.op("dve", lambda e, CI=CI: e.memset(CI.t[:, 0:3], 0.0), writes=[CI.b])
            self.wload(WZs[0].t[:], win[:, :, 0:256], WZs[0].b)
            self.wload(WOs[0].t[:], self.mb_w_out[j, 0:256, :].rearrange("(k p) n -> p k n", p=128), WOs[0].b)
            for g in range(8):
                self.ssd_group(j, g, win, cols, hv, DT, DTA, ACU, EA, W2, ET, WXr, WZs, WOs, CIs)

    def ssd_group(self, j, g, win, cols, hv, DT, DTA, ACU, EA, W2, ET, WXr, WZs, WOs, CIs):
        S = self.S
        XT = self.XT
        WZ, WOg = WZs[g % 2], WOs[g % 2]
        if g < 7:
            self.wload(WZs[(g + 1) % 2].t[:], win[:, :, (g + 1) * 256:(g + 2) * 256], WZs[(g + 1) % 2].b)
            self.wload(WOs[(g + 1) % 2].t[:], self.mb_w_out[j, (g + 1) * 256:(g + 2) * 256, :].rearrange("(k p) n -> p k n", p=128), WOs[(g + 1) % 2].b)
        wcol = [2048 + g * 256, 2048 + g * 256 + 128, 4096 + g * 128, 5120 + g * 128]
        fcs = [2 * g, 2 * g + 1, 16 + g, 24 + g]
        with S.scope():
            FBC = S.alloc("FBC", [128, 2, SEQ], BF16)
            FBCb = [[S.buf("FBC%d_%d" % (i, tb)) for tb in range(NTB)] for i in range(2)]
            XTK = S.tile("XTK", [128, NT, 256], BF16)
            BTK = S.tile("BTK", [128, NT, 128], BF16)
            NG = S.tile("NGc", [128, 2], F32)
            S.dma("sp", NG.t[:], self.mb_ngcol[j, :, 2 * g:2 * g + 2], writes=[NG.b])
            SS = S.tile("SS", [128, NT], F32)
            DGD = S.tile("DGD", [128, 4, 128], BF16)
            for h in range(4):
                hh = 4 * g + h
                S.op("dve", lambda e, h=h, hh=hh: e.tensor_scalar(DGD.t[:, h, :], self.ident, hv.t[:, 2, hh:hh + 1], None, ALU.mult),
                     reads=[self.cst.b, hv.b, DGD.b], writes=[DGD.b])
            DGr = Rot([S.tile("DGC%d" % i, [128, 4, 128], BF16) for i in range(2)])
            with S.scope():
                FX = S.alloc("FX", [128, 2, SEQ], BF16)
                FXb = [[S.buf("FX%d_%d" % (i, tb)) for tb in range(NTB)] for i in range(2)]
                dst_of = lambda i4: (FX, FXb, i4) if i4 < 2 else (FBC, FBCb, i4 - 2)
                st = {}

                def P0(i4):
                    CI = CIs[i4 % 2]
                    W = WXr.next()
                    self.wload(W.t[:], win[:, :, wcol[i4]:wcol[i4] + 128], W.b)
                    DG = DGr.next()
                    st[i4] = DG
                    for k in range(4):
                        col = k * 32 + fcs[i4]
                        S.op("dve", lambda e, k=k, col=col: e.tensor_scalar(DG.t[:, k, :], self.ident, cols.t[:, col:col + 1], None, ALU.mult),
                             reads=[self.cst.b, cols.b, DG.b], writes=[DG.b])
                    for tb in range(NTB):
                        p = self.ps()
                        for k in range(8):
                            self.mm(p.t[:, :], W.t[:, k, :], XT[:, k, tb * 512:(tb + 1) * 512], k == 0, k == 7,
                                    reads=[W.b] + self.xt_reads(tb), writes=[p.b])
                        S.op("act", lambda e, p=p, tb=tb: e.activation(CI.t[:, 3 + tb * 512:3 + (tb + 1) * 512], p.t[:, :], AF.Copy), reads=[p.b], writes=[CI.b])

                def P1(i4):
                    CI = CIs[i4 % 2]
                    DG = st.pop(i4)
                    fc = fcs[i4]
                    T, Tb, ti = dst_of(i4)
                    for tb in range(NTB):
                        p = self.ps()
                        for k in range(4):
                            self.mm(p.t[:, :], DG.t[:, k, :], CI.t[:, k + tb * 512:k + (tb + 1) * 512], k == 0, k == 3,
                                    reads=[DG.b, CI.b], writes=[p.b])
                        S.op("act", lambda e, p=p, tb=tb: e.activation(T[:, ti, tb * 512:(tb + 1) * 512], p.t[:, :], AF.Silu, bias=cols.t[:, 128 + fc:129 + fc], scale=1.0),
                             reads=[p.b, cols.b], writes=[Tb[ti][tb]])
                pipeline(4, [P0, P1])
                for t4 in range(4):
                    for T, Tb, ti, dst, w in ((FX, FXb, 0, XTK, 0), (FX, FXb, 1, XTK, 128), (FBC, FBCb, 0, BTK, 0)):
                        p = self.ps()
                        pv = p.t[:].bitcast(BF16)
                        for tt in range(4):
                            t = t4 * 4 + tt
                            S.op("pe", lambda e, pv=pv, tt=tt, t=t, T=T, ti=ti: e.transpose(pv[:, tt * 128:(tt + 1) * 128], T[:, ti, t * 128:(t + 1) * 128], self.ident),
                                 reads=[Tb[ti][t4], self.cst.b], writes=[p.b])
                        S.op("dve", lambda e, pv=pv, dst=dst, w=w, t4=t4: e.tensor_copy(dst.t[:, t4 * 4:t4 * 4 + 4, w:w + 128], pv[:, 0:512].rearrange("p (t n) -> p t n", t=4)),
                             reads=[p.b], writes=[dst.b])

            YT = S.alloc("YT", [128, 2, SEQ], BF16)
            YTb = [S.buf("YT%d" % tb) for tb in range(NTB)]
            HS = S.tile("HS", [128, 256], F32)
            HSbs = [S.tile("HSb%d" % i, [128, 256], BF16) for i in range(4)]
            CBr = Rot([S.tile("CBm%d" % i, [128, 128], F32) for i in range(3)])
            TMr = Rot([S.tile("TMs%d" % i, [128, 128], F32) for i in range(8)])
            WTr = Rot([S.tile("WTs%d" % i, [128, 128], BF16) for i in range(8)])
            YGr = Rot([S.tile("YG%d" % i, [128, 256], F32) for i in range(3)])
            Y2r = Rot([S.tile("Y2%d" % i, [128, 256], F32) for i in range(2)])
            ZSr = Rot([S.tile("ZSc%d" % i, [128, 256], BF16) for i in range(4)])
            THr = Rot([S.tile("THc%d" % i, [128, 256], BF16) for i in range(1)])
            XWr = Rot([S.tile("XW%d" % i, [128, 256], BF16) for i in range(3)])
            YNr = Rot([S.tile("YN%d" % i, [128, 256], BF16) for i in range(3)])
            rsR = Rot([S.tile("rss%d" % i, [128, 4], F32) for i in range(3)])
            bc4 = lambda T, ch: T.t[:, ch, 4 * g:4 * g + 4].unsqueeze(2).to_broadcast([128, 4, 64])
            v4 = lambda ap: ap.rearrange("p (h d) -> p h d", h=4)
            WTs = {}
            ZSs = {}

            PHs = {}
            YGs = {}
            YNs = {}
            CBs = {}
            TMs = {}
            XWs = {}
            RSs = {}
            phbanks = [self.banks[6], self.banks[7]]

            def T0(ch):
                tsl = slice(ch * 128, (ch + 1) * 128)
                pcb = self.ps()
                self.mm(pcb.t[:, 0:128], FBC[:, 0, tsl], FBC[:, 1, tsl], True, True, reads=[FBCb[0][ch // 4], FBCb[1][ch // 4]], writes=[pcb.b])
                CBm = CBr.next()
                CBs[ch] = CBm
                S.op("dve", lambda e: e.tensor_tensor(CBm.t[:], pcb.t[:, 0:128], self.tri32, ALU.mult), reads=[pcb.b, self.cst32.b], writes=[CBm.b])
                for h in range(4):
                    hh = 4 * g + h
                    pseg = self.ps()
                    self.mm(pseg.t[:, 0:128], DTA.t[:, ch, hh:hh + 1].to_broadcast([128, 128]), self.tri32, True, True, reads=[DTA.b, self.cst32.b], writes=[pseg.b])
                    TM = TMr.next()
                    TMs[(ch, h)] = TM
                    S.op("dve", lambda e, TM=TM, pseg=pseg, hh=hh: e.tensor_scalar(TM.t[:], pseg.t[:, 0:128], ACU.t[:, ch, hh:hh + 1], 0.0, ALU.subtract, ALU.min),
                         reads=[pseg.b, ACU.b], writes=[TM.b])
                    S.op("act", lambda e, TM=TM: e.activation(TM.t[:], TM.t[:], AF.Exp), reads=[TM.b], writes=[TM.b])
                pz = self.ps()
                for k in range(8):
                    self.mm(pz.t[:, 0:256], XT[:, k, tsl], WZ.t[:, k, :], k == 0, k == 7, reads=[WZ.b, self.XTb[ch]], writes=[pz.b])
                ZS = ZSr.next()
                TH = THr.next()
                S.op("act", lambda e: e.activation(TH.t[:], pz.t[:, 0:256], AF.Tanh, scale=0.5), reads=[pz.b], writes=[TH.b])
                S.op("dve", lambda e: e.scalar_tensor_tensor(ZS.t[:], TH.t[:], 1.0, pz.t[:, 0:256], ALU.add, ALU.mult), reads=[TH.b, pz.b], writes=[ZS.b])
                ZSs[ch] = ZS
                if ch < NT - 1:
                    XW = XWr.next()
                    XWs[ch] = XW
                    S.op("pool", lambda e: e.tensor_tensor(v4(XW.t[:]), v4(XTK.t[:, ch, :]), bc4(W2, ch), ALU.mult), reads=[XTK.b, W2.b], writes=[XW.b])

            def T1(ch):
                CBm = CBs.pop(ch)
                for h in range(4):
                    hh = 4 * g + h
                    TM = TMs.pop((ch, h))
                    WT = WTr.next()
                    S.op("dve", lambda e, TM=TM, WT=WT, hh=hh: e.scalar_tensor_tensor(WT.t[:], TM.t[:], DT.t[:, ch, hh:hh + 1], CBm.t[:], ALU.mult, ALU.mult),
                         reads=[TM.b, DT.b, CBm.b], writes=[WT.b])
                    WTs[(ch, h)] = WT
                if ch < NT - 1:
                    XW = XWs.pop(ch)
                    ph = phbanks[ch % 2]
                    self.mm(ph.t[:, 0:256], BTK.t[:, ch, :], XW.t[:], True, True, reads=[BTK.b, XW.b], writes=[ph.b])
                    PHs[ch] = ph

            def T2(ch):
                tsl = slice(ch * 128, (ch + 1) * 128)
                py = self.ps()
                for h in range(4):
                    WT = WTs.pop((ch, h))
                    self.mm(py.t[:, h * 64:(h + 1) * 64], WT.t[:], XTK.t[:, ch, h * 64:(h + 1) * 64], True, False, reads=[WT.b, XTK.b], writes=[py.b], sgc=True)
                    self.mm(py.t[:, h * 64:(h + 1) * 64], DGD.t[:, h, :], XTK.t[:, ch, h * 64:(h + 1) * 64], False, True, reads=[DGD.b, XTK.b], writes=[py.b], sgc=True)
                YG = YGr.next()
                YGs[ch] = YG
                if ch > 0:
                    HSb = HSbs[ch % 4]
                    po = self.ps()
                    self.mm(po.t[:, 0:256], FBC[:, 1, tsl], HSb.t[:], True, True, reads=[FBCb[1][ch // 4], HSb.b], writes=[po.b])
                    Y2 = Y2r.next()
                    S.op("dve", lambda e: e.tensor_tensor(v4(Y2.t[:]), v4(po.t[:, 0:256]), bc4(EA, ch), ALU.mult), reads=[po.b, EA.b], writes=[Y2.b])
                    S.op("dve", lambda e: e.tensor_tensor(YG.t[:], Y2.t[:], py.t[:, 0:256], ALU.add), reads=[Y2.b, py.b], writes=[YG.b])
                else:
                    S.op("act", lambda e: e.activation(YG.t[:], py.t[:, 0:256], AF.Copy), reads=[py.b], writes=[YG.b])

            def R(ch):
                if ch < NT - 1:
                    ph = PHs.pop(ch)
                    if ch == 0:
                        S.op("act", lambda e: e.activation(HS.t[:], ph.t[:, 0:256], AF.Copy), reads=[ph.b], writes=[HS.b])
                    else:
                        S.op("dve", lambda e: e.tensor_tensor(v4(HS.t[:]), v4(HS.t[:]), bc4(ET, ch), ALU.mult), reads=[HS.b, ET.b], writes=[HS.b])
                        S.op("dve", lambda e: e.tensor_tensor(HS.t[:], HS.t[:], ph.t[:, 0:256], ALU.add), reads=[HS.b, ph.b], writes=[HS.b])
                    HSn = HSbs[(ch + 1) % 4]
                    S.op("act", lambda e: e.activation(HSn.t[:], HS.t[:], AF.Copy), reads=[HS.b], writes=[HSn.b])

            def T3(ch):
                YG = YGs.pop(ch)
                ZS = ZSs.pop(ch)
                YN = YNr.next()
                YNs[ch] = YN
                S.op("pool", lambda e: e.tensor_tensor(YN.t[:], YG.t[:], ZS.t[:], ALU.mult), reads=[YG.b, ZS.b], writes=[YN.b])
                sqj = Y2r.next()
                S.op("act", lambda e: e.activation(sqj.t[:], YN.t[:], AF.Square, accum_out=SS.t[:, ch:ch + 1]), reads=[YN.b], writes=[sqj.b, SS.b])

            def T5(ch):
                tsl = slice(ch * 128, (ch + 1) * 128)
                YN = YNs.pop(ch)
                ptp = self.ps()
                ptv = ptp.t[:].bitcast(BF16)
                for c2 in range(2):
                    S.op("pe", lambda e, c2=c2: e.transpose(ptv[:, c2 * 128:(c2 + 1) * 128], YN.t[:, c2 * 128:(c2 + 1) * 128], self.ident),
                         reads=[YN.b, self.cst.b], writes=[ptp.b])
                for c2 in range(2):
                    S.op("act", lambda e, c2=c2: e.activation(YT[:, c2, tsl], ptv[:, c2 * 128:(c2 + 1) * 128], AF.Copy, scale=NG.t[:, c2:c2 + 1]),
                         reads=[ptp.b, NG.b], writes=[YTb[ch // 4]])

            self.rot_banks = list(range(6))
            def T2x(ch):
                if ch == 0:
                    R(0)
                if ch + 1 < NT:
                    R(ch + 1)
                T2(ch)
            pipeline(NT, [T0, T1, T2x, T3, T5])
            self.rot_banks = list(range(8))
            S.op("act", lambda e: e.activation(SS.t[:], SS.t[:], AF.Sqrt, bias=4e-5, scale=1.0 / 256.0), reads=[SS.b], writes=[SS.b])
            S.op("dve", lambda e: e.reciprocal(SS.t[:], SS.t[:]), reads=[SS.b], writes=[SS.b])
            self.outproj_norm(WOg.t, [WOg.b], 2, lambda k, t: (YT[:, k, t * 128:(t + 1) * 128], [YTb[t // 4]]),
                              first=(g == 0), last=(g == 7), bias=None, want_xt=True, store_b=None, rowscale=SS)

    def build(self):
        S = self.S
        nst = len(self.stages)
        for b in range(self.nseq):
            with S.scope():
                self.load_x(b)
                for t in range(NT):
                    self.make_xt(t)
                for si, (li, sub) in enumerate(self.stages):
                    lastst = si == nst - 1
                    if sub == "a":
                        if li % 3 == 0:
                            self.conformer(li, li // 3)
                        elif li % 3 == 2:
                            self.diffattn(li, li // 3, b)
                        else:
                            self.ssd(li, li // 3)
                        if lastst:
                            for t in range(NT):
                                self.store_x(b, t)
                    elif sub == "b":
                        self.xattn(li, b)
                        if lastst:
                            for t in range(NT):
                                self.store_x(b, t)
                    else:
                        self.ffn(li, want_xt=not lastst, store_b=b if lastst else None)
        S.finish()
        return self.nc


def host_consts():
    c = np.zeros((128, NCST * 128), np.float32)
    c[:, 0:128] = np.eye(128, dtype=np.float32)
    c[:, 128:256] = 1.0
    c[:, 256:384] = 1.0 / 1024.0
    prot = np.zeros((128, 128), np.float32)
    for base in (0, 64):
        for i in range(8):
            prot[base + i, base + i + 8] = 1.0
            prot[base + i + 8, base + i] = 1.0
    c[:, 384:512] = prot
    c[0:64, 512:640] = 1.0
    c[64:128, 640:768] = 1.0
    k = np.arange(128)[:, None]
    q = np.arange(128)[None, :]
    c[:, 768:896] = np.where(q < k, -30000.0, 0.0)
    return c


def host_consts_f():
    c = np.zeros((128, 16), np.float32)
    for p in range(128):
        d = p % 64
        if d < 16:
            i = d % 8
            c[p, 0] = ROPE_THETA ** (-(2.0 * i) / 16.0) / (2.0 * np.pi)
            sgn = -1.0 if d < 8 else 1.0
        else:
            sgn = 1.0
        c[p, 1] = sgn * 2.0 * np.pi
    return c


def colpack(v):
    return np.ascontiguousarray(v.reshape(-1, 128).T)


def host_layout(inputs):
    m = {}
    m["cst"] = host_consts()
    m["cstf"] = host_consts_f()
    c32 = np.zeros((128, 384), np.float32)
    c32[:, 256:384] = np.eye(128, dtype=np.float32)
    c32[:, 0:128] = np.triu(np.ones((128, 128), np.float32))
    c32[:, 128:256] = 1.0
    m["cst32"] = c32
    mbc = np.zeros((1, 128, 160), np.float32)
    for k in range(4):
        mbc[0, :, k * 32:(k + 1) * 32] = colpack(inputs["mb_w_conv"][0, k])
    mbc[0, :, 128:160] = colpack(inputs["mb_b_conv"][0])
    m["mb_cols"] = mbc
    m["mb_ngcol"] = np.ascontiguousarray(colpack(inputs["mb_norm_g"][0])[None])
    m["mb_hv"] = np.ascontiguousarray(np.concatenate([inputs["mb_dt_bias"], inputs["mb_a_log"], inputs["mb_d"]], 1))
    m["da_lqk"] = np.ascontiguousarray(np.concatenate([inputs[k] for k in ("da_lq1", "da_lk1", "da_lq2", "da_lk2")], 0))
    cvc = np.zeros((2, 128, CV_N), np.float32)
    for j in range(2):
        cvc[j, :, CV_BIN:CV_BIN + 16] = colpack(inputs["cv_b_in"][j])
        wd = inputs["cv_w_dw"][j]
        for k in range(CONVW):
            cvc[j, :, CV_WDW + k * 8:CV_WDW + (k + 1) * 8] = colpack(wd[k])
        cvc[j, :, CV_BDW:CV_BDW + 8] = colpack(inputs["cv_b_dw"][j])
        cvc[j, :, CV_LNG:CV_LNG + 8] = colpack(inputs["cv_ln_g"][j])
        cvc[j, :, CV_LNB:CV_LNB + 8] = colpack(inputs["cv_ln_b"][j])
    m["cv_cols"] = cvc
    for k in ("mb_w_in", "mb_w_out", "mb_norm_g", "da_w_qkv", "da_w_out", "da_subln_g", "cv_w_in", "cv_w_out", "cv_b_out", "xa_w_q", "xa_w_kv", "xa_w_out", "ff_w_in", "ff_w_out", "ln_g", "ln_b"):
        m[k] = np.ascontiguousarray(inputs[k])
    return m


ALL_STAGES = [(li, s) for li in range(DEPTH) for s in "abc"]


def run(inputs, nseq, stages, batch_ids, trace=False):
    nc = Builder(nseq, stages).build()
    shared = host_layout(inputs)
    in_maps = []
    for ids in batch_ids:
        m = dict(shared)
        m["x"] = np.ascontiguousarray(inputs["x"][ids])
        m["mem"] = np.ascontiguousarray(inputs["mem"][ids])
        m["pos"] = np.ascontiguousarray(inputs["positions"][ids])
        in_maps.append(m)
    res = run_bass_kernel_spmd(nc, in_maps, core_ids=list(range(len(batch_ids))), trace=trace)
    return res


def kernel(**inputs):
    ncore = 8
    nseq = inputs["x"].shape[0] // ncore
    ids = [list(range(c * nseq, (c + 1) * nseq)) for c in range(ncore)]
    res = run(inputs, nseq, ALL_STAGES, ids)
    out = np.empty(inputs["x"].shape, np.float32)
    for c in range(ncore):
        out[ids[c]] = res.results[c]["out"]
    return out
```

```python
import contextlib
import numpy as np
import concourse.bass as bass
import concourse.mybir as mybir
from concourse.bass_utils import run_bass_kernel_spmd

F32 = mybir.dt.float32
BF16 = mybir.dt.bfloat16
I32 = mybir.dt.int32
ALU = mybir.AluOpType
AF = mybir.ActivationFunctionType
AX = mybir.AxisListType

ENGS = ("pe", "act", "dve", "pool", "sp")
EPOCH = 20000


class Buf:
    __slots__ = ("name", "w", "r", "dsem", "excl")

    def __init__(self, name, alias=None):
        self.name = name
        self.excl = False
        self.w = None
        self.r = dict(alias or {})
        self.dsem = None


class Tile:
    __slots__ = ("t", "b")

    def __init__(self, t, b):
        self.t = t
        self.b = b


def pipeline(n, stages, lag=1):
    ns = len(stages)
    for step in range(n + (ns - 1) * lag):
        for si, fn in enumerate(stages):
            c = step - si * lag
            if 0 <= c < n:
                fn(c)


class Rot:
    def __init__(self, items):
        self.items = list(items)
        self.i = 0

    def next(self):
        x = self.items[self.i % len(self.items)]
        self.i += 1
        return x


class Sched:
    def __init__(self, nc):
        self.nc = nc
        self.stack = contextlib.ExitStack()
        self.scopes = []
        self.alias = {}
        self.prog = {e: [] for e in ENGS}
        self.cnt = {e: 0 for e in ENGS}
        self.epoch = {e: 0 for e in ENGS}
        self.seen = {e: {} for e in ENGS}
        self.semh = {}
        self.dval = {}
        self.finals = {}
        self.nsem = 0
        self.uid = 0
        self.free_dsems = []
        for e in ENGS:
            self._newsem((e, 0))

    def _newsem(self, key):
        self.nsem += 1
        self.semh[key] = self.stack.enter_context(self.nc.semaphore("s%d" % self.nsem))
        return key

    @contextlib.contextmanager
    def scope(self):
        st = contextlib.ExitStack()
        self.scopes.append((st, []))
        try:
            yield
        finally:
            st2, bufs = self.scopes.pop()
            for b in bufs:
                deps = dict(b.r)
                if b.w is not None:
                    deps[b.w[0]] = max(deps.get(b.w[0], 0), b.w[1])
                for k, v in deps.items():
                    if self.alias.get(k, 0) < v:
                        self.alias[k] = v
                if b.dsem is not None:
                    self.free_dsems.append(b.dsem)
                    b.dsem = None
            st2.close()

    def alloc(self, name, shape, dtype):
        self.uid += 1
        st = self.scopes[-1][0] if self.scopes else self.stack
        return st.enter_context(self.nc.sbuf_tensor("%s_%d" % (name, self.uid), list(shape), dtype))

    def buf(self, name):
        b = Buf(name, self.alias)
        if self.scopes:
            self.scopes[-1][1].append(b)
        return b

    def tile(self, name, shape, dtype):
        return Tile(self.alloc(name, shape, dtype), self.buf(name))

    def psum_banks(self):
        out = []
        for i in range(8):
            t = self.stack.enter_context(self.nc.psum_tensor("ps%d" % i, [128, 512], F32))
            out.append(Tile(t, Buf("ps%d" % i)))
            out[-1].b.excl = True
        return out

    def _waits(self, eng, reads, writes):
        waits = {}
        seen = self.seen[eng]

        def need(k, v):
            if seen.get(k, 0) >= v:
                return
            if waits.get(k, 0) < v:
                waits[k] = v
        for b in reads:
            if b.w is not None:
                need(*b.w)
            if b.excl:
                for k, v in b.r.items():
                    if k[0] != eng:
                        need(k, v)
        for b in writes:
            if b.w is not None:
                need(*b.w)
            for k, v in b.r.items():
                need(k, v)
        if eng == "pe":
            for k in [k for k in waits if k[0] == "pe"]:
                del waits[k]
        for k, v in waits.items():
            seen[k] = v
        return list(waits.items())

    def op(self, eng, fn, reads=(), writes=()):
        waits = self._waits(eng, reads, writes)
        if self.cnt[eng] >= EPOCH:
            self.epoch[eng] += 1
            self.cnt[eng] = 0
            self._newsem((eng, self.epoch[eng]))
        self.cnt[eng] += 1
        key = (eng, self.epoch[eng])
        c = self.cnt[eng]
        self.prog[eng].append((waits, fn, key, 1))
        for b in reads:
            b.r[key] = c
        for b in writes:
            b.w = (key, c)
            b.r = {}

    def dma(self, q, out_ap, in_ap, reads=(), writes=(), out_final=False, **kw):
        waits = self._waits(q, reads, writes)
        owner = writes[0] if writes else reads[0]
        if owner.dsem is None:
            if self.free_dsems:
                owner.dsem = self.free_dsems.pop()
            else:
                owner.dsem = self._newsem(("d", self.nsem))
                self.dval[owner.dsem] = 0
        key = owner.dsem
        self.dval[key] += 16
        v = self.dval[key]
        self.prog[q].append((waits, lambda e: e.dma_start(out=out_ap, in_=in_ap, **kw), key, 16))
        for b in reads:
            b.r[key] = v
        for b in writes:
            b.w = (key, v)
            b.r = {}
        if out_final:
            self.finals[key] = v

    def finish(self):
        fw = list(self.finals.items())
        nc = self.nc
        with nc.Block() as block:
            def replay(name, e, tail=()):
                for waits, fn, key, inc in self.prog[name]:
                    for k, v in waits:
                        e.wait_ge(self.semh[k], v)
                    fn(e).then_inc(self.semh[key], inc)
                for k, v in tail:
                    e.wait_ge(self.semh[k], v)

            @block.tensor
            def _(e):
                replay("pe", e)

            @block.scalar
            def _(e):
                replay("act", e)

            @block.vector
            def _(e):
                replay("dve", e)

            @block.gpsimd
            def _(e):
                replay("pool", e)

            @block.sync
            def _(e):
                replay("sp", e, fw)
        self.stack.close()


D = 1024
SEQ = 2048
NT = 16
NTB = 4
NMEM = 256
DFF = 2816
DEPTH = 4
ALPHA = (2.0 * DEPTH) ** 0.25
LN_EPS = 1e-5
CONVW = 31
FF_GROUPS = [(0, 6), (6, 12), (12, 17), (17, 22)]
DEBUG = None
DSTOP = 99
VARIANT = 0
NCST = 8
LAMBDA_INIT = {2: 0.8 - 0.6 * float(np.exp(-0.3 * 2))}
ROPE_THETA = 500000.0

CV_BIN, CV_WDW, CV_BDW, CV_LNG, CV_LNB, CV_N = 0, 16, 16 + 248, 16 + 256, 16 + 264, 16 + 272


class Builder:
    def __init__(self, nseq, stages):
        self.nseq = nseq
        self.stages = stages
        nc = self.nc = bass.Bass("TRN2", target_bir_lowering=False)
        S = self.S = Sched(nc)

        def din(name, shape, dt=F32):
            return nc.dram_tensor(name, list(shape), dt, kind="ExternalInput").ap()
        self.dx = din("x", [nseq, SEQ, D])
        self.dmem = din("mem", [nseq, NMEM, D])
        self.dpos = din("pos", [nseq, SEQ], I32)
        self.dcst = din("cst", [128, NCST * 128])
        self.dcstf = din("cstf", [128, 16])
        self.dcst32 = din("cst32", [128, 384])
        self.mb_w_in = din("mb_w_in", [1, D, 6176])
        self.mb_w_out = din("mb_w_out", [1, 2 * D, D])
        self.mb_cols = din("mb_cols", [1, 128, 160])
        self.mb_hv = din("mb_hv", [1, 96])
        self.mb_norm_g = din("mb_norm_g", [1, 2 * D])
        self.mb_ngcol = din("mb_ngcol", [1, 128, 16])
        self.da_w_qkv = din("da_w_qkv", [1, D, 3 * D])
        self.da_w_out = din("da_w_out", [1, D, D])
        self.da_lqk = din("da_lqk", [4, 64])
        self.da_subln_g = din("da_subln_g", [1, 128])
        self.cv_w_in = din("cv_w_in", [2, D, 2 * D])
        self.cv_w_out = din("cv_w_out", [2, D, D])
        self.cv_b_out = din("cv_b_out", [2, D])
        self.cv_cols = din("cv_cols", [2, 128, CV_N])
        self.xa_w_q = din("xa_w_q", [4, D, D])
        self.xa_w_kv = din("xa_w_kv", [4, D, 2 * D])
        self.xa_w_out = din("xa_w_out", [4, D, D])
        self.ff_w_in = din("ff_w_in", [4, D, 2 * DFF])
        self.ff_w_out = din("ff_w_out", [4, DFF, D])
        self.ln_g = din("ln_g", [4, 3, D])
        self.ln_b = din("ln_b", [4, 3, D])
        self.dout = nc.dram_tensor("out", [nseq, SEQ, D], F32, kind="ExternalOutput").ap()
        self.dbg = nc.dram_tensor("dbg", [128, 8, SEQ], F32, kind="ExternalOutput").ap() if DEBUG else None

        self.X = S.alloc("X", [128, NT, D], F32)
        self.Xb = [S.buf("X%d" % t) for t in range(NT)]
        self.XT = S.alloc("XT", [128, 8, SEQ], BF16)
        self.XTb = [S.buf("XT%d" % t) for t in range(NT)]
        self.G = S.tile("G", [128, D], F32)
        self.Bt = S.tile("Bt", [128, D], F32)
        self.cst = S.tile("cst", [128, NCST * 128], BF16)
        cb = lambda i: self.cst.t[:, i * 128:(i + 1) * 128]
        self.ident, self.ones, self.onesD, self.prot, self.sel0, self.sel1, self.negm = [cb(i) for i in range(7)]
        self.cstf = S.tile("cstf", [128, 16], F32)
        S.dma("sp", self.cstf.t[:], self.dcstf, writes=[self.cstf.b])
        self.rot_banks = list(range(8))
        self.cst32 = S.tile("cst32", [128, 384], F32)
        S.dma("sp", self.cst32.t[:], self.dcst32, writes=[self.cst32.b])
        self.tri32 = self.cst32.t[:, 0:128]
        self.ones32 = self.cst32.t[:, 128:256]
        self.ident32 = self.cst32.t[:, 256:384]
        self.banks = S.psum_banks()
        self.bank_i = 0
        self.xbR = Rot([S.tile("xb%d" % i, [128, D], BF16) for i in range(2)])
        self.stR = Rot([S.tile("st%d" % i, [128, 2, 6], F32) for i in range(3)])
        self.mvR = Rot([S.tile("mv%d" % i, [128, 4], F32) for i in range(3)])
        S.dma("pool", self.cst.t[:], self.dcst, writes=[self.cst.b])

    def ps(self):
        b = self.banks[self.rot_banks[self.bank_i % len(self.rot_banks)]]
        self.bank_i += 1
        return b

    def mm(self, out, lhsT, rhs, start, stop, reads, writes, sgc=False):
        self.S.op("pe", lambda e: e.matmul(out, lhsT, rhs, start=start, stop=stop, skip_group_check=sgc), reads=reads, writes=writes)

    def xt_reads(self, tb):
        return self.XTb[4 * tb:4 * tb + 4]

    def load_x(self, b):
        S = self.S
        for t in range(NT):
            S.dma("sp", self.X[:, t, :], self.dx[b, t * 128:(t + 1) * 128, :], writes=[self.Xb[t]])

    def store_x(self, b, t):
        self.S.dma("sp", self.dout[b, t * 128:(t + 1) * 128, :], self.X[:, t, :], reads=[self.Xb[t]], out_final=True)

    def make_xt(self, t):
        S = self.S
        xb = self.xbR.next()
        X = self.X
        S.op("act", lambda e: e.activation(xb.t[:], X[:, t, :], AF.Copy), reads=[self.Xb[t]], writes=[xb.b])
        self.xt_from(t, xb)

    def xt_from(self, t, xb):
        S = self.S
        p = self.ps()
        pv = p.t[:].bitcast(BF16)
        for k in range(8):
            S.op("pe", lambda e, k=k: e.transpose(pv[:, k * 128:(k + 1) * 128], xb.t[:, k * 128:(k + 1) * 128], self.ident),
                 reads=[xb.b, self.cst.b], writes=[p.b])
        XT = self.XT
        S.op("act", lambda e: e.activation(XT[:, :, t * 128:(t + 1) * 128], pv.rearrange("p (k n) -> p k n", k=8), AF.Copy),
             reads=[p.b], writes=[self.XTb[t]])

    def load_ln(self, li, sub):
        S = self.S
        S.dma("sp", self.G.t[:], self.ln_g[li, sub:sub + 1, :].broadcast_to([128, D]), writes=[self.G.b])
        S.dma("sp", self.Bt.t[:], self.ln_b[li, sub:sub + 1, :].broadcast_to([128, D]), writes=[self.Bt.b])

    def norm_tile_a(self, t):
        S = self.S
        X = self.X
        Xb = self.Xb[t]
        st = self.stR.next()
        mv = self.mvR.next()
        for c in range(2):
            S.op("dve", lambda e, c=c: e.bn_stats(st.t[:, c, :], X[:, t, c * 512:(c + 1) * 512]), reads=[Xb], writes=[st.b])
        S.op("dve", lambda e: e.bn_aggr(mv.t[:, 0:2], st.t[:]), reads=[st.b], writes=[mv.b])
        S.op("act", lambda e: e.activation(mv.t[:, 2:3], mv.t[:, 1:2], AF.Ln, bias=LN_EPS, scale=1.0), reads=[mv.b], writes=[mv.b])
        S.op("act", lambda e: e.activation(mv.t[:, 2:3], mv.t[:, 2:3], AF.Exp, scale=-0.5), reads=[mv.b], writes=[mv.b])
        return mv

    def norm_tile_b(self, t, mv, store_b=None):
        S = self.S
        X = self.X
        Xb = self.Xb[t]
        S.op("dve", lambda e: e.scalar_tensor_tensor(X[:, t, :], X[:, t, :], mv.t[:, 0:1], self.G.t[:], ALU.subtract, ALU.mult),
             reads=[Xb, mv.b, self.G.b], writes=[Xb])
        S.op("dve", lambda e: e.scalar_tensor_tensor(X[:, t, :], X[:, t, :], mv.t[:, 2:3], self.Bt.t[:], ALU.mult, ALU.add),
             reads=[Xb, mv.b, self.Bt.b], writes=[Xb])
        if store_b is not None:
            self.store_x(store_b, t)

    def outproj_norm(self, W, Wb, nk, lhs_fn, first, last, bias, want_xt, store_b, rowscale=None):
        S = self.S
        X = self.X
        xbs = {}

        def s_mm(t):
            pp = [self.ps(), self.ps()]
            for h in range(2):
                for k in range(nk):
                    lap, lb = lhs_fn(k, t)
                    self.mm(pp[h].t[:, :], lap, W[:, k, h * 512:(h + 1) * 512], k == 0, k == nk - 1,
                            reads=list(lb) + list(Wb), writes=[pp[h].b])
            if rowscale is not None and first:
                S.op("act", lambda e: e.activation(X[:, t, :], X[:, t, :], AF.Copy, scale=ALPHA), reads=[self.Xb[t]], writes=[self.Xb[t]])
            for h in range(2):
                xs = X[:, t, h * 512:(h + 1) * 512]
                if rowscale is not None:
                    S.op("dve", lambda e, xs=xs, p=pp[h]: e.scalar_tensor_tensor(xs, p.t[:, :], rowscale.t[:, t:t + 1], xs, ALU.mult, ALU.add),
                         reads=[self.Xb[t], pp[h].b, rowscale.b], writes=[self.Xb[t]])
                elif first:
                    S.op("dve", lambda e, xs=xs, p=pp[h]: e.scalar_tensor_tensor(xs, xs, ALPHA, p.t[:, :], ALU.mult, ALU.add),
                         reads=[self.Xb[t], pp[h].b], writes=[self.Xb[t]])
                else:
                    S.op("dve", lambda e, xs=xs, p=pp[h]: e.tensor_tensor(xs, xs, p.t[:, :], ALU.add),
                         reads=[self.Xb[t], pp[h].b], writes=[self.Xb[t]])
            if last and bias is not None:
                S.op("dve", lambda e, t=t: e.tensor_tensor(X[:, t, :], X[:, t, :], bias.t[:], ALU.add),
                     reads=[self.Xb[t], bias.b], writes=[self.Xb[t]])

        mvs = {}

        def s_ln(t):
            mvs[t] = self.norm_tile_a(t)

        def s_ln2(t):
            self.norm_tile_b(t, mvs.pop(t), store_b)
            if want_xt:
                xb = self.xbR.next()
                xbs[t] = xb
                S.op("act", lambda e: e.activation(xb.t[:], X[:, t, :], AF.Copy), reads=[self.Xb[t]], writes=[xb.b])

        def s_xt(t):
            self.xt_from(t, xbs.pop(t))

        if not last:
            pipeline(NT, [s_mm])
        elif want_xt:
            pipeline(NT, [s_mm, s_ln, s_ln2, s_xt])
        else:
            pipeline(NT, [s_mm, s_ln, s_ln2])

    def wload(self, dst_ap, src_ap, buf):
        self.S.dma("pool", dst_ap, src_ap, writes=[buf])

    def ffn(self, li, want_xt, store_b):
        S = self.S
        self.load_ln(li, 2)
        with S.scope():
            HT = S.alloc("HT", [128, 6, SEQ], BF16)
            HTb = [[S.buf("HT%d_%d" % (j, tb)) for tb in range(NTB)] for j in range(6)]
            WOr = Rot([S.tile("WO%d" % i, [128, 6, D], BF16) for i in range(2)])
            WIr = Rot([S.tile("WI%d" % i, [128, 8, 2, 128], BF16) for i in range(3)])
            sgR = Rot([S.tile("sg%d" % i, [128, 512], F32) for i in range(2)])
            win = self.ff_w_in[li].rearrange("(k p) n -> p k n", p=128)
            for gi, (c0, c1) in enumerate(FF_GROUPS):
                n = c1 - c0
                WO = WOr.next()
                self.wload(WO.t[:, 0:n, :], self.ff_w_out[li, c0 * 128:c1 * 128, :].rearrange("(k p) n -> p k n", p=128), WO.b)
                for jl in range(n):
                    j = c0 + jl
                    WI = WIr.next()
                    self.wload(WI.t[:, :, 0, :], win[:, :, j * 128:(j + 1) * 128], WI.b)
                    self.wload(WI.t[:, :, 1, :], win[:, :, DFF + j * 128:DFF + (j + 1) * 128], WI.b)
                    for tb in range(NTB):
                        pg, pu = self.ps(), self.ps()
                        for which, p in ((0, pg), (1, pu)):
                            for k in range(8):
                                self.mm(p.t[:, :], WI.t[:, k, which, :], self.XT[:, k, tb * 512:(tb + 1) * 512], k == 0, k == 7,
                                        reads=[WI.b] + self.xt_reads(tb), writes=[p.b])
                        sg = sgR.next()
                        S.op("act", lambda e, sg=sg, pg=pg: e.activation(sg.t[:], pg.t[:, :], AF.Silu), reads=[pg.b], writes=[sg.b])
                        S.op("dve", lambda e, sg=sg, pu=pu, jl=jl, tb=tb: e.tensor_tensor(HT[:, jl, tb * 512:(tb + 1) * 512], sg.t[:], pu.t[:, :], ALU.mult),
                             reads=[sg.b, pu.b], writes=[HTb[jl][tb]])
                last = gi == len(FF_GROUPS) - 1
                self.outproj_norm(WO.t, [WO.b], n, lambda k, t: (HT[:, k, t * 128:(t + 1) * 128], [HTb[k][t // 4]]),
                                  first=(gi == 0), last=last, bias=None, want_xt=want_xt, store_b=store_b)

    def prep_mem(self, b):
        S = self.S
        self.memT = S.tile("memT", [128, 8, NMEM], BF16)
        self.memf = [S.tile("memf%d" % mt, [128, D], F32) for mt in range(2)]
        if True:
            for mt in range(2):
                mf = self.memf[mt]
                S.dma("sp", mf.t[:], self.dmem[b, mt * 128:(mt + 1) * 128, :], writes=[mf.b])
                xb = self.xbR.next()
                S.op("act", lambda e, xb=xb, mf=mf: e.activation(xb.t[:], mf.t[:], AF.Copy), reads=[mf.b], writes=[xb.b])
                p = self.ps()
                pv = p.t[:].bitcast(BF16)
                for k in range(8):
                    S.op("pe", lambda e, k=k, pv=pv, xb=xb: e.transpose(pv[:, k * 128:(k + 1) * 128], xb.t[:, k * 128:(k + 1) * 128], self.ident),
                         reads=[xb.b, self.cst.b], writes=[p.b])
                S.op("dve", lambda e, pv=pv, mt=mt: e.tensor_copy(self.memT.t[:, :, mt * 128:(mt + 1) * 128], pv.rearrange("p (k n) -> p k n", k=8)),
                     reads=[p.b], writes=[self.memT.b])

    def colnorm_max(self, srcs, ncols, out_col, tmpR):
        S = self.S
        p = self.ps()
        n = len(srcs)
        for i, (ap, bufs) in enumerate(srcs):
            sq = tmpR.next()
            S.op("act", lambda e, sq=sq, ap=ap: e.activation(sq.t[:, 0:ncols], ap, AF.Square), reads=bufs, writes=[sq.b])
            self.mm(p.t[:, 0:ncols], self.ones, sq.t[:, 0:ncols], i == 0, i == n - 1, reads=[sq.b, self.cst.b], writes=[p.b])
        S.op("dve", lambda e: e.reduce_max(out_col[0], p.t[:, 0:ncols], AX.X), reads=[p.b], writes=[out_col[1]])

    def xattn(self, li, b):
        S = self.S
        self.load_ln(li, 1)
        XT = self.XT
        scale = 1.0 / 16.0
        with S.scope():
            memT = S.tile("memT", [128, 8, NMEM], BF16)
            KT = S.tile("KT", [128, 8, NMEM], BF16)
            V = S.tile("V", [128, 2, D], BF16)
            OT = S.alloc("OT", [128, 8, SEQ], BF16)
            OTb = [[S.buf("OT%d_%d" % (c, tb)) for tb in range(NTB)] for c in range(8)]
            sqR = Rot([S.tile("sq%d" % i, [128, 512], BF16) for i in range(2)])
            nrm = S.tile("nrm", [128, 16], F32)
            kmx = S.tile("kmx", [128, 4], F32)
            nbs = [S.tile("nbx%d" % i, [128, 1], F32) for i in range(4)]
            WQr = Rot([S.tile("WQ%d" % i, [128, 8, 256], BF16) for i in range(2)])
            QTs = [S.tile("QT%d" % i, [128, 2, SEQ], BF16) for i in range(2)]
            wq = self.xa_w_q[li].rearrange("(k p) n -> p k n", p=128)
            WQs = {}

            def load_wq(h):
                WQ = WQr.next()
                self.wload(WQ.t[:], wq[:, :, h * 256:(h + 1) * 256], WQ.b)
                WQs[h] = WQ

            def qproj(h):
                WQ = WQs.pop(h)
                QT = QTs[h % 2]
                if h + 1 < 4:
                    load_wq(h + 1)
                for tb in range(NTB):
                    for c in range(2):
                        p = self.ps()
                        for k in range(8):
                            self.mm(p.t[:, :], WQ.t[:, k, c * 128:(c + 1) * 128], XT[:, k, tb * 512:(tb + 1) * 512], k == 0, k == 7,
                                    reads=[WQ.b] + self.xt_reads(tb), writes=[p.b])
                        S.op("act", lambda e, p=p, c=c, tb=tb: e.activation(QT.t[:, c, tb * 512:(tb + 1) * 512], p.t[:, :], AF.Copy), reads=[p.b], writes=[QT.b])
                for tb in range(NTB):
                    self.colnorm_max([(QT.t[:, c, tb * 512:(tb + 1) * 512], [QT.b]) for c in range(2)], 512, (nrm.t[:, 4 + tb:5 + tb], nrm.b), sqR)

            def qbias(h):
                S.op("dve", lambda e: e.reduce_max(nrm.t[:, 12:13], nrm.t[:, 4:8], AX.X), reads=[nrm.b], writes=[nrm.b])
                S.op("dve", lambda e: e.tensor_tensor(nrm.t[:, 12:13], nrm.t[:, 12:13], kmx.t[:, h:h + 1], ALU.mult), reads=[nrm.b, kmx.b], writes=[nrm.b])
                S.op("act", lambda e: e.activation(nrm.t[:, 13:14], nrm.t[:, 12:13], AF.Ln), reads=[nrm.b], writes=[nrm.b])
                S.op("act", lambda e: e.activation(nrm.t[:, 13:14], nrm.t[:, 13:14], AF.Exp, scale=0.5), reads=[nrm.b], writes=[nrm.b])
                S.op("dve", lambda e: e.tensor_scalar(nbs[h].t[:, 0:1], nrm.t[:, 13:14], -scale, None, ALU.mult), reads=[nrm.b], writes=[nbs[h].b])

            load_wq(0)
            with S.scope():
                KVr = Rot([S.tile("KV%d" % i, [128, 8, 512], BF16) for i in range(2)])
                wkv = self.xa_w_kv[li].rearrange("(k p) n -> p k n", p=128)
                KVs = []
                for q4 in range(2):
                    W = KVr.next()
                    self.wload(W.t[:], wkv[:, :, q4 * 512:(q4 + 1) * 512], W.b)
                    KVs.append(W)
                memf = [S.tile("memf%d" % mt, [128, D], F32) for mt in range(2)]
                for mt in range(2):
                    mf = memf[mt]
                    S.dma("sp", mf.t[:], self.dmem[b, mt * 128:(mt + 1) * 128, :], writes=[mf.b])
                qproj(0)
                for mt in range(2):
                    mf = memf[mt]
                    xb = self.xbR.next()
                    S.op("act", lambda e, xb=xb, mf=mf: e.activation(xb.t[:], mf.t[:], AF.Copy), reads=[mf.b], writes=[xb.b])
                    p = self.ps()
                    pv = p.t[:].bitcast(BF16)
                    for k in range(8):
                        S.op("pe", lambda e, k=k, pv=pv, xb=xb: e.transpose(pv[:, k * 128:(k + 1) * 128], xb.t[:, k * 128:(k + 1) * 128], self.ident),
                             reads=[xb.b, self.cst.b], writes=[p.b])
                    S.op("dve", lambda e, pv=pv, mt=mt: e.tensor_copy(memT.t[:, :, mt * 128:(mt + 1) * 128], pv.rearrange("p (k n) -> p k n", k=8)),
                         reads=[p.b], writes=[memT.b])
                for q4 in range(4):
                    W = KVs[q4]
                    if q4 < 2:
                        for c in range(4):
                            p = self.ps()
                            for k in range(8):
                                self.mm(p.t[:, 0:NMEM], W.t[:, k, c * 128:(c + 1) * 128], memT.t[:, k, :], k == 0, k == 7,
                                        reads=[W.b, memT.b], writes=[p.b])
                            S.op("act", lambda e, p=p, cc=q4 * 4 + c: e.activation(KT.t[:, cc, :], p.t[:, 0:NMEM], AF.Copy), reads=[p.b], writes=[KT.b])
                    else:
                        for mt in range(2):
                            p = self.ps()
                            for k in range(8):
                                self.mm(p.t[:, :], memT.t[:, k, mt * 128:(mt + 1) * 128], W.t[:, k, :], k == 0, k == 7,
                                        reads=[W.b, memT.b], writes=[p.b])
                            S.op("act", lambda e, p=p, mt=mt, h2=q4 - 2: e.activation(V.t[:, mt, h2 * 512:(h2 + 1) * 512], p.t[:, :], AF.Copy), reads=[p.b], writes=[V.b])
                    if q4 + 2 < 4:
                        W2 = KVr.next()
                        self.wload(W2.t[:], wkv[:, :, (q4 + 2) * 512:(q4 + 3) * 512], W2.b)
                        KVs.append(W2)
            for h in range(4):
                self.colnorm_max([(KT.t[:, 2 * h + c, :], [KT.b]) for c in range(2)], NMEM, (kmx.t[:, h:h + 1], kmx.b), sqR)
            qbias(0)
            WO = S.tile("WOx", [128, 8, D], BF16)
            self.wload(WO.t[:], self.xa_w_out[li].rearrange("(k p) n -> p k n", p=128), WO.b)
            PTr = Rot([S.tile("PT%d" % i, [128, 2, 512], BF16) for i in range(2)])
            rvR = Rot([S.tile("rv%d" % i, [128, 512], F32) for i in range(2)])

            def attend(h):
                QT = QTs[h % 2]
                stt = {}

                def A0(tb):
                    PT = PTr.next()
                    stt[tb] = PT
                    for mt in range(2):
                        p = self.ps()
                        for c in range(2):
                            self.mm(p.t[:, :], KT.t[:, 2 * h + c, mt * 128:(mt + 1) * 128], QT.t[:, c, tb * 512:(tb + 1) * 512], c == 0, c == 1,
                                    reads=[KT.b, QT.b], writes=[p.b])
                        S.op("act", lambda e, p=p, mt=mt: e.activation(PT.t[:, mt, :], p.t[:, :], AF.Exp, bias=nbs[h].t[:, 0:1], scale=scale),
                             reads=[p.b, nbs[h].b], writes=[PT.b])

                def A1(tb):
                    PT = stt.pop(tb)
                    prs = self.ps()
                    for mt in range(2):
                        self.mm(prs.t[:, :], self.ones, PT.t[:, mt, :], mt == 0, mt == 1, reads=[PT.b, self.cst.b], writes=[prs.b])
                    rv = rvR.next()
                    S.op("dve", lambda e: e.reciprocal(rv.t[:], prs.t[:, :]), reads=[prs.b], writes=[rv.b])
                    for c in range(2):
                        p = self.ps()
                        for mt in range(2):
                            self.mm(p.t[:, :], V.t[:, mt, h * 256 + c * 128:h * 256 + (c + 1) * 128], PT.t[:, mt, :], mt == 0, mt == 1,
                                    reads=[V.b, PT.b], writes=[p.b])
                        S.op("dve", lambda e, p=p, cc=2 * h + c: e.tensor_tensor(OT[:, cc, tb * 512:(tb + 1) * 512], p.t[:, :], rv.t[:], ALU.mult),
                             reads=[p.b, rv.b], writes=[OTb[2 * h + c][tb]])
                pipeline(NTB, [A0, A1])

            for h in range(4):
                if h + 1 < 4:
                    qproj(h + 1)
                attend(h)
                if h + 1 < 4:
                    qbias(h + 1)
            self.outproj_norm(WO.t, [WO.b], 8, lambda k, t: (OT[:, k, t * 128:(t + 1) * 128], [OTb[k][t // 4]]),
                              first=True, last=True, bias=None, want_xt=True, store_b=None)

    def conformer(self, li, j):
        S = self.S
        self.load_ln(li, 0)
        with S.scope():
            cols = S.tile("cvcols", [128, CV_N], F32)
            S.dma("sp", cols.t[:], self.cv_cols[j], writes=[cols.b])
            bout = S.tile("bout", [128, D], F32)
            S.dma("sp", bout.t[:], self.cv_b_out[j:j + 1, :].broadcast_to([128, D]), writes=[bout.b])
            WO = S.tile("WOc", [128, 8, D], BF16)
            self.wload(WO.t[:], self.cv_w_out[j].rearrange("(k p) n -> p k n", p=128), WO.b)
            CV = S.alloc("CV", [128, 8, SEQ], BF16)
            CVb = [[S.buf("CV%d_%d" % (c, tb)) for tb in range(NTB)] for c in range(8)]
            with S.scope():
                WIr = Rot([S.tile("WIc%d" % i, [128, 8, 2, 128], BF16) for i in range(2)])
                GLr = Rot([S.tile("GLU%d" % i, [128, CONVW - 1 + SEQ], BF16) for i in range(2)])
                DGr = Rot([S.tile("DG%d" % i, [128, CONVW, 128], BF16) for i in range(2)])
                sgR = Rot([S.tile("sgc%d" % i, [128, 512], F32) for i in range(2)])
                win = self.cv_w_in[j].rearrange("(k p) n -> p k n", p=128)
                H0 = CONVW - 1
                for c in range(8):
                    WI = WIr.next()
                    self.wload(WI.t[:, :, 0, :], win[:, :, c * 128:(c + 1) * 128], WI.b)
                    self.wload(WI.t[:, :, 1, :], win[:, :, D + c * 128:D + (c + 1) * 128], WI.b)
                    GL = GLr.next()
                    DG = DGr.next()
                    S.op("dve", lambda e, GL=GL: e.memset(GL.t[:, 0:H0], 0.0), writes=[GL.b])
                    for k in range(CONVW):
                        S.op("dve", lambda e, DG=DG, k=k, c=c: e.tensor_scalar(DG.t[:, k, :], self.ident, cols.t[:, CV_WDW + k * 8 + c:CV_WDW + k * 8 + c + 1], None, ALU.mult),
                             reads=[self.cst.b, cols.b], writes=[DG.b])
                    for tb in range(NTB):
                        pa, pg = self.ps(), self.ps()
                        for which, p in ((0, pa), (1, pg)):
                            for k in range(8):
                                self.mm(p.t[:, :], WI.t[:, k, which, :], self.XT[:, k, tb * 512:(tb + 1) * 512], k == 0, k == 7,
                                        reads=[WI.b] + self.xt_reads(tb), writes=[p.b])
                        sg = sgR.next()
                        S.op("act", lambda e, sg=sg, pg=pg, c=c: e.activation(sg.t[:], pg.t[:, :], AF.Sigmoid, bias=cols.t[:, CV_BIN + 8 + c:CV_BIN + 9 + c], scale=1.0),
                             reads=[pg.b, cols.b], writes=[sg.b])
                        S.op("dve", lambda e, sg=sg, pa=pa, GL=GL, c=c, tb=tb: e.scalar_tensor_tensor(GL.t[:, H0 + tb * 512:H0 + (tb + 1) * 512], pa.t[:, :], cols.t[:, CV_BIN + c:CV_BIN + c + 1], sg.t[:], ALU.add, ALU.mult),
                             reads=[pa.b, sg.b, cols.b], writes=[GL.b])
                    for tb in range(NTB):
                        p = self.ps()
                        for k in range(CONVW):
                            self.mm(p.t[:, :], DG.t[:, k, :], GL.t[:, k + tb * 512:k + (tb + 1) * 512], k == 0, k == CONVW - 1,
                                    reads=[DG.b, GL.b], writes=[p.b])
                        S.op("act", lambda e, p=p, c=c, tb=tb: e.activation(CV[:, c, tb * 512:(tb + 1) * 512], p.t[:, :], AF.Identity, bias=cols.t[:, CV_BDW + c:CV_BDW + c + 1], scale=1.0),
                             reads=[p.b, cols.b], writes=[CVb[c][tb]])
            if DEBUG == "conv":
                S.dma("pool", self.dbg, CV[:], reads=[b_ for r_ in CVb for b_ in r_], out_final=True)
            sqR = Rot([S.tile("sqc%d" % i, [128, 512], BF16) for i in range(2)])
            mr = Rot([S.tile("mr%d" % i, [128, 3, 512], F32) for i in range(2)])
            t1R = Rot([S.tile("t1c%d" % i, [128, 512], F32) for i in range(2)])
            for tb in range(NTB):
                pm, pq = self.ps(), self.ps()
                for c in range(8):
                    self.mm(pm.t[:, :], self.onesD, CV[:, c, tb * 512:(tb + 1) * 512], c == 0, c == 7, reads=[CVb[c][tb], self.cst.b], writes=[pm.b])
                for c in range(8):
                    sq = sqR.next()
                    S.op("act", lambda e, sq=sq, c=c, tb=tb: e.activation(sq.t[:], CV[:, c, tb * 512:(tb + 1) * 512], AF.Square), reads=[CVb[c][tb]], writes=[sq.b])
                    self.mm(pq.t[:, :], self.onesD, sq.t[:], c == 0, c == 7, reads=[sq.b, self.cst.b], writes=[pq.b])
                m = mr.next()
                S.op("act", lambda e, m=m, pm=pm: e.activation(m.t[:, 0, :], pm.t[:, :], AF.Square), reads=[pm.b], writes=[m.b])
                S.op("dve", lambda e, m=m, pq=pq: e.tensor_tensor(m.t[:, 1, :], pq.t[:, :], m.t[:, 0, :], ALU.subtract), reads=[pq.b, m.b], writes=[m.b])
                S.op("act", lambda e, m=m: e.activation(m.t[:, 1, :], m.t[:, 1, :], AF.Sqrt, bias=LN_EPS, scale=1.0), reads=[m.b], writes=[m.b])
                S.op("dve", lambda e, m=m: e.reciprocal(m.t[:, 1, :], m.t[:, 1, :]), reads=[m.b], writes=[m.b])
                S.op("dve", lambda e, m=m, pm=pm: e.scalar_tensor_tensor(m.t[:, 2, :], pm.t[:, :], -1.0, m.t[:, 1, :], ALU.mult, ALU.mult), reads=[pm.b, m.b], writes=[m.b])
                for c in range(8):
                    t1 = t1R.next()
                    S.op("dve", lambda e, t1=t1, m=m, c=c, tb=tb: e.tensor_tensor(t1.t[:], CV[:, c, tb * 512:(tb + 1) * 512], m.t[:, 1, :], ALU.mult), reads=[CVb[c][tb], m.b], writes=[t1.b])
                    S.op("dve", lambda e, t1=t1, m=m: e.tensor_tensor(t1.t[:], t1.t[:], m.t[:, 2, :], ALU.add), reads=[t1.b, m.b], writes=[t1.b])
                    S.op("act", lambda e, t1=t1, c=c, tb=tb: e.activation(CV[:, c, tb * 512:(tb + 1) * 512], t1.t[:], AF.Silu, bias=cols.t[:, CV_LNB + c:CV_LNB + c + 1], scale=cols.t[:, CV_LNG + c:CV_LNG + c + 1]),
                         reads=[t1.b, cols.b], writes=[CVb[c][tb]])
            if DEBUG == "z":
                S.dma("pool", self.dbg, CV[:], reads=[b_ for r_ in CVb for b_ in r_], out_final=True)
            self.outproj_norm(WO.t, [WO.b], 8, lambda k, t: (CV[:, k, t * 128:(t + 1) * 128], [CVb[k][t // 4]]),
                              first=True, last=True, bias=bout, want_xt=True, store_b=None)

    def diffattn(self, li, j, b):
        S = self.S
        self.load_ln(li, 0)
        lam0 = LAMBDA_INIT[li]
        PI = float(np.pi)
        cf = self.cstf
        with S.scope():
            OT = S.alloc("OTd", [128, 8, SEQ], BF16)
            OTb = [[S.buf("OTd%d_%d" % (c, tb)) for tb in range(NTB)] for c in range(8)]
            WO = S.tile("WOd", [128, 8, D], BF16)
            self.wload(WO.t[:], self.da_w_out[j].rearrange("(k p) n -> p k n", p=128), WO.b)
            COS = S.tile("COS", [128, SEQ], BF16)
            SIN = S.tile("SIN", [128, SEQ], BF16)
            sm = S.tile("dsm", [128, 32], F32)
            gsc = S.tile("gsc", [128, 128], F32)
            with S.scope():
                lq = S.tile("lqk", [128, 4, 64], F32)
                S.dma("sp", lq.t[:].rearrange("p a d -> p (a d)"), self.da_lqk.rearrange("a d -> (a d)").partition_broadcast(128), writes=[lq.b])
                lpr = S.tile("lpr", [128, 2, 64], F32)
                S.op("dve", lambda e: e.tensor_tensor(lpr.t[:, 0, :], lq.t[:, 0, :], lq.t[:, 1, :], ALU.mult), reads=[lq.b], writes=[lpr.b])
                S.op("dve", lambda e: e.tensor_tensor(lpr.t[:, 1, :], lq.t[:, 2, :], lq.t[:, 3, :], ALU.mult), reads=[lq.b, lpr.b], writes=[lpr.b])
                S.op("dve", lambda e: e.reduce_sum(sm.t[:, 0:2], lpr.t[:], AX.X), reads=[lpr.b], writes=[sm.b])
                S.op("act", lambda e: e.activation(sm.t[:, 2:4], sm.t[:, 0:2], AF.Exp), reads=[sm.b], writes=[sm.b])
                S.op("dve", lambda e: e.tensor_tensor(sm.t[:, 4:5], sm.t[:, 3:4], sm.t[:, 2:3], ALU.subtract), reads=[sm.b], writes=[sm.b])
                S.op("dve", lambda e: e.tensor_scalar(sm.t[:, 5:6], sm.t[:, 4:5], -lam0, None, ALU.add), reads=[sm.b], writes=[sm.b])
                S.dma("sp", gsc.t[:], self.da_subln_g[0:1, :].broadcast_to([128, 128]), writes=[gsc.b])
                S.op("dve", lambda e: e.tensor_scalar(gsc.t[:], gsc.t[:], 1.0 - lam0, None, ALU.mult), reads=[gsc.b], writes=[gsc.b])
                HSEQ = SEQ // 2
                posi = S.tile("posi", [128, HSEQ], I32)
                ang = S.tile("ang", [128, HSEQ], F32)
                rr = S.tile("rr", [128, HSEQ], F32)
                r2 = S.tile("r2", [128, HSEQ], F32)
                TWO_PI = 2.0 * PI
                for hs in range(2):
                    hsl = slice(hs * HSEQ, (hs + 1) * HSEQ)
                    S.dma("sp", posi.t[:], self.dpos[b:b + 1, hsl].broadcast_to([128, HSEQ]), writes=[posi.b])
                    S.op("dve", lambda e: e.tensor_copy(ang.t[:], posi.t[:]), reads=[posi.b, ang.b], writes=[ang.b])
                    for shift, TAB, sc in ((0.25, COS, TWO_PI), (0.0, SIN, cf.t[:, 1:2])):
                        S.op("dve", lambda e, shift=shift: e.tensor_scalar(rr.t[:], ang.t[:], cf.t[:, 0:1], shift, ALU.mult, ALU.add), reads=[ang.b, cf.b, rr.b], writes=[rr.b])
                        S.op("dve", lambda e: e.tensor_copy(posi.t[:], rr.t[:]), reads=[rr.b, posi.b], writes=[posi.b])
                        S.op("dve", lambda e: e.tensor_copy(r2.t[:], posi.t[:]), reads=[posi.b, r2.b], writes=[r2.b])
                        S.op("dve", lambda e: e.tensor_tensor(rr.t[:], rr.t[:], r2.t[:], ALU.subtract), reads=[rr.b, r2.b], writes=[rr.b])
                        S.op("dve", lambda e: e.tensor_scalar(r2.t[:], rr.t[:], 0.5, None, ALU.is_gt), reads=[rr.b, r2.b], writes=[r2.b])
                        S.op("dve", lambda e: e.tensor_tensor(rr.t[:], rr.t[:], r2.t[:], ALU.subtract), reads=[rr.b, r2.b], writes=[rr.b])
                        S.op("act", lambda e, TAB=TAB, sc=sc, hsl=hsl: e.activation(TAB.t[:, hsl], rr.t[:], AF.Sin, scale=sc), reads=[rr.b, cf.b], writes=[TAB.b])
            ctx = dict(
                WXr=Rot([S.tile("Wd%d" % i, [128, 8, 128], BF16) for i in range(3)]),
                QK=[S.tile("QTd", [128, SEQ], BF16), S.tile("KTd", [128, SEQ], BF16)],
                Vh=S.tile("Vh", [128, NT, 130], BF16),
                qbR=Rot([S.tile("qb%d" % i, [128, 512], BF16) for i in range(2)]),
                t1R=Rot([S.tile("t1d%d" % i, [128, 512], F32) for i in range(2)]),
                t2R=Rot([S.tile("t2d%d" % i, [128, 512], F32) for i in range(1)]),
                sqR=Rot([S.tile("sqd%d" % i, [128, 512], BF16) for i in range(2)]),
                PTr=Rot([S.tile("PTd%d" % i, [128, 512], BF16) for i in range(4)]),
                O1=S.tile("O1", [128, 4, 128], F32),
                ONr=Rot([S.tile("ONb%d" % i, [128, 128], BF16) for i in range(4)]),
                rsR=Rot([S.tile("rsd%d" % i, [128, 16], F32) for i in range(2)]),
                OT=OT, OTb=OTb, COS=COS, SIN=SIN, sm=sm, gsc=gsc,
                wqkv=self.da_w_qkv[j].rearrange("(k p) n -> p k n", p=128))
            Vh = ctx["Vh"]
            S.op("dve", lambda e: e.memset(Vh.t[:, :, 128:130], 1.0), writes=[Vh.b])
            for h in range(8):
                self.diff_head(h, ctx)
            self.rot_banks = list(range(8))
            self.outproj_norm(WO.t, [WO.b], 8, lambda k, t: (OT[:, k, t * 128:(t + 1) * 128], [OTb[k][t // 4]]),
                              first=True, last=True, bias=None, want_xt=True, store_b=None)

    def diff_head(self, h, ctx):
        S = self.S
        XT = self.XT
        QK, Vh, OT, OTb, COS, SIN, sm, gsc, O1 = (ctx[k] for k in ("QK", "Vh", "OT", "OTb", "COS", "SIN", "sm", "gsc", "O1"))
        qbR, t1R, t2R, sqR, PTr, ONr, rsR, WXr, wqkv = (ctx[k] for k in ("qbR", "t1R", "t2R", "sqR", "PTr", "ONr", "rsR", "WXr", "wqkv"))
        scale = 0.125
        Ws = []
        for w3 in range(3):
            W = WXr.next()
            self.wload(W.t[:], wqkv[:, :, w3 * D + h * 128:w3 * D + (h + 1) * 128], W.b)
            Ws.append(W)
        self.rot_banks = list(range(8))
        st = {}

        def P0(i):
            which, tb = divmod(i, 4)
            W = Ws[which]
            sl = slice(tb * 512, (tb + 1) * 512)
            p = self.ps()
            for k in range(8):
                self.mm(p.t[:, :], W.t[:, k, :], XT[:, k, sl], k == 0, k == 7, reads=[W.b] + self.xt_reads(tb), writes=[p.b])
            qb = qbR.next()
            t1 = t1R.next()
            S.op("act", lambda e: e.activation(qb.t[:], p.t[:, :], AF.Copy), reads=[p.b], writes=[qb.b])
            S.op("dve", lambda e: e.tensor_tensor(t1.t[:], p.t[:, :], COS.t[:, sl], ALU.mult), reads=[p.b, COS.b], writes=[t1.b])
            st[i] = (qb, t1)

        def P1(i):
            which, tb = divmod(i, 4)
            T = QK[which]
            sl = slice(tb * 512, (tb + 1) * 512)
            qb, t1 = st.pop(i)
            pr = self.ps()
            self.mm(pr.t[:, :], self.prot, qb.t[:], True, True, reads=[qb.b, self.cst.b], writes=[pr.b])
            t2 = t2R.next()
            S.op("dve", lambda e: e.tensor_tensor(t2.t[:], pr.t[:, :], SIN.t[:, sl], ALU.mult), reads=[pr.b, SIN.b], writes=[t2.b])
            S.op("dve", lambda e: e.tensor_tensor(T.t[:, sl], t1.t[:], t2.t[:], ALU.add), reads=[t1.b, t2.b], writes=[T.b])
            sq = sqR.next()
            S.op("act", lambda e: e.activation(sq.t[:], T.t[:, sl], AF.Square), reads=[T.b], writes=[sq.b])
            st[("sq", i)] = sq

        def P2(i):
            which, tb = divmod(i, 4)
            sq = st.pop(("sq", i))
            for c in range(2):
                pn = self.ps()
                self.mm(pn.t[:, :], self.sel0 if c == 0 else self.sel1, sq.t[:], True, True, reads=[sq.b, self.cst.b], writes=[pn.b])
                col = 8 + which * 8 + c * 4 + tb
                S.op("dve", lambda e, pn=pn, col=col: e.reduce_max(sm.t[:, col:col + 1], pn.t[:, :], AX.X), reads=[pn.b], writes=[sm.b])
        pipeline(8, [P0, P1, P2])
        Wv = Ws[2]
        for t in range(NT):
            p = self.ps()
            for k in range(8):
                self.mm(p.t[:, 0:128], XT[:, k, t * 128:(t + 1) * 128], Wv.t[:, k, :], k == 0, k == 7, reads=[Wv.b, self.XTb[t]], writes=[p.b])
            S.op("act", lambda e, p=p, t=t: e.activation(Vh.t[:, t, 0:128], p.t[:, 0:128], AF.Copy), reads=[p.b], writes=[Vh.b])
        for c in range(2):
            S.op("dve", lambda e, c=c: e.reduce_max(sm.t[:, 6:7], sm.t[:, 8 + c * 4:12 + c * 4], AX.X), reads=[sm.b], writes=[sm.b])
            S.op("dve", lambda e, c=c: e.reduce_max(sm.t[:, 7:8], sm.t[:, 16 + c * 4:20 + c * 4], AX.X), reads=[sm.b], writes=[sm.b])
            S.op("dve", lambda e: e.tensor_tensor(sm.t[:, 6:7], sm.t[:, 6:7], sm.t[:, 7:8], ALU.mult), reads=[sm.b], writes=[sm.b])
            S.op("act", lambda e: e.activation(sm.t[:, 7:8], sm.t[:, 6:7], AF.Ln), reads=[sm.b], writes=[sm.b])
            S.op("act", lambda e: e.activation(sm.t[:, 7:8], sm.t[:, 7:8], AF.Exp, scale=0.5), reads=[sm.b], writes=[sm.b])
            S.op("dve", lambda e, c=c: e.tensor_scalar(sm.t[:, 24 + c:25 + c], sm.t[:, 7:8], -scale, None, ALU.mult), reads=[sm.b], writes=[sm.b])
        QT, KT = QK
        self.rot_banks = [0, 1, 2]
        ptb = self.banks[3]
        accs = [(self.banks[4], self.banks[5]), (self.banks[6], self.banks[7])]
        items = [(I, c, kb) for I in range(4) for c in range(2) for kb in range(4 * I + 4)]
        ast = {}

        def S0(n):
            I, c, kb = items[n]
            pb = slice(c * 64, (c + 1) * 64)
            r0 = max(0, kb - 4 * I)
            nq = (4 - r0) * 128
            q0 = I * 512 + r0 * 128
            p = self.ps()
            diag = kb >= 4 * I
            self.mm(p.t[:, 0:nq], KT.t[pb, kb * 128:(kb + 1) * 128], QT.t[pb, q0:q0 + nq], True, not diag, reads=[KT.b, QT.b], writes=[p.b])
            if diag:
                self.mm(p.t[:, 0:128], self.ident, self.negm, False, True, reads=[self.cst.b], writes=[p.b])
            ast[n] = p

        def S1(n):
            I, c, kb = items[n]
            r0 = max(0, kb - 4 * I)
            nq = (4 - r0) * 128
            p = ast.pop(n)
            PT = PTr.next()
            S.op("act", lambda e: e.activation(PT.t[:, 0:nq], p.t[:, 0:nq], AF.Exp, bias=sm.t[:, 24 + c:25 + c], scale=scale), reads=[p.b, sm.b], writes=[PT.b])
            ast[("pt", n)] = PT

        def S2(n):
            I, c, kb = items[n]
            r0 = max(0, kb - 4 * I)
            PT = ast.pop(("pt", n))
            aset = accs[(I * 2 + c) % 2]
            for r in range(r0, 4):
                acc = aset[r // 2]
                co = (r % 2) * 256
                self.mm(acc.t[:, co:co + 129], PT.t[:, (r - r0) * 128:(r - r0 + 1) * 128], Vh.t[:, kb, 0:129], kb == 0 and r % 2 == 0, kb == 4 * I + r,
                        reads=[PT.b, Vh.b], writes=[acc.b], sgc=True)

        def evac_all(I, c):
            aset = accs[(I * 2 + c) % 2]
            rs = rsR.next()
            for r in range(4):
                acc = aset[r // 2]
                co = (r % 2) * 256
                S.op("dve", lambda e, acc=acc, co=co, r=r: e.reciprocal(rs.t[:, r:r + 1], acc.t[:, co + 128:co + 129]), reads=[acc.b], writes=[rs.b])
                if c == 0:
                    S.op("dve", lambda e, acc=acc, co=co, r=r: e.tensor_scalar(O1.t[:, r, :], acc.t[:, co:co + 128], rs.t[:, r:r + 1], None, ALU.mult),
                         reads=[acc.b, rs.b, O1.b], writes=[O1.b])
                else:
                    S.op("dve", lambda e, r=r: e.tensor_tensor(rs.t[:, 4 + r:5 + r], rs.t[:, r:r + 1], sm.t[:, 5:6], ALU.mult), reads=[rs.b, sm.b], writes=[rs.b])
                    S.op("dve", lambda e, acc=acc, co=co, r=r: e.scalar_tensor_tensor(O1.t[:, r, :], acc.t[:, co:co + 128], rs.t[:, 4 + r:5 + r], O1.t[:, r, :], ALU.mult, ALU.add),
                         reads=[acc.b, rs.b, O1.b], writes=[O1.b])
            if c == 0:
                return
            ONs = [ONr.next() for r in range(4)]
            for r in range(4):
                S.op("act", lambda e, r=r: e.activation(ONs[r].t[:], O1.t[:, r, :], AF.Square, accum_out=rs.t[:, 8 + r:9 + r]), reads=[O1.b], writes=[ONs[r].b, rs.b])
            S.op("act", lambda e: e.activation(rs.t[:, 8:12], rs.t[:, 8:12], AF.Ln, bias=1e-5, scale=1.0 / 128.0), reads=[rs.b], writes=[rs.b])
            S.op("act", lambda e: e.activation(rs.t[:, 8:12], rs.t[:, 8:12], AF.Exp, scale=-0.5), reads=[rs.b], writes=[rs.b])
            for r in range(4):
                ON = ONs[r]
                S.op("dve", lambda e, r=r, ON=ON: e.scalar_tensor_tensor(ON.t[:], O1.t[:, r, :], rs.t[:, 8 + r:9 + r], gsc.t[:], ALU.mult, ALU.mult),
                     reads=[O1.b, rs.b, gsc.b, ON.b], writes=[ON.b])

            def transposes():
                ptv = ptb.t[:].bitcast(BF16)
                for r in range(4):
                    ON = ONs[r]
                    S.op("pe", lambda e, ON=ON, r=r: e.transpose(ptv[:, r * 128:(r + 1) * 128], ON.t[:], self.ident), reads=[ON.b, self.cst.b], writes=[ptb.b])
                S.op("act", lambda e: e.activation(OT[:, h, I * 512:(I + 1) * 512], ptv[:, 0:512], AF.Copy), reads=[ptb.b], writes=[OTb[h][I]])
            return transposes

        deferred = []

        def S3(n):
            I, c, kb = items[n]
            while deferred and deferred[0][0] <= n:
                deferred.pop(0)[1]()
            if kb == 4 * I + 3:
                fn = evac_all(I, c)
                if fn is not None:
                    deferred.append((n + 6, fn))
        pipeline(len(items), [S0, S1, S2, S3])
        while deferred:
            deferred.pop(0)[1]()
        self.rot_banks = list(range(8))

    def ssd(self, li, j):
        S = self.S
        self.load_ln(li, 0)
        XT = self.XT
        win = self.mb_w_in[j].rearrange("(k p) n -> p k n", p=128)
        with S.scope():
            cols = S.tile("mbcols", [128, 160], F32)
            S.dma("sp", cols.t[:], self.mb_cols[j], writes=[cols.b])
            hv = S.tile("mbhv", [128, 3, 32], F32)
            S.dma("sp", hv.t[:].rearrange("p a h -> p (a h)"), self.mb_hv[j].partition_broadcast(128), writes=[hv.b])
            DT = S.tile("DT", [128, NT, 32], F32)
            DTA = S.tile("DTA", [128, NT, 32], F32)
            ACU = S.tile("ACU", [128, NT, 32], F32)
            EA = S.tile("EA", [128, NT, 32], F32)
            W2 = S.tile("W2", [128, NT, 32], F32)
            ET = S.tile("ET", [128, NT, 32], F32)
            v3 = lambda ap: ap.rearrange("p (t h) -> p t h", t=NT)
            with S.scope():
                Wdt = S.tile("Wdt", [128, 8, 32], F32)
                S.dma("sp", Wdt.t[:], win[:, :, 6144:6176], writes=[Wdt.b])
                L1 = S.tile("L1", [128, NT, 32], F32)
                X32r = Rot([S.tile("X32_%d" % i, [128, 8, 128], F32) for i in range(2)])
                self.rot_banks = list(range(7))
                p = self.banks[7]
                for t in range(NT):
                    X32 = X32r.next()
                    for hf in range(2):
                        pq = self.ps()
                        for kk in range(4):
                            k = hf * 4 + kk
                            self.mm(pq.t[:, kk * 128:(kk + 1) * 128], self.X[:, t, k * 128:(k + 1) * 128], self.ident32, True, True,
                                    reads=[self.Xb[t], self.cst32.b], writes=[pq.b], sgc=True)
                        S.op("act" if hf == 0 else "dve", (lambda e, pq=pq, X32=X32, hf=hf: e.activation(X32.t[:, hf * 4:hf * 4 + 4, :], pq.t[:, :].rearrange("p (k n) -> p k n", k=4), AF.Copy)) if hf == 0 else
                             (lambda e, pq=pq, X32=X32, hf=hf: e.tensor_copy(X32.t[:, hf * 4:hf * 4 + 4, :], pq.t[:, :].rearrange("p (k n) -> p k n", k=4))),
                             reads=[pq.b], writes=[X32.b])
                    for k in range(8):
                        self.mm(p.t[:, t * 32:(t + 1) * 32], X32.t[:, k, :], Wdt.t[:, k, :], t == 0 and k == 0, k == 7,
                                reads=[Wdt.b, X32.b], writes=[p.b], sgc=True)
                self.rot_banks = list(range(8))
                S.op("dve", lambda e, p=p: e.tensor_tensor(DT.t[:], v3(p.t[:, :]), hv.t[:, 0:1, :].to_broadcast([128, NT, 32]), ALU.add), reads=[p.b, hv.b], writes=[DT.b])
                S.op("act", lambda e: e.activation(L1.t[:], DT.t[:], AF.Abs), reads=[DT.b], writes=[L1.b])
                S.op("act", lambda e: e.activation(L1.t[:], L1.t[:], AF.Exp, scale=-1.0), reads=[L1.b], writes=[L1.b])
                S.op("act", lambda e: e.activation(L1.t[:], L1.t[:], AF.Ln, bias=1.0, scale=1.0), reads=[L1.b], writes=[L1.b])
                S.op("dve", lambda e: e.scalar_tensor_tensor(DT.t[:], DT.t[:], 0.0, L1.t[:], ALU.max, ALU.add), reads=[DT.b, L1.b], writes=[DT.b])
                S.op("act", lambda e: e.activation(hv.t[:, 1, :], hv.t[:, 1, :], AF.Exp), reads=[hv.b], writes=[hv.b])
                S.op("dve", lambda e: e.scalar_tensor_tensor(DTA.t[:], DT.t[:], -1.0, hv.t[:, 1:2, :].to_broadcast([128, NT, 32]), ALU.mult, ALU.mult), reads=[DT.b, hv.b], writes=[DTA.b])
                pc, pt = self.ps(), self.ps()
                for ch in range(NT):
                    self.mm(pc.t[:, ch * 32:(ch + 1) * 32], self.tri32, DTA.t[:, ch, :], ch == 0, True, reads=[self.cst32.b, DTA.b], writes=[pc.b], sgc=True)
                for ch in range(NT):
                    self.mm(pt.t[:, ch * 32:(ch + 1) * 32], self.ones32, DTA.t[:, ch, :], ch == 0, True, reads=[self.cst32.b, DTA.b], writes=[pt.b], sgc=True)
                S.op("act", lambda e, pc=pc: e.activation(ACU.t[:], v3(pc.t[:, :]), AF.Copy), reads=[pc.b], writes=[ACU.b])
                S.op("act", lambda e: e.activation(EA.t[:], ACU.t[:], AF.Exp), reads=[ACU.b], writes=[EA.b])
                S.op("act", lambda e, pt=pt: e.activation(ET.t[:], v3(pt.t[:, :]), AF.Exp), reads=[pt.b], writes=[ET.b])
                S.op("dve", lambda e, pt=pt: e.tensor_tensor(W2.t[:], v3(pt.t[:, :]), ACU.t[:], ALU.subtract), reads=[pt.b, ACU.b], writes=[W2.b])
                S.op("act", lambda e: e.activation(W2.t[:], W2.t[:], AF.Exp), reads=[W2.b], writes=[W2.b])
                S.op("dve", lambda e: e.tensor_tensor(W2.t[:], W2.t[:], DT.t[:], ALU.mult), reads=[W2.b, DT.b], writes=[W2.b])
            if DEBUG == "ssd":
                for i_, T_ in enumerate((DT, DTA, ACU, EA, W2, ET)):
                    S.dma("sp", self.dbg[:, i_, 0:512], T_.t[:].rearrange("p t h -> p (t h)"), reads=[T_.b], out_final=True)
            WXr = Rot([S.tile("WX%d" % i, [128, 8, 128], BF16) for i in range(3)])
            WZs = [S.tile("WZ%d" % i, [128, 8, 256], BF16) for i in range(2)]
            WOs = [S.tile("WOg%d" % i, [128, 2, D], BF16) for i in range(2)]
            CIs = [S.tile("CI%d" % i, [128, 3 + SEQ], BF16) for i in range(2)]
            for CI in CIs:
                S.op("dve", lambda e, CI=CI: e.memset(CI.t[:, 0:3], 0.0), writes=[CI.b])
            self.wload(WZs[0].t[:], win[:, :, 0:256], WZs[0].b)
            self.wload(WOs[0].t[:], self.mb_w_out[j, 0:256, :].rearrange("(k p) n -> p k n", p=128), WOs[0].b)
            for g in range(8):
                self.ssd_group(j, g, win, cols, hv, DT, DTA, ACU, EA, W2, ET, WXr, WZs, WOs, CIs)

    def ssd_group(self, j, g, win, cols, hv, DT, DTA, ACU, EA, W2, ET, WXr, WZs, WOs, CIs):
        S = self.S
        XT = self.XT
        WZ, WOg = WZs[g % 2], WOs[g % 2]
        if g < 7:
            self.wload(WZs[(g + 1) % 2].t[:], win[:, :, (g + 1) * 256:(g + 2) * 256], WZs[(g + 1) % 2].b)
            self.wload(WOs[(g + 1) % 2].t[:], self.mb_w_out[j, (g + 1) * 256:(g + 2) * 256, :].rearrange("(k p) n -> p k n", p=128), WOs[(g + 1) % 2].b)
        wcol = [2048 + g * 256, 2048 + g * 256 + 128, 4096 + g * 128, 5120 + g * 128]
        fcs = [2 * g, 2 * g + 1, 16 + g, 24 + g]
        with S.scope():
            FBC = S.alloc("FBC", [128, 2, SEQ], BF16)
            FBCb = [[S.buf("FBC%d_%d" % (i, tb)) for tb in range(NTB)] for i in range(2)]
            XTK = S.tile("XTK", [128, NT, 256], BF16)
            BTK = S.tile("BTK", [128, NT, 128], BF16)
            NG = S.tile("NGc", [128, 2], F32)
            S.dma("sp", NG.t[:], self.mb_ngcol[j, :, 2 * g:2 * g + 2], writes=[NG.b])
            SS = S.tile("SS", [128, NT], F32)
            DGD = S.tile("DGD", [128, 4, 128], BF16)
            for h in range(4):
                hh = 4 * g + h
                S.op("dve", lambda e, h=h, hh=hh: e.tensor_scalar(DGD.t[:, h, :], self.ident, hv.t[:, 2, hh:hh + 1], None, ALU.mult),
                     reads=[self.cst.b, hv.b, DGD.b], writes=[DGD.b])
            DGr = Rot([S.tile("DGC%d" % i, [128, 4, 128], BF16) for i in range(2)])
            with S.scope():
                FX = S.alloc("FX", [128, 2, SEQ], BF16)
                FXb = [[S.buf("FX%d_%d" % (i, tb)) for tb in range(NTB)] for i in range(2)]
                dst_of = lambda i4: (FX, FXb, i4) if i4 < 2 else (FBC, FBCb, i4 - 2)
                st = {}

                def P0(i4):
                    CI = CIs[i4 % 2]
                    W = WXr.next()
                    self.wload(W.t[:], win[:, :, wcol[i4]:wcol[i4] + 128], W.b)
                    DG = DGr.next()
                    st[i4] = DG
                    for k in range(4):
                        col = k * 32 + fcs[i4]
                        S.op("dve", lambda e, k=k, col=col: e.tensor_scalar(DG.t[:, k, :], self.ident, cols.t[:, col:col + 1], None, ALU.mult),
                             reads=[self.cst.b, cols.b, DG.b], writes=[DG.b])
                    for tb in range(NTB):
                        p = self.ps()
                        for k in range(8):
                            self.mm(p.t[:, :], W.t[:, k, :], XT[:, k, tb * 512:(tb + 1) * 512], k == 0, k == 7,
                                    reads=[W.b] + self.xt_reads(tb), writes=[p.b])
                        S.op("act", lambda e, p=p, tb=tb: e.activation(CI.t[:, 3 + tb * 512:3 + (tb + 1) * 512], p.t[:, :], AF.Copy), reads=[p.b], writes=[CI.b])

                def P1(i4):
                    CI = CIs[i4 % 2]
                    DG = st.pop(i4)
                    fc = fcs[i4]
                    T, Tb, ti = dst_of(i4)
                    for tb in range(NTB):
                        p = self.ps()
                        for k in range(4):
                            self.mm(p.t[:, :], DG.t[:, k, :], CI.t[:, k + tb * 512:k + (tb + 1) * 512], k == 0, k == 3,
                                    reads=[DG.b, CI.b], writes=[p.b])
                        S.op("act", lambda e, p=p, tb=tb: e.activation(T[:, ti, tb * 512:(tb + 1) * 512], p.t[:, :], AF.Silu, bias=cols.t[:, 128 + fc:129 + fc], scale=1.0),
                             reads=[p.b, cols.b], writes=[Tb[ti][tb]])
                pipeline(4, [P0, P1])
                for t4 in range(4):
                    for T, Tb, ti, dst, w in ((FX, FXb, 0, XTK, 0), (FX, FXb, 1, XTK, 128), (FBC, FBCb, 0, BTK, 0)):
                        p = self.ps()
                        pv = p.t[:].bitcast(BF16)
                        for tt in range(4):
                            t = t4 * 4 + tt
                            S.op("pe", lambda e, pv=pv, tt=tt, t=t, T=T, ti=ti: e.transpose(pv[:, tt * 128:(tt + 1) * 128], T[:, ti, t * 128:(t + 1) * 128], self.ident),
                                 reads=[Tb[ti][t4], self.cst.b], writes=[p.b])
                        S.op("dve", lambda e, pv=pv, dst=dst, w=w, t4=t4: e.tensor_copy(dst.t[:, t4 * 4:t4 * 4 + 4, w:w + 128], pv[:, 0:512].rearrange("p (t n) -> p t n", t=4)),
                             reads=[p.b], writes=[dst.b])

            YT = S.alloc("YT", [128, 2, SEQ], BF16)
            YTb = [S.buf("YT%d" % tb) for tb in range(NTB)]
            HS = S.tile("HS", [128, 256], F32)
            HSbs = [S.tile("HSb%d" % i, [128, 256], BF16) for i in range(4)]
            CBr = Rot([S.tile("CBm%d" % i, [128, 128], F32) for i in range(3)])
            TMr = Rot([S.tile("TMs%d" % i, [128, 128], F32) for i in range(8)])
            WTr = Rot([S.tile("WTs%d" % i, [128, 128], BF16) for i in range(8)])
            YGr = Rot([S.tile("YG%d" % i, [128, 256], F32) for i in range(3)])
            Y2r = Rot([S.tile("Y2%d" % i, [128, 256], F32) for i in range(2)])
            ZSr = Rot([S.tile("ZSc%d" % i, [128, 256], BF16) for i in range(4)])
            THr = Rot([S.tile("THc%d" % i, [128, 256], BF16) for i in range(1)])
            XWr = Rot([S.tile("XW%d" % i, [128, 256], BF16) for i in range(3)])
            YNr = Rot([S.tile("YN%d" % i, [128, 256], BF16) for i in range(3)])
            rsR = Rot([S.tile("rss%d" % i, [128, 4], F32) for i in range(3)])
            bc4 = lambda T, ch: T.t[:, ch, 4 * g:4 * g + 4].unsqueeze(2).to_broadcast([128, 4, 64])
            v4 = lambda ap: ap.rearrange("p (h d) -> p h d", h=4)
            WTs = {}
            ZSs = {}

            PHs = {}
            YGs = {}
            YNs = {}
            CBs = {}
            TMs = {}
            XWs = {}
            RSs = {}
            phbanks = [self.banks[6], self.banks[7]]

            def T0(ch):
                tsl = slice(ch * 128, (ch + 1) * 128)
                pcb = self.ps()
                self.mm(pcb.t[:, 0:128], FBC[:, 0, tsl], FBC[:, 1, tsl], True, True, reads=[FBCb[0][ch // 4], FBCb[1][ch // 4]], writes=[pcb.b])
                CBm = CBr.next()
                CBs[ch] = CBm
                S.op("dve", lambda e: e.tensor_tensor(CBm.t[:], pcb.t[:, 0:128], self.tri32, ALU.mult), reads=[pcb.b, self.cst32.b], writes=[CBm.b])
                for h in range(4):
                    hh = 4 * g + h
                    pseg = self.ps()
                    self.mm(pseg.t[:, 0:128], DTA.t[:, ch, hh:hh + 1].to_broadcast([128, 128]), self.tri32, True, True, reads=[DTA.b, self.cst32.b], writes=[pseg.b])
                    TM = TMr.next()
                    TMs[(ch, h)] = TM
                    S.op("dve", lambda e, TM=TM, pseg=pseg, hh=hh: e.tensor_scalar(TM.t[:], pseg.t[:, 0:128], ACU.t[:, ch, hh:hh + 1], 0.0, ALU.subtract, ALU.min),
                         reads=[pseg.b, ACU.b], writes=[TM.b])
                    S.op("act", lambda e, TM=TM: e.activation(TM.t[:], TM.t[:], AF.Exp), reads=[TM.b], writes=[TM.b])
                pz = self.ps()
                for k in range(8):
                    self.mm(pz.t[:, 0:256], XT[:, k, tsl], WZ.t[:, k, :], k == 0, k == 7, reads=[WZ.b, self.XTb[ch]], writes=[pz.b])
                ZS = ZSr.next()
                TH = THr.next()
                S.op("act", lambda e: e.activation(TH.t[:], pz.t[:, 0:256], AF.Tanh, scale=0.5), reads=[pz.b], writes=[TH.b])
                S.op("dve", lambda e: e.scalar_tensor_tensor(ZS.t[:], TH.t[:], 1.0, pz.t[:, 0:256], ALU.add, ALU.mult), reads=[TH.b, pz.b], writes=[ZS.b])
                ZSs[ch] = ZS
                if ch < NT - 1:
                    XW = XWr.next()
                    XWs[ch] = XW
                    S.op("pool", lambda e: e.tensor_tensor(v4(XW.t[:]), v4(XTK.t[:, ch, :]), bc4(W2, ch), ALU.mult), reads=[XTK.b, W2.b], writes=[XW.b])

            def T1(ch):
                CBm = CBs.pop(ch)
                for h in range(4):
                    hh = 4 * g + h
                    TM = TMs.pop((ch, h))
                    WT = WTr.next()
                    S.op("dve", lambda e, TM=TM, WT=WT, hh=hh: e.scalar_tensor_tensor(WT.t[:], TM.t[:], DT.t[:, ch, hh:hh + 1], CBm.t[:], ALU.mult, ALU.mult),
                         reads=[TM.b, DT.b, CBm.b], writes=[WT.b])
                    WTs[(ch, h)] = WT
                if ch < NT - 1:
                    XW = XWs.pop(ch)
                    ph = phbanks[ch % 2]
                    self.mm(ph.t[:, 0:256], BTK.t[:, ch, :], XW.t[:], True, True, reads=[BTK.b, XW.b], writes=[ph.b])
                    PHs[ch] = ph

            def T2(ch):
                tsl = slice(ch * 128, (ch + 1) * 128)
                py = self.ps()
                for h in range(4):
                    WT = WTs.pop((ch, h))
                    self.mm(py.t[:, h * 64:(h + 1) * 64], WT.t[:], XTK.t[:, ch, h * 64:(h + 1) * 64], True, False, reads=[WT.b, XTK.b], writes=[py.b], sgc=True)
                    self.mm(py.t[:, h * 64:(h + 1) * 64], DGD.t[:, h, :], XTK.t[:, ch, h * 64:(h + 1) * 64], False, True, reads=[DGD.b, XTK.b], writes=[py.b], sgc=True)
                YG = YGr.next()
                YGs[ch] = YG
                if ch > 0:
                    HSb = HSbs[ch % 4]
                    po = self.ps()
                    self.mm(po.t[:, 0:256], FBC[:, 1, tsl], HSb.t[:], True, True, reads=[FBCb[1][ch // 4], HSb.b], writes=[po.b])
                    Y2 = Y2r.next()
                    S.op("dve", lambda e: e.tensor_tensor(v4(Y2.t[:]), v4(po.t[:, 0:256]), bc4(EA, ch), ALU.mult), reads=[po.b, EA.b], writes=[Y2.b])
                    S.op("dve", lambda e: e.tensor_tensor(YG.t[:], Y2.t[:], py.t[:, 0:256], ALU.add), reads=[Y2.b, py.b], writes=[YG.b])
                else:
                    S.op("act", lambda e: e.activation(YG.t[:], py.t[:, 0:256], AF.Copy), reads=[py.b], writes=[YG.b])

            def R(ch):
                if ch < NT - 1:
                    ph = PHs.pop(ch)
                    if ch == 0:
                        S.op("act", lambda e: e.activation(HS.t[:], ph.t[:, 0:256], AF.Copy), reads=[ph.b], writes=[HS.b])
                    else:
                        S.op("dve", lambda e: e.tensor_tensor(v4(HS.t[:]), v4(HS.t[:]), bc4(ET, ch), ALU.mult), reads=[HS.b, ET.b], writes=[HS.b])
                        S.op("dve", lambda e: e.tensor_tensor(HS.t[:], HS.t[:], ph.t[:, 0:256], ALU.add), reads=[HS.b, ph.b], writes=[HS.b])
                    HSn = HSbs[(ch + 1) % 4]
                    S.op("act", lambda e: e.activation(HSn.t[:], HS.t[:], AF.Copy), reads=[HS.b], writes=[HSn.b])

            def T3(ch):
                YG = YGs.pop(ch)
                ZS = ZSs.pop(ch)
                YN = YNr.next()
                YNs[ch] = YN
                S.op("pool", lambda e: e.tensor_tensor(YN.t[:], YG.t[:], ZS.t[:], ALU.mult), reads=[YG.b, ZS.b], writes=[YN.b])
                sqj = Y2r.next()
                S.op("act", lambda e: e.activation(sqj.t[:], YN.t[:], AF.Square, accum_out=SS.t[:, ch:ch + 1]), reads=[YN.b], writes=[sqj.b, SS.b])

            def T5(ch):
                tsl = slice(ch * 128, (ch + 1) * 128)
                YN = YNs.pop(ch)
                ptp = self.ps()
                ptv = ptp.t[:].bitcast(BF16)
                for c2 in range(2):
                    S.op("pe", lambda e, c2=c2: e.transpose(ptv[:, c2 * 128:(c2 + 1) * 128], YN.t[:, c2 * 128:(c2 + 1) * 128], self.ident),
                         reads=[YN.b, self.cst.b], writes=[ptp.b])
                for c2 in range(2):
                    S.op("act", lambda e, c2=c2: e.activation(YT[:, c2, tsl], ptv[:, c2 * 128:(c2 + 1) * 128], AF.Copy, scale=NG.t[:, c2:c2 + 1]),
                         reads=[ptp.b, NG.b], writes=[YTb[ch // 4]])

            self.rot_banks = list(range(6))
            def T2x(ch):
                if ch == 0:
                    R(0)
                if ch + 1 < NT:
                    R(ch + 1)
                T2(ch)
            pipeline(NT, [T0, T1, T2x, T3, T5])
            self.rot_banks = list(range(8))
            S.op("act", lambda e: e.activation(SS.t[:], SS.t[:], AF.Sqrt, bias=4e-5, scale=1.0 / 256.0), reads=[SS.b], writes=[SS.b])
            S.op("dve", lambda e: e.reciprocal(SS.t[:], SS.t[:]), reads=[SS.b], writes=[SS.b])
            self.outproj_norm(WOg.t, [WOg.b], 2, lambda k, t: (YT[:, k, t * 128:(t + 1) * 128], [YTb[t // 4]]),
                              first=(g == 0), last=(g == 7), bias=None, want_xt=True, store_b=None, rowscale=SS)

    def build(self):
        S = self.S
        nst = len(self.stages)
        for b in range(self.nseq):
            with S.scope():
                self.load_x(b)
                for t in range(NT):
                    self.make_xt(t)
                for si, (li, sub) in enumerate(self.stages):
                    lastst = si == nst - 1
                    if sub == "a":
                        if li % 3 == 0:
                            self.conformer(li, li // 3)
                        elif li % 3 == 2:
                            self.diffattn(li, li // 3, b)
                        else:
                            self.ssd(li, li // 3)
                        if lastst:
                            for t in range(NT):
                                self.store_x(b, t)
                    elif sub == "b":
                        self.xattn(li, b)
                        if lastst:
                            for t in range(NT):
                                self.store_x(b, t)
                    else:
                        self.ffn(li, want_xt=not lastst, store_b=b if lastst else None)
        S.finish()
        return self.nc


def host_consts():
    c = np.zeros((128, NCST * 128), np.float32)
    c[:, 0:128] = np.eye(128, dtype=np.float32)
    c[:, 128:256] = 1.0
    c[:, 256:384] = 1.0 / 1024.0
    prot = np.zeros((128, 128), np.float32)
    for base in (0, 64):
        for i in range(8):
            prot[base + i, base + i + 8] = 1.0
            prot[base + i + 8, base + i] = 1.0
    c[:, 384:512] = prot
    c[0:64, 512:640] = 1.0
    c[64:128, 640:768] = 1.0
    k = np.arange(128)[:, None]
    q = np.arange(128)[None, :]
    c[:, 768:896] = np.where(q < k, -30000.0, 0.0)
    return c


def host_consts_f():
    c = np.zeros((128, 16), np.float32)
    for p in range(128):
        d = p % 64
        if d < 16:
            i = d % 8
            c[p, 0] = ROPE_THETA ** (-(2.0 * i) / 16.0) / (2.0 * np.pi)
            sgn = -1.0 if d < 8 else 1.0
        else:
            sgn = 1.0
        c[p, 1] = sgn * 2.0 * np.pi
    return c


def colpack(v):
    return np.ascontiguousarray(v.reshape(-1, 128).T)


def host_layout(inputs):
    m = {}
    m["cst"] = host_consts()
    m["cstf"] = host_consts_f()
    c32 = np.zeros((128, 384), np.float32)
    c32[:, 256:384] = np.eye(128, dtype=np.float32)
    c32[:, 0:128] = np.triu(np.ones((128, 128), np.float32))
    c32[:, 128:256] = 1.0
    m["cst32"] = c32
    mbc = np.zeros((1, 128, 160), np.float32)
    for k in range(4):
        mbc[0, :, k * 32:(k + 1) * 32] = colpack(inputs["mb_w_conv"][0, k])
    mbc[0, :, 128:160] = colpack(inputs["mb_b_conv"][0])
    m["mb_cols"] = mbc
    m["mb_ngcol"] = np.ascontiguousarray(colpack(inputs["mb_norm_g"][0])[None])
    m["mb_hv"] = np.ascontiguousarray(np.concatenate([inputs["mb_dt_bias"], inputs["mb_a_log"], inputs["mb_d"]], 1))
    m["da_lqk"] = np.ascontiguousarray(np.concatenate([inputs[k] for k in ("da_lq1", "da_lk1", "da_lq2", "da_lk2")], 0))
    cvc = np.zeros((2, 128, CV_N), np.float32)
    for j in range(2):
        cvc[j, :, CV_BIN:CV_BIN + 16] = colpack(inputs["cv_b_in"][j])
        wd = inputs["cv_w_dw"][j]
        for k in range(CONVW):
            cvc[j, :, CV_WDW + k * 8:CV_WDW + (k + 1) * 8] = colpack(wd[k])
        cvc[j, :, CV_BDW:CV_BDW + 8] = colpack(inputs["cv_b_dw"][j])
        cvc[j, :, CV_LNG:CV_LNG + 8] = colpack(inputs["cv_ln_g"][j])
        cvc[j, :, CV_LNB:CV_LNB + 8] = colpack(inputs["cv_ln_b"][j])
    m["cv_cols"] = cvc
    for k in ("mb_w_in", "mb_w_out", "mb_norm_g", "da_w_qkv", "da_w_out", "da_subln_g", "cv_w_in", "cv_w_out", "cv_b_out", "xa_w_q", "xa_w_kv", "xa_w_out", "ff_w_in", "ff_w_out", "ln_g", "ln_b"):
        m[k] = np.ascontiguousarray(inputs[k])
    return m


ALL_STAGES = [(li, s) for li in range(DEPTH) for s in "abc"]


def run(inputs, nseq, stages, batch_ids, trace=False):
    nc = Builder(nseq, stages).build()
    shared = host_layout(inputs)
    in_maps = []
    for ids in batch_ids:
        m = dict(shared)
        m["x"] = np.ascontiguousarray(inputs["x"][ids])
        m["mem"] = np.ascontiguousarray(inputs["mem"][ids])
        m["pos"] = np.ascontiguousarray(inputs["positions"][ids])
        in_maps.append(m)
    res = run_bass_kernel_spmd(nc, in_maps, core_ids=list(range(len(batch_ids))), trace=trace)
    return res


def kernel(**inputs):
    ncore = 8
    nseq = inputs["x"].shape[0] // ncore
    ids = [list(range(c * nseq, (c + 1) * nseq)) for c in range(ncore)]
    res = run(inputs, nseq, ALL_STAGES, ids)
    out = np.empty(inputs["x"].shape, np.float32)
    for c in range(ncore):
        out[ids[c]] = res.results[c]["out"]
    return out
```

```python
import contextlib
import numpy as np
import concourse.bass as bass
import concourse.mybir as mybir
from concourse.bass_utils import run_bass_kernel_spmd

F32 = mybir.dt.float32
BF16 = mybir.dt.bfloat16
I32 = mybir.dt.int32
ALU = mybir.AluOpType
AF = mybir.ActivationFunctionType
AX = mybir.AxisListType

ENGS = ("pe", "act", "dve", "pool", "sp")
EPOCH = 20000


class Buf:
    __slots__ = ("name", "w", "r", "dsem", "excl")

    def __init__(self, name, alias=None):
        self.name = name
        self.excl = False
        self.w = None
        self.r = dict(alias or {})
        self.dsem = None


class Tile:
    __slots__ = ("t", "b")

    def __init__(self, t, b):
        self.t = t
        self.b = b


def pipeline(n, stages, lag=1, hooks=None):
    ns = len(stages)
    for step in range(n + (ns - 1) * lag):
        if hooks and step in hooks:
            hooks[step]()
        for si, fn in enumerate(stages):
            c = step - si * lag
            if 0 <= c < n:
                fn(c)


class Rot:
    def __init__(self, items):
        self.items = list(items)
        self.i = 0

    def next(self):
        x = self.items[self.i % len(self.items)]
        self.i += 1
        return x


class Sched:
    def __init__(self, nc):
        self.nc = nc
        self.stack = contextlib.ExitStack()
        self.scopes = []
        self.alias = {}
        self.prog = {e: [] for e in ENGS}
        self.cnt = {e: 0 for e in ENGS}
        self.epoch = {e: 0 for e in ENGS}
        self.seen = {e: {} for e in ENGS}
        self.semh = {}
        self.dval = {}
        self.finals = {}
        self.nsem = 0
        self.uid = 0
        self.free_dsems = []
        for e in ENGS:
            self._newsem((e, 0))

    def _newsem(self, key):
        self.nsem += 1
        self.semh[key] = self.stack.enter_context(self.nc.semaphore("s%d" % self.nsem))
        return key

    @contextlib.contextmanager
    def scope(self):
        st = contextlib.ExitStack()
        self.scopes.append((st, []))
        try:
            yield
        finally:
            st2, bufs = self.scopes.pop()
            for b in bufs:
                deps = dict(b.r)
                if b.w is not None:
                    deps[b.w[0]] = max(deps.get(b.w[0], 0), b.w[1])
                for k, v in deps.items():
                    if self.alias.get(k, 0) < v:
                        self.alias[k] = v
                if b.dsem is not None:
                    self.free_dsems.append(b.dsem)
                    b.dsem = None
            st2.close()

    def alloc(self, name, shape, dtype):
        self.uid += 1
        st = self.scopes[-1][0] if self.scopes else self.stack
        return st.enter_context(self.nc.sbuf_tensor("%s_%d" % (name, self.uid), list(shape), dtype))

    def buf(self, name):
        b = Buf(name, self.alias)
        if self.scopes:
            self.scopes[-1][1].append(b)
        return b

    def tile(self, name, shape, dtype):
        return Tile(self.alloc(name, shape, dtype), self.buf(name))

    def psum_banks(self):
        out = []
        for i in range(8):
            t = self.stack.enter_context(self.nc.psum_tensor("ps%d" % i, [128, 512], F32))
            out.append(Tile(t, Buf("ps%d" % i)))
            out[-1].b.excl = True
        return out

    def _waits(self, eng, reads, writes):
        waits = {}
        seen = self.seen[eng]

        def need(k, v):
            if seen.get(k, 0) >= v:
                return
            if waits.get(k, 0) < v:
                waits[k] = v
        for b in reads:
            if b.w is not None:
                need(*b.w)
            if b.excl:
                for k, v in b.r.items():
                    if k[0] != eng:
                        need(k, v)
        for b in writes:
            if b.w is not None:
                need(*b.w)
            for k, v in b.r.items():
                need(k, v)
        if eng == "pe":
            for k in [k for k in waits if k[0] == "pe"]:
                del waits[k]
        for k, v in waits.items():
            seen[k] = v
        return list(waits.items())

    def op(self, eng, fn, reads=(), writes=()):
        waits = self._waits(eng, reads, writes)
        if self.cnt[eng] >= EPOCH:
            self.epoch[eng] += 1
            self.cnt[eng] = 0
            self._newsem((eng, self.epoch[eng]))
        self.cnt[eng] += 1
        key = (eng, self.epoch[eng])
        c = self.cnt[eng]
        self.prog[eng].append((waits, fn, key, 1))
        for b in reads:
            b.r[key] = c
        for b in writes:
            b.w = (key, c)
            b.r = {}

    def dma(self, q, out_ap, in_ap, reads=(), writes=(), out_final=False, **kw):
        waits = self._waits(q, reads, writes)
        owner = writes[0] if writes else reads[0]
        if owner.dsem is None:
            if self.free_dsems:
                owner.dsem = self.free_dsems.pop()
            else:
                owner.dsem = self._newsem(("d", self.nsem))
                self.dval[owner.dsem] = 0
        key = owner.dsem
        self.dval[key] += 16
        v = self.dval[key]
        self.prog[q].append((waits, lambda e: e.dma_start(out=out_ap, in_=in_ap, **kw), key, 16))
        for b in reads:
            b.r[key] = v
        for b in writes:
            b.w = (key, v)
            b.r = {}
        if out_final:
            self.finals[key] = v

    def finish(self):
        fw = list(self.finals.items())
        nc = self.nc
        with nc.Block() as block:
            def replay(name, e, tail=()):
                for waits, fn, key, inc in self.prog[name]:
                    for k, v in waits:
                        e.wait_ge(self.semh[k], v)
                    fn(e).then_inc(self.semh[key], inc)
                for k, v in tail:
                    e.wait_ge(self.semh[k], v)

            @block.tensor
            def _(e):
                replay("pe", e)

            @block.scalar
            def _(e):
                replay("act", e)

            @block.vector
            def _(e):
                replay("dve", e)

            @block.gpsimd
            def _(e):
                replay("pool", e)

            @block.sync
            def _(e):
                replay("sp", e, fw)
        self.stack.close()


D = 1024
SEQ = 2048
NT = 16
NTB = 4
NMEM = 256
DFF = 2816
DEPTH = 4
ALPHA = (2.0 * DEPTH) ** 0.25
LN_EPS = 1e-5
CONVW = 31
FF_GROUPS = [(0, 6), (6, 12), (12, 17), (17, 22)]
DEBUG = None
DSTOP = 99
VARIANT = 0
NCST = 8
LAMBDA_INIT = {2: 0.8 - 0.6 * float(np.exp(-0.3 * 2))}
ROPE_THETA = 500000.0

CV_BIN, CV_WDW, CV_BDW, CV_LNG, CV_LNB, CV_N = 0, 16, 16 + 248, 16 + 256, 16 + 264, 16 + 272


class Builder:
    def __init__(self, nseq, stages):
        self.nseq = nseq
        self.stages = stages
        nc = self.nc = bass.Bass("TRN2", target_bir_lowering=False)
        S = self.S = Sched(nc)

        def din(name, shape, dt=F32):
            return nc.dram_tensor(name, list(shape), dt, kind="ExternalInput").ap()
        self.dx = din("x", [nseq, SEQ, D])
        self.dmem = din("mem", [nseq, NMEM, D])
        self.dpos = din("pos", [nseq, SEQ], I32)
        self.dcst = din("cst", [128, NCST * 128])
        self.dcstf = din("cstf", [128, 16])
        self.dcst32 = din("cst32", [128, 384])
        self.mb_w_in = din("mb_w_in", [1, D, 6176])
        self.mb_w_out = din("mb_w_out", [1, 2 * D, D])
        self.mb_cols = din("mb_cols", [1, 128, 160])
        self.mb_hv = din("mb_hv", [1, 96])
        self.mb_norm_g = din("mb_norm_g", [1, 2 * D])
        self.mb_ngcol = din("mb_ngcol", [1, 128, 16])
        self.da_w_qkv = din("da_w_qkv", [1, D, 3 * D])
        self.da_w_out = din("da_w_out", [1, D, D])
        self.da_lqk = din("da_lqk", [4, 64])
        self.da_subln_g = din("da_subln_g", [1, 128])
        self.cv_w_in = din("cv_w_in", [2, D, 2 * D])
        self.cv_w_out = din("cv_w_out", [2, D, D])
        self.cv_b_out = din("cv_b_out", [2, D])
        self.cv_cols = din("cv_cols", [2, 128, CV_N])
        self.xa_w_q = din("xa_w_q", [4, D, D])
        self.xa_w_kv = din("xa_w_kv", [4, D, 2 * D])
        self.xa_w_out = din("xa_w_out", [4, D, D])
        self.ff_w_in = din("ff_w_in", [4, D, 2 * DFF])
        self.ff_w_out = din("ff_w_out", [4, DFF, D])
        self.ln_g = din("ln_g", [4, 3, D])
        self.ln_b = din("ln_b", [4, 3, D])
        self.dout = nc.dram_tensor("out", [nseq, SEQ, D], F32, kind="ExternalOutput").ap()
        self.dbg = nc.dram_tensor("dbg", [128, 8, SEQ], F32, kind="ExternalOutput").ap() if DEBUG else None

        self.X = S.alloc("X", [128, NT, D], F32)
        self.Xb = [S.buf("X%d" % t) for t in range(NT)]
        self.XT = S.alloc("XT", [128, 8, SEQ], BF16)
        self.XTb = [S.buf("XT%d" % t) for t in range(NT)]
        self.G = S.tile("G", [128, D], F32)
        self.Bt = S.tile("Bt", [128, D], F32)
        self.cst = S.tile("cst", [128, NCST * 128], BF16)
        cb = lambda i: self.cst.t[:, i * 128:(i + 1) * 128]
        self.ident, self.ones, self.onesD, self.prot, self.sel0, self.sel1, self.negm = [cb(i) for i in range(7)]
        self.cstf = S.tile("cstf", [128, 16], F32)
        S.dma("sp", self.cstf.t[:], self.dcstf, writes=[self.cstf.b])
        self.rot_banks = list(range(8))
        self.cst32 = S.tile("cst32", [128, 384], F32)
        S.dma("sp", self.cst32.t[:], self.dcst32, writes=[self.cst32.b])
        self.tri32 = self.cst32.t[:, 0:128]
        self.ones32 = self.cst32.t[:, 128:256]
        self.ident32 = self.cst32.t[:, 256:384]
        self.banks = S.psum_banks()
        self.bank_i = 0
        self.xbR = Rot([S.tile("xb%d" % i, [128, D], BF16) for i in range(2)])
        self.stR = Rot([S.tile("st%d" % i, [128, 2, 6], F32) for i in range(3)])
        self.mvR = Rot([S.tile("mv%d" % i, [128, 4], F32) for i in range(3)])
        S.dma("pool", self.cst.t[:], self.dcst, writes=[self.cst.b])

    def ps(self):
        b = self.banks[self.rot_banks[self.bank_i % len(self.rot_banks)]]
        self.bank_i += 1
        return b

    def mm(self, out, lhsT, rhs, start, stop, reads, writes, sgc=False):
        self.S.op("pe", lambda e: e.matmul(out, lhsT, rhs, start=start, stop=stop, skip_group_check=sgc), reads=reads, writes=writes)

    def xt_reads(self, tb):
        return self.XTb[4 * tb:4 * tb + 4]

    def load_x(self, b):
        S = self.S
        for t in range(NT):
            S.dma("sp", self.X[:, t, :], self.dx[b, t * 128:(t + 1) * 128, :], writes=[self.Xb[t]])

    def store_x(self, b, t):
        self.S.dma("sp", self.dout[b, t * 128:(t + 1) * 128, :], self.X[:, t, :], reads=[self.Xb[t]], out_final=True)

    def make_xt(self, t):
        S = self.S
        xb = self.xbR.next()
        X = self.X
        S.op("act", lambda e: e.activation(xb.t[:], X[:, t, :], AF.Copy), reads=[self.Xb[t]], writes=[xb.b])
        self.xt_from(t, xb)

    def xt_from(self, t, xb):
        S = self.S
        p = self.ps()
        pv = p.t[:].bitcast(BF16)
        for k in range(8):
            S.op("pe", lambda e, k=k: e.transpose(pv[:, k * 128:(k + 1) * 128], xb.t[:, k * 128:(k + 1) * 128], self.ident),
                 reads=[xb.b, self.cst.b], writes=[p.b])
        XT = self.XT
        S.op("act", lambda e: e.activation(XT[:, :, t * 128:(t + 1) * 128], pv.rearrange("p (k n) -> p k n", k=8), AF.Copy),
             reads=[p.b], writes=[self.XTb[t]])

    def load_ln(self, li, sub):
        S = self.S
        S.dma("sp", self.G.t[:], self.ln_g[li, sub:sub + 1, :].broadcast_to([128, D]), writes=[self.G.b])
        S.dma("sp", self.Bt.t[:], self.ln_b[li, sub:sub + 1, :].broadcast_to([128, D]), writes=[self.Bt.b])

    def norm_tile_a(self, t):
        S = self.S
        X = self.X
        Xb = self.Xb[t]
        st = self.stR.next()
        mv = self.mvR.next()
        for c in range(2):
            S.op("dve", lambda e, c=c: e.bn_stats(st.t[:, c, :], X[:, t, c * 512:(c + 1) * 512]), reads=[Xb], writes=[st.b])
        S.op("dve", lambda e: e.bn_aggr(mv.t[:, 0:2], st.t[:]), reads=[st.b], writes=[mv.b])
        S.op("act", lambda e: e.activation(mv.t[:, 2:3], mv.t[:, 1:2], AF.Ln, bias=LN_EPS, scale=1.0), reads=[mv.b], writes=[mv.b])
        S.op("act", lambda e: e.activation(mv.t[:, 2:3], mv.t[:, 2:3], AF.Exp, scale=-0.5), reads=[mv.b], writes=[mv.b])
        return mv

    def norm_tile_b(self, t, mv, store_b=None):
        S = self.S
        X = self.X
        Xb = self.Xb[t]
        S.op("dve", lambda e: e.scalar_tensor_tensor(X[:, t, :], X[:, t, :], mv.t[:, 0:1], self.G.t[:], ALU.subtract, ALU.mult),
             reads=[Xb, mv.b, self.G.b], writes=[Xb])
        S.op("dve", lambda e: e.scalar_tensor_tensor(X[:, t, :], X[:, t, :], mv.t[:, 2:3], self.Bt.t[:], ALU.mult, ALU.add),
             reads=[Xb, mv.b, self.Bt.b], writes=[Xb])
        if store_b is not None:
            self.store_x(store_b, t)

    def outproj_norm(self, W, Wb, nk, lhs_fn, first, last, bias, want_xt, store_b, rowscale=None, tiles=None, hooks=None):
        S = self.S
        X = self.X
        xbs = {}

        def s_mm(t):
            pp = [self.ps(), self.ps()]
            for h in range(2):
                for k in range(nk):
                    lap, lb = lhs_fn(k, t)
                    self.mm(pp[h].t[:, :], lap, W[:, k, h * 512:(h + 1) * 512], k == 0, k == nk - 1,
                            reads=list(lb) + list(Wb), writes=[pp[h].b])
            if rowscale is not None and first:
                S.op("act", lambda e: e.activation(X[:, t, :], X[:, t, :], AF.Copy, scale=ALPHA), reads=[self.Xb[t]], writes=[self.Xb[t]])
            for h in range(2):
                xs = X[:, t, h * 512:(h + 1) * 512]
                if rowscale is not None:
                    S.op("dve", lambda e, xs=xs, p=pp[h]: e.scalar_tensor_tensor(xs, p.t[:, :], rowscale.t[:, t:t + 1], xs, ALU.mult, ALU.add),
                         reads=[self.Xb[t], pp[h].b, rowscale.b], writes=[self.Xb[t]])
                elif first:
                    S.op("dve", lambda e, xs=xs, p=pp[h]: e.scalar_tensor_tensor(xs, xs, ALPHA, p.t[:, :], ALU.mult, ALU.add),
                         reads=[self.Xb[t], pp[h].b], writes=[self.Xb[t]])
                else:
                    S.op("dve", lambda e, xs=xs, p=pp[h]: e.tensor_tensor(xs, xs, p.t[:, :], ALU.add),
                         reads=[self.Xb[t], pp[h].b], writes=[self.Xb[t]])
            if last and bias is not None:
                S.op("dve", lambda e, t=t: e.tensor_tensor(X[:, t, :], X[:, t, :], bias.t[:], ALU.add),
                     reads=[self.Xb[t], bias.b], writes=[self.Xb[t]])

        mvs = {}

        def s_ln(t):
            mvs[t] = self.norm_tile_a(t)

        def s_ln2(t):
            self.norm_tile_b(t, mvs.pop(t), store_b)
            if want_xt:
                xb = self.xbR.next()
                xbs[t] = xb
                S.op("act", lambda e: e.activation(xb.t[:], X[:, t, :], AF.Copy), reads=[self.Xb[t]], writes=[xb.b])

        def s_xt(t):
            self.xt_from(t, xbs.pop(t))

        tl = list(range(NT)) if tiles is None else list(tiles)
        on = lambda f: (lambda i: f(tl[i]))
        if not last:
            pipeline(len(tl), [on(s_mm)], hooks=hooks)
        elif want_xt:
            pipeline(len(tl), [on(s_mm), on(s_ln), on(s_ln2), on(s_xt)], hooks=hooks)
        else:
            pipeline(len(tl), [on(s_mm), on(s_ln), on(s_ln2)], hooks=hooks)

    def wload(self, dst_ap, src_ap, buf):
        self.S.dma("pool", dst_ap, src_ap, writes=[buf])

    def ffn(self, li, want_xt, store_b):
        S = self.S
        self.load_ln(li, 2)
        GR = [(0, 8), (8, 15), (15, 22)]
        with S.scope():
            WIr = Rot([S.tile("WI%d" % i, [128, 8, 2, 128], BF16) for i in range(3)])
            WOr = Rot([S.tile("WO%d" % i, [128, 8, D], BF16) for i in range(2)])
            HTs = [S.alloc("HT%d" % i, [128, 8, SEQ // 2], BF16) for i in range(2)]
            HTbs = [[[S.buf("HT%d_%d_%d" % (i, j, tb)) for tb in range(2)] for j in range(8)] for i in range(2)]
            sgR = Rot([S.tile("sg%d" % i, [128, 512], F32) for i in range(2)])
            win = self.ff_w_in[li].rearrange("(k p) n -> p k n", p=128)
            units = [(half, gi) for half in range(2) for gi in range(len(GR))]
            WOs = {}

            def load_wo(u):
                half, gi = units[u]
                c0, c1 = GR[gi]
                WO = WOr.next()
                self.wload(WO.t[:, 0:c1 - c0, :], self.ff_w_out[li, c0 * 128:c1 * 128, :].rearrange("(k p) n -> p k n", p=128), WO.b)
                WOs[u] = WO

            def in_chunk(u, jl):
                half, gi = units[u]
                c0, c1 = GR[gi]
                j = c0 + jl
                HT, HTb = HTs[u % 2], HTbs[u % 2]
                WI = WIr.next()
                self.wload(WI.t[:, :, 0, :], win[:, :, j * 128:(j + 1) * 128], WI.b)
                self.wload(WI.t[:, :, 1, :], win[:, :, DFF + j * 128:DFF + (j + 1) * 128], WI.b)
                for tl in range(2):
                    tb = 2 * half + tl
                    pg, pu = self.ps(), self.ps()
                    for which, p in ((0, pg), (1, pu)):
                        for k in range(8):
                            self.mm(p.t[:, :], WI.t[:, k, which, :], self.XT[:, k, tb * 512:(tb + 1) * 512], k == 0, k == 7,
                                    reads=[WI.b] + self.xt_reads(tb), writes=[p.b])
                    sg = sgR.next()
                    S.op("act", lambda e, sg=sg, pg=pg: e.activation(sg.t[:], pg.t[:, :], AF.Silu), reads=[pg.b], writes=[sg.b])
                    S.op("dve", lambda e, sg=sg, pu=pu, tl=tl: e.tensor_tensor(HT[:, jl, tl * 512:(tl + 1) * 512], sg.t[:], pu.t[:, :], ALU.mult),
                         reads=[sg.b, pu.b], writes=[HTb[jl][tl]])

            def IN(u):
                c0, c1 = GR[units[u][1]]
                for jl in range(c1 - c0):
                    in_chunk(u, jl)

            def OUT(u, hooks=None):
                half, gi = units[u]
                c0, c1 = GR[gi]
                HT, HTb = HTs[u % 2], HTbs[u % 2]
                WO = WOs.pop(u)
                t0 = 8 * half
                last = gi == len(GR) - 1
                self.outproj_norm(WO.t, [WO.b], c1 - c0,
                                  lambda k, t: (HT[:, k, (t - t0) * 128:(t - t0 + 1) * 128], [HTb[k][(t - t0) // 4]]),
                                  first=(gi == 0), last=last, bias=None, want_xt=want_xt, store_b=store_b,
                                  tiles=range(t0, t0 + 8), hooks=hooks)

            nu = len(units)
            load_wo(0)
            IN(0)
            for u in range(nu):
                if u + 1 < nu:
                    load_wo(u + 1)
                half, gi = units[u]
                if gi == len(GR) - 1 and u + 1 < nu:
                    nch = GR[units[u + 1][1]][1] - GR[units[u + 1][1]][0]
                    OUT(u, hooks={1 + jl: (lambda jl=jl: in_chunk(u + 1, jl)) for jl in range(nch)})
                else:
                    OUT(u)
                    if u + 1 < nu:
                        IN(u + 1)

    def prep_mem(self, b):
        S = self.S
        self.memT = S.tile("memT", [128, 8, NMEM], BF16)
        self.memf = [S.tile("memf%d" % mt, [128, D], F32) for mt in range(2)]
        if True:
            for mt in range(2):
                mf = self.memf[mt]
                S.dma("sp", mf.t[:], self.dmem[b, mt * 128:(mt + 1) * 128, :], writes=[mf.b])
                xb = self.xbR.next()
                S.op("act", lambda e, xb=xb, mf=mf: e.activation(xb.t[:], mf.t[:], AF.Copy), reads=[mf.b], writes=[xb.b])
                p = self.ps()
                pv = p.t[:].bitcast(BF16)
                for k in range(8):
                    S.op("pe", lambda e, k=k, pv=pv, xb=xb: e.transpose(pv[:, k * 128:(k + 1) * 128], xb.t[:, k * 128:(k + 1) * 128], self.ident),
                         reads=[xb.b, self.cst.b], writes=[p.b])
                S.op("dve", lambda e, pv=pv, mt=mt: e.tensor_copy(self.memT.t[:, :, mt * 128:(mt + 1) * 128], pv.rearrange("p (k n) -> p k n", k=8)),
                     reads=[p.b], writes=[self.memT.b])

    def colnorm_max(self, srcs, ncols, out_col, tmpR):
        S = self.S
        p = self.ps()
        n = len(srcs)
        for i, (ap, bufs) in enumerate(srcs):
            sq = tmpR.next()
            S.op("act", lambda e, sq=sq, ap=ap: e.activation(sq.t[:, 0:ncols], ap, AF.Square), reads=bufs, writes=[sq.b])
            self.mm(p.t[:, 0:ncols], self.ones, sq.t[:, 0:ncols], i == 0, i == n - 1, reads=[sq.b, self.cst.b], writes=[p.b])
        S.op("dve", lambda e: e.reduce_max(out_col[0], p.t[:, 0:ncols], AX.X), reads=[p.b], writes=[out_col[1]])

    def xattn(self, li, b):
        S = self.S
        self.load_ln(li, 1)
        XT = self.XT
        scale = 1.0 / 16.0
        with S.scope():
            memT = S.tile("memT", [128, 8, NMEM], BF16)
            KT = S.tile("KT", [128, 8, NMEM], BF16)
            V = S.tile("V", [128, 2, D], BF16)
            OT = S.alloc("OT", [128, 8, SEQ], BF16)
            OTb = [[S.buf("OT%d_%d" % (c, tb)) for tb in range(NTB)] for c in range(8)]
            sqR = Rot([S.tile("sq%d" % i, [128, 512], BF16) for i in range(2)])
            nrm = S.tile("nrm", [128, 16], F32)
            kmx = S.tile("kmx", [128, 4], F32)
            nbs = [S.tile("nbx%d" % i, [128, 1], F32) for i in range(4)]
            WQr = Rot([S.tile("WQ%d" % i, [128, 8, 256], BF16) for i in range(2)])
            QTs = [S.tile("QT%d" % i, [128, 2, SEQ], BF16) for i in range(2)]
            wq = self.xa_w_q[li].rearrange("(k p) n -> p k n", p=128)
            WQs = {}

            def load_wq(h):
                WQ = WQr.next()
                self.wload(WQ.t[:], wq[:, :, h * 256:(h + 1) * 256], WQ.b)
                WQs[h] = WQ

            def qproj(h):
                WQ = WQs.pop(h)
                QT = QTs[h % 2]
                if h + 1 < 4:
                    load_wq(h + 1)
                for tb in range(NTB):
                    for c in range(2):
                        p = self.ps()
                        for k in range(8):
                            self.mm(p.t[:, :], WQ.t[:, k, c * 128:(c + 1) * 128], XT[:, k, tb * 512:(tb + 1) * 512], k == 0, k == 7,
                                    reads=[WQ.b] + self.xt_reads(tb), writes=[p.b])
                        S.op("act", lambda e, p=p, c=c, tb=tb: e.activation(QT.t[:, c, tb * 512:(tb + 1) * 512], p.t[:, :], AF.Copy), reads=[p.b], writes=[QT.b])
                for tb in range(NTB):
                    self.colnorm_max([(QT.t[:, c, tb * 512:(tb + 1) * 512], [QT.b]) for c in range(2)], 512, (nrm.t[:, 4 + tb:5 + tb], nrm.b), sqR)

            def qbias(h):
                S.op("dve", lambda e: e.reduce_max(nrm.t[:, 12:13], nrm.t[:, 4:8], AX.X), reads=[nrm.b], writes=[nrm.b])
                S.op("dve", lambda e: e.tensor_tensor(nrm.t[:, 12:13], nrm.t[:, 12:13], kmx.t[:, h:h + 1], ALU.mult), reads=[nrm.b, kmx.b], writes=[nrm.b])
                S.op("act", lambda e: e.activation(nrm.t[:, 13:14], nrm.t[:, 12:13], AF.Ln), reads=[nrm.b], writes=[nrm.b])
                S.op("act", lambda e: e.activation(nrm.t[:, 13:14], nrm.t[:, 13:14], AF.Exp, scale=0.5), reads=[nrm.b], writes=[nrm.b])
                S.op("dve", lambda e: e.tensor_scalar(nbs[h].t[:, 0:1], nrm.t[:, 13:14], -scale, None, ALU.mult), reads=[nrm.b], writes=[nbs[h].b])

            load_wq(0)
            with S.scope():
                KVr = Rot([S.tile("KV%d" % i, [128, 8, 512], BF16) for i in range(2)])
                wkv = self.xa_w_kv[li].rearrange("(k p) n -> p k n", p=128)
                KVs = []
                for q4 in range(2):
                    W = KVr.next()
                    self.wload(W.t[:], wkv[:, :, q4 * 512:(q4 + 1) * 512], W.b)
                    KVs.append(W)
                memf = [S.tile("memf%d" % mt, [128, D], F32) for mt in range(2)]
                for mt in range(2):
                    mf = memf[mt]
                    S.dma("sp", mf.t[:], self.dmem[b, mt * 128:(mt + 1) * 128, :], writes=[mf.b])
                qproj(0)
                for mt in range(2):
                    mf = memf[mt]
                    xb = self.xbR.next()
                    S.op("act", lambda e, xb=xb, mf=mf: e.activation(xb.t[:], mf.t[:], AF.Copy), reads=[mf.b], writes=[xb.b])
                    p = self.ps()
                    pv = p.t[:].bitcast(BF16)
                    for k in range(8):
                        S.op("pe", lambda e, k=k, pv=pv, xb=xb: e.transpose(pv[:, k * 128:(k + 1) * 128], xb.t[:, k * 128:(k + 1) * 128], self.ident),
                             reads=[xb.b, self.cst.b], writes=[p.b])
                    S.op("dve", lambda e, pv=pv, mt=mt: e.tensor_copy(memT.t[:, :, mt * 128:(mt + 1) * 128], pv.rearrange("p (k n) -> p k n", k=8)),
                         reads=[p.b], writes=[memT.b])
                for q4 in range(4):
                    W = KVs[q4]
                    if q4 < 2:
                        for c in range(4):
                            p = self.ps()
                            for k in range(8):
                                self.mm(p.t[:, 0:NMEM], W.t[:, k, c * 128:(c + 1) * 128], memT.t[:, k, :], k == 0, k == 7,
                                        reads=[W.b, memT.b], writes=[p.b])
                            S.op("act", lambda e, p=p, cc=q4 * 4 + c: e.activation(KT.t[:, cc, :], p.t[:, 0:NMEM], AF.Copy), reads=[p.b], writes=[KT.b])
                    else:
                        for mt in range(2):
                            p = self.ps()
                            for k in range(8):
                                self.mm(p.t[:, :], memT.t[:, k, mt * 128:(mt + 1) * 128], W.t[:, k, :], k == 0, k == 7,
                                        reads=[W.b, memT.b], writes=[p.b])
                            S.op("act", lambda e, p=p, mt=mt, h2=q4 - 2: e.activation(V.t[:, mt, h2 * 512:(h2 + 1) * 512], p.t[:, :], AF.Copy), reads=[p.b], writes=[V.b])
                    if q4 + 2 < 4:
                        W2 = KVr.next()
                        self.wload(W2.t[:], wkv[:, :, (q4 + 2) * 512:(q4 + 3) * 512], W2.b)
                        KVs.append(W2)
            for h in range(4):
                self.colnorm_max([(KT.t[:, 2 * h + c, :], [KT.b]) for c in range(2)], NMEM, (kmx.t[:, h:h + 1], kmx.b), sqR)
            qbias(0)
            WO = S.tile("WOx", [128, 8, D], BF16)
            self.wload(WO.t[:], self.xa_w_out[li].rearrange("(k p) n -> p k n", p=128), WO.b)
            PTr = Rot([S.tile("PT%d" % i, [128, 2, 512], BF16) for i in range(2)])
            rvR = Rot([S.tile("rv%d" % i, [128, 512], F32) for i in range(2)])

            def attend(h):
                QT = QTs[h % 2]
                stt = {}

                def A0(tb):
                    PT = PTr.next()
                    stt[tb] = PT
                    for mt in range(2):
                        p = self.ps()
                        for c in range(2):
                            self.mm(p.t[:, :], KT.t[:, 2 * h + c, mt * 128:(mt + 1) * 128], QT.t[:, c, tb * 512:(tb + 1) * 512], c == 0, c == 1,
                                    reads=[KT.b, QT.b], writes=[p.b])
                        S.op("act", lambda e, p=p, mt=mt: e.activation(PT.t[:, mt, :], p.t[:, :], AF.Exp, bias=nbs[h].t[:, 0:1], scale=scale),
                             reads=[p.b, nbs[h].b], writes=[PT.b])

                def A1(tb):
                    PT = stt.pop(tb)
                    prs = self.ps()
                    for mt in range(2):
                        self.mm(prs.t[:, :], self.ones, PT.t[:, mt, :], mt == 0, mt == 1, reads=[PT.b, self.cst.b], writes=[prs.b])
                    rv = rvR.next()
                    S.op("dve", lambda e: e.reciprocal(rv.t[:], prs.t[:, :]), reads=[prs.b], writes=[rv.b])
                    for c in range(2):
                        p = self.ps()
                        for mt in range(2):
                            self.mm(p.t[:, :], V.t[:, mt, h * 256 + c * 128:h * 256 + (c + 1) * 128], PT.t[:, mt, :], mt == 0, mt == 1,
                                    reads=[V.b, PT.b], writes=[p.b])
                        S.op("dve", lambda e, p=p, cc=2 * h + c: e.tensor_tensor(OT[:, cc, tb * 512:(tb + 1) * 512], p.t[:, :], rv.t[:], ALU.mult),
                             reads=[p.b, rv.b], writes=[OTb[2 * h + c][tb]])
                pipeline(NTB, [A0, A1])

            for h in range(4):
                if h + 1 < 4:
                    qproj(h + 1)
                attend(h)
                if h + 1 < 4:
                    qbias(h + 1)
            self.outproj_norm(WO.t, [WO.b], 8, lambda k, t: (OT[:, k, t * 128:(t + 1) * 128], [OTb[k][t // 4]]),
                              first=True, last=True, bias=None, want_xt=True, store_b=None)

    def conformer(self, li, j):
        S = self.S
        self.load_ln(li, 0)
        with S.scope():
            cols = S.tile("cvcols", [128, CV_N], F32)
            S.dma("sp", cols.t[:], self.cv_cols[j], writes=[cols.b])
            bout = S.tile("bout", [128, D], F32)
            S.dma("sp", bout.t[:], self.cv_b_out[j:j + 1, :].broadcast_to([128, D]), writes=[bout.b])
            WO = S.tile("WOc", [128, 8, D], BF16)
            self.wload(WO.t[:], self.cv_w_out[j].rearrange("(k p) n -> p k n", p=128), WO.b)
            CV = S.alloc("CV", [128, 8, SEQ], BF16)
            CVb = [[S.buf("CV%d_%d" % (c, tb)) for tb in range(NTB)] for c in range(8)]
            with S.scope():
                WIr = Rot([S.tile("WIc%d" % i, [128, 8, 2, 128], BF16) for i in range(2)])
                GLr = Rot([S.tile("GLU%d" % i, [128, CONVW - 1 + SEQ], BF16) for i in range(2)])
                DGr = Rot([S.tile("DG%d" % i, [128, CONVW, 128], BF16) for i in range(2)])
                sgR = Rot([S.tile("sgc%d" % i, [128, 512], F32) for i in range(2)])
                win = self.cv_w_in[j].rearrange("(k p) n -> p k n", p=128)
                H0 = CONVW - 1
                for c in range(8):
                    WI = WIr.next()
                    self.wload(WI.t[:, :, 0, :], win[:, :, c * 128:(c + 1) * 128], WI.b)
                    self.wload(WI.t[:, :, 1, :], win[:, :, D + c * 128:D + (c + 1) * 128], WI.b)
                    GL = GLr.next()
                    DG = DGr.next()
                    S.op("dve", lambda e, GL=GL: e.memset(GL.t[:, 0:H0], 0.0), writes=[GL.b])
                    for k in range(CONVW):
                        S.op("dve", lambda e, DG=DG, k=k, c=c: e.tensor_scalar(DG.t[:, k, :], self.ident, cols.t[:, CV_WDW + k * 8 + c:CV_WDW + k * 8 + c + 1], None, ALU.mult),
                             reads=[self.cst.b, cols.b], writes=[DG.b])
                    for tb in range(NTB):
                        pa, pg = self.ps(), self.ps()
                        for which, p in ((0, pa), (1, pg)):
                            for k in range(8):
                                self.mm(p.t[:, :], WI.t[:, k, which, :], self.XT[:, k, tb * 512:(tb + 1) * 512], k == 0, k == 7,
                                        reads=[WI.b] + self.xt_reads(tb), writes=[p.b])
                        sg = sgR.next()
                        S.op("act", lambda e, sg=sg, pg=pg, c=c: e.activation(sg.t[:], pg.t[:, :], AF.Sigmoid, bias=cols.t[:, CV_BIN + 8 + c:CV_BIN + 9 + c], scale=1.0),
                             reads=[pg.b, cols.b], writes=[sg.b])
                        S.op("dve", lambda e, sg=sg, pa=pa, GL=GL, c=c, tb=tb: e.scalar_tensor_tensor(GL.t[:, H0 + tb * 512:H0 + (tb + 1) * 512], pa.t[:, :], cols.t[:, CV_BIN + c:CV_BIN + c + 1], sg.t[:], ALU.add, ALU.mult),
                             reads=[pa.b, sg.b, cols.b], writes=[GL.b])
                    for tb in range(NTB):
                        p = self.ps()
                        for k in range(CONVW):
                            self.mm(p.t[:, :], DG.t[:, k, :], GL.t[:, k + tb * 512:k + (tb + 1) * 512], k == 0, k == CONVW - 1,
                                    reads=[DG.b, GL.b], writes=[p.b])
                        S.op("act", lambda e, p=p, c=c, tb=tb: e.activation(CV[:, c, tb * 512:(tb + 1) * 512], p.t[:, :], AF.Identity, bias=cols.t[:, CV_BDW + c:CV_BDW + c + 1], scale=1.0),
                             reads=[p.b, cols.b], writes=[CVb[c][tb]])
            if DEBUG == "conv":
                S.dma("pool", self.dbg, CV[:], reads=[b_ for r_ in CVb for b_ in r_], out_final=True)
            sqR = Rot([S.tile("sqc%d" % i, [128, 512], BF16) for i in range(2)])
            mr = Rot([S.tile("mr%d" % i, [128, 3, 512], F32) for i in range(2)])
            t1R = Rot([S.tile("t1c%d" % i, [128, 512], F32) for i in range(2)])
            for tb in range(NTB):
                pm, pq = self.ps(), self.ps()
                for c in range(8):
                    self.mm(pm.t[:, :], self.onesD, CV[:, c, tb * 512:(tb + 1) * 512], c == 0, c == 7, reads=[CVb[c][tb], self.cst.b], writes=[pm.b])
                for c in range(8):
                    sq = sqR.next()
                    S.op("act", lambda e, sq=sq, c=c, tb=tb: e.activation(sq.t[:], CV[:, c, tb * 512:(tb + 1) * 512], AF.Square), reads=[CVb[c][tb]], writes=[sq.b])
                    self.mm(pq.t[:, :], self.onesD, sq.t[:], c == 0, c == 7, reads=[sq.b, self.cst.b], writes=[pq.b])
                m = mr.next()
                S.op("act", lambda e, m=m, pm=pm: e.activation(m.t[:, 0, :], pm.t[:, :], AF.Square), reads=[pm.b], writes=[m.b])
                S.op("dve", lambda e, m=m, pq=pq: e.tensor_tensor(m.t[:, 1, :], pq.t[:, :], m.t[:, 0, :], ALU.subtract), reads=[pq.b, m.b], writes=[m.b])
                S.op("act", lambda e, m=m: e.activation(m.t[:, 1, :], m.t[:, 1, :], AF.Sqrt, bias=LN_EPS, scale=1.0), reads=[m.b], writes=[m.b])
                S.op("dve", lambda e, m=m: e.reciprocal(m.t[:, 1, :], m.t[:, 1, :]), reads=[m.b], writes=[m.b])
                S.op("dve", lambda e, m=m, pm=pm: e.scalar_tensor_tensor(m.t[:, 2, :], pm.t[:, :], -1.0, m.t[:, 1, :], ALU.mult, ALU.mult), reads=[pm.b, m.b], writes=[m.b])
                for c in range(8):
                    t1 = t1R.next()
                    S.op("dve", lambda e, t1=t1, m=m, c=c, tb=tb: e.tensor_tensor(t1.t[:], CV[:, c, tb * 512:(tb + 1) * 512], m.t[:, 1, :], ALU.mult), reads=[CVb[c][tb], m.b], writes=[t1.b])
                    S.op("dve", lambda e, t1=t1, m=m: e.tensor_tensor(t1.t[:], t1.t[:], m.t[:, 2, :], ALU.add), reads=[t1.b, m.b], writes=[t1.b])
                    S.op("act", lambda e, t1=t1, c=c, tb=tb: e.activation(CV[:, c, tb * 512:(tb + 1) * 512], t1.t[:], AF.Silu, bias=cols.t[:, CV_LNB + c:CV_LNB + c + 1], scale=cols.t[:, CV_LNG + c:CV_LNG + c + 1]),
                         reads=[t1.b, cols.b], writes=[CVb[c][tb]])
            if DEBUG == "z":
                S.dma("pool", self.dbg, CV[:], reads=[b_ for r_ in CVb for b_ in r_], out_final=True)
            self.outproj_norm(WO.t, [WO.b], 8, lambda k, t: (CV[:, k, t * 128:(t + 1) * 128], [CVb[k][t // 4]]),
                              first=True, last=True, bias=bout, want_xt=True, store_b=None)

    def diffattn(self, li, j, b):
        S = self.S
        self.load_ln(li, 0)
        lam0 = LAMBDA_INIT[li]
        PI = float(np.pi)
        cf = self.cstf
        with S.scope():
            OT = S.alloc("OTd", [128, 8, SEQ], BF16)
            OTb = [[S.buf("OTd%d_%d" % (c, tb)) for tb in range(NTB)] for c in range(8)]
            WO = S.tile("WOd", [128, 8, D], BF16)
            self.wload(WO.t[:], self.da_w_out[j].rearrange("(k p) n -> p k n", p=128), WO.b)
            COS = S.tile("COS", [128, SEQ], BF16)
            SIN = S.tile("SIN", [128, SEQ], BF16)
            sm = S.tile("dsm", [128, 32], F32)
            gsc = S.tile("gsc", [128, 128], F32)
            with S.scope():
                lq = S.tile("lqk", [128, 4, 64], F32)
                S.dma("sp", lq.t[:].rearrange("p a d -> p (a d)"), self.da_lqk.rearrange("a d -> (a d)").partition_broadcast(128), writes=[lq.b])
                lpr = S.tile("lpr", [128, 2, 64], F32)
                S.op("dve", lambda e: e.tensor_tensor(lpr.t[:, 0, :], lq.t[:, 0, :], lq.t[:, 1, :], ALU.mult), reads=[lq.b], writes=[lpr.b])
                S.op("dve", lambda e: e.tensor_tensor(lpr.t[:, 1, :], lq.t[:, 2, :], lq.t[:, 3, :], ALU.mult), reads=[lq.b, lpr.b], writes=[lpr.b])
                S.op("dve", lambda e: e.reduce_sum(sm.t[:, 0:2], lpr.t[:], AX.X), reads=[lpr.b], writes=[sm.b])
                S.op("act", lambda e: e.activation(sm.t[:, 2:4], sm.t[:, 0:2], AF.Exp), reads=[sm.b], writes=[sm.b])
                S.op("dve", lambda e: e.tensor_tensor(sm.t[:, 4:5], sm.t[:, 3:4], sm.t[:, 2:3], ALU.subtract), reads=[sm.b], writes=[sm.b])
                S.op("dve", lambda e: e.tensor_scalar(sm.t[:, 5:6], sm.t[:, 4:5], -lam0, None, ALU.add), reads=[sm.b], writes=[sm.b])
                S.dma("sp", gsc.t[:], self.da_subln_g[0:1, :].broadcast_to([128, 128]), writes=[gsc.b])
                S.op("dve", lambda e: e.tensor_scalar(gsc.t[:], gsc.t[:], 1.0 - lam0, None, ALU.mult), reads=[gsc.b], writes=[gsc.b])
                HSEQ = SEQ // 2
                posi = S.tile("posi", [128, HSEQ], I32)
                ang = S.tile("ang", [128, HSEQ], F32)
                rr = S.tile("rr", [128, HSEQ], F32)
                r2 = S.tile("r2", [128, HSEQ], F32)
                TWO_PI = 2.0 * PI
                for hs in range(2):
                    hsl = slice(hs * HSEQ, (hs + 1) * HSEQ)
                    S.dma("sp", posi.t[:], self.dpos[b:b + 1, hsl].broadcast_to([128, HSEQ]), writes=[posi.b])
                    S.op("dve", lambda e: e.tensor_copy(ang.t[:], posi.t[:]), reads=[posi.b, ang.b], writes=[ang.b])
                    for shift, TAB, sc in ((0.25, COS, TWO_PI), (0.0, SIN, cf.t[:, 1:2])):
                        S.op("dve", lambda e, shift=shift: e.tensor_scalar(rr.t[:], ang.t[:], cf.t[:, 0:1], shift, ALU.mult, ALU.add), reads=[ang.b, cf.b, rr.b], writes=[rr.b])
                        S.op("dve", lambda e: e.tensor_copy(posi.t[:], rr.t[:]), reads=[rr.b, posi.b], writes=[posi.b])
                        S.op("dve", lambda e: e.tensor_copy(r2.t[:], posi.t[:]), reads=[posi.b, r2.b], writes=[r2.b])
                        S.op("dve", lambda e: e.tensor_tensor(rr.t[:], rr.t[:], r2.t[:], ALU.subtract), reads=[rr.b, r2.b], writes=[rr.b])
                        S.op("dve", lambda e: e.tensor_scalar(r2.t[:], rr.t[:], 0.5, None, ALU.is_gt), reads=[rr.b, r2.b], writes=[r2.b])
                        S.op("dve", lambda e: e.tensor_tensor(rr.t[:], rr.t[:], r2.t[:], ALU.subtract), reads=[rr.b, r2.b], writes=[rr.b])
                        S.op("act", lambda e, TAB=TAB, sc=sc, hsl=hsl: e.activation(TAB.t[:, hsl], rr.t[:], AF.Sin, scale=sc), reads=[rr.b, cf.b], writes=[TAB.b])
            ctx = dict(
                WXr=Rot([S.tile("Wd%d" % i, [128, 8, 128], BF16) for i in range(3)]),
                QK=[S.tile("QTd", [128, SEQ], BF16), S.tile("KTd", [128, SEQ], BF16)],
                Vh=S.tile("Vh", [128, NT, 130], BF16),
                qbR=Rot([S.tile("qb%d" % i, [128, 512], BF16) for i in range(2)]),
                t1R=Rot([S.tile("t1d%d" % i, [128, 512], F32) for i in range(2)]),
                t2R=Rot([S.tile("t2d%d" % i, [128, 512], F32) for i in range(1)]),
                sqR=Rot([S.tile("sqd%d" % i, [128, 512], BF16) for i in range(2)]),
                PTr=Rot([S.tile("PTd%d" % i, [128, 512], BF16) for i in range(4)]),
                O1=S.tile("O1", [128, 4, 128], F32),
                ONr=Rot([S.tile("ONb%d" % i, [128, 128], BF16) for i in range(4)]),
                rsR=Rot([S.tile("rsd%d" % i, [128, 16], F32) for i in range(2)]),
                OT=OT, OTb=OTb, COS=COS, SIN=SIN, sm=sm, gsc=gsc,
                wqkv=self.da_w_qkv[j].rearrange("(k p) n -> p k n", p=128))
            Vh = ctx["Vh"]
            S.op("dve", lambda e: e.memset(Vh.t[:, :, 128:130], 1.0), writes=[Vh.b])
            for h in range(8):
                self.diff_head(h, ctx)
            self.rot_banks = list(range(8))
            self.outproj_norm(WO.t, [WO.b], 8, lambda k, t: (OT[:, k, t * 128:(t + 1) * 128], [OTb[k][t // 4]]),
                              first=True, last=True, bias=None, want_xt=True, store_b=None)

    def diff_head(self, h, ctx):
        S = self.S
        XT = self.XT
        QK, Vh, OT, OTb, COS, SIN, sm, gsc, O1 = (ctx[k] for k in ("QK", "Vh", "OT", "OTb", "COS", "SIN", "sm", "gsc", "O1"))
        qbR, t1R, t2R, sqR, PTr, ONr, rsR, WXr, wqkv = (ctx[k] for k in ("qbR", "t1R", "t2R", "sqR", "PTr", "ONr", "rsR", "WXr", "wqkv"))
        scale = 0.125
        Ws = []
        for w3 in range(3):
            W = WXr.next()
            self.wload(W.t[:], wqkv[:, :, w3 * D + h * 128:w3 * D + (h + 1) * 128], W.b)
            Ws.append(W)
        self.rot_banks = list(range(8))
        st = {}

        def P0(i):
            which, tb = divmod(i, 4)
            W = Ws[which]
            sl = slice(tb * 512, (tb + 1) * 512)
            p = self.ps()
            for k in range(8):
                self.mm(p.t[:, :], W.t[:, k, :], XT[:, k, sl], k == 0, k == 7, reads=[W.b] + self.xt_reads(tb), writes=[p.b])
            qb = qbR.next()
            t1 = t1R.next()
            S.op("act", lambda e: e.activation(qb.t[:], p.t[:, :], AF.Copy), reads=[p.b], writes=[qb.b])
            S.op("dve", lambda e: e.tensor_tensor(t1.t[:], p.t[:, :], COS.t[:, sl], ALU.mult), reads=[p.b, COS.b], writes=[t1.b])
            st[i] = (qb, t1)

        def P1(i):
            which, tb = divmod(i, 4)
            T = QK[which]
            sl = slice(tb * 512, (tb + 1) * 512)
            qb, t1 = st.pop(i)
            pr = self.ps()
            self.mm(pr.t[:, :], self.prot, qb.t[:], True, True, reads=[qb.b, self.cst.b], writes=[pr.b])
            t2 = t2R.next()
            S.op("dve", lambda e: e.tensor_tensor(t2.t[:], pr.t[:, :], SIN.t[:, sl], ALU.mult), reads=[pr.b, SIN.b], writes=[t2.b])
            S.op("dve", lambda e: e.tensor_tensor(T.t[:, sl], t1.t[:], t2.t[:], ALU.add), reads=[t1.b, t2.b], writes=[T.b])
            sq = sqR.next()
            S.op("act", lambda e: e.activation(sq.t[:], T.t[:, sl], AF.Square), reads=[T.b], writes=[sq.b])
            st[("sq", i)] = sq

        def P2(i):
            which, tb = divmod(i, 4)
            sq = st.pop(("sq", i))
            for c in range(2):
                pn = self.ps()
                self.mm(pn.t[:, :], self.sel0 if c == 0 else self.sel1, sq.t[:], True, True, reads=[sq.b, self.cst.b], writes=[pn.b])
                col = 8 + which * 8 + c * 4 + tb
                S.op("dve", lambda e, pn=pn, col=col: e.reduce_max(sm.t[:, col:col + 1], pn.t[:, :], AX.X), reads=[pn.b], writes=[sm.b])
        pipeline(8, [P0, P1, P2])
        Wv = Ws[2]
        for t in range(NT):
            p = self.ps()
            for k in range(8):
                self.mm(p.t[:, 0:128], XT[:, k, t * 128:(t + 1) * 128], Wv.t[:, k, :], k == 0, k == 7, reads=[Wv.b, self.XTb[t]], writes=[p.b])
            S.op("act", lambda e, p=p, t=t: e.activation(Vh.t[:, t, 0:128], p.t[:, 0:128], AF.Copy), reads=[p.b], writes=[Vh.b])
        for c in range(2):
            S.op("dve", lambda e, c=c: e.reduce_max(sm.t[:, 6:7], sm.t[:, 8 + c * 4:12 + c * 4], AX.X), reads=[sm.b], writes=[sm.b])
            S.op("dve", lambda e, c=c: e.reduce_max(sm.t[:, 7:8], sm.t[:, 16 + c * 4:20 + c * 4], AX.X), reads=[sm.b], writes=[sm.b])
            S.op("dve", lambda e: e.tensor_tensor(sm.t[:, 6:7], sm.t[:, 6:7], sm.t[:, 7:8], ALU.mult), reads=[sm.b], writes=[sm.b])
            S.op("act", lambda e: e.activation(sm.t[:, 7:8], sm.t[:, 6:7], AF.Ln), reads=[sm.b], writes=[sm.b])
            S.op("act", lambda e: e.activation(sm.t[:, 7:8], sm.t[:, 7:8], AF.Exp, scale=0.5), reads=[sm.b], writes=[sm.b])
            S.op("dve", lambda e, c=c: e.tensor_scalar(sm.t[:, 24 + c:25 + c], sm.t[:, 7:8], -scale, None, ALU.mult), reads=[sm.b], writes=[sm.b])
        QT, KT = QK
        self.rot_banks = [0, 1, 2]
        ptb = self.banks[3]
        accs = [(self.banks[4], self.banks[5]), (self.banks[6], self.banks[7])]
        items = [(I, c, kb) for I in range(4) for c in range(2) for kb in range(4 * I + 4)]
        ast = {}

        def S0(n):
            I, c, kb = items[n]
            pb = slice(c * 64, (c + 1) * 64)
            r0 = max(0, kb - 4 * I)
            nq = (4 - r0) * 128
            q0 = I * 512 + r0 * 128
            p = self.ps()
            diag = kb >= 4 * I
            self.mm(p.t[:, 0:nq], KT.t[pb, kb * 128:(kb + 1) * 128], QT.t[pb, q0:q0 + nq], True, not diag, reads=[KT.b, QT.b], writes=[p.b])
            if diag:
                self.mm(p.t[:, 0:128], self.ident, self.negm, False, True, reads=[self.cst.b], writes=[p.b])
            ast[n] = p

        def S1(n):
            I, c, kb = items[n]
            r0 = max(0, kb - 4 * I)
            nq = (4 - r0) * 128
            p = ast.pop(n)
            PT = PTr.next()
            S.op("act", lambda e: e.activation(PT.t[:, 0:nq], p.t[:, 0:nq], AF.Exp, bias=sm.t[:, 24 + c:25 + c], scale=scale), reads=[p.b, sm.b], writes=[PT.b])
            ast[("pt", n)] = PT

        def S2(n):
            I, c, kb = items[n]
            r0 = max(0, kb - 4 * I)
            PT = ast.pop(("pt", n))
            aset = accs[(I * 2 + c) % 2]
            for r in range(r0, 4):
                acc = aset[r // 2]
                co = (r % 2) * 256
                self.mm(acc.t[:, co:co + 129], PT.t[:, (r - r0) * 128:(r - r0 + 1) * 128], Vh.t[:, kb, 0:129], kb == 0 and r % 2 == 0, kb == 4 * I + r,
                        reads=[PT.b, Vh.b], writes=[acc.b], sgc=True)

        def evac_all(I, c):
            aset = accs[(I * 2 + c) % 2]
            rs = rsR.next()
            for r in range(4):
                acc = aset[r // 2]
                co = (r % 2) * 256
                S.op("dve", lambda e, acc=acc, co=co, r=r: e.reciprocal(rs.t[:, r:r + 1], acc.t[:, co + 128:co + 129]), reads=[acc.b], writes=[rs.b])
                if c == 0:
                    S.op("dve", lambda e, acc=acc, co=co, r=r: e.tensor_scalar(O1.t[:, r, :], acc.t[:, co:co + 128], rs.t[:, r:r + 1], None, ALU.mult),
                         reads=[acc.b, rs.b, O1.b], writes=[O1.b])
                else:
                    S.op("dve", lambda e, r=r: e.tensor_tensor(rs.t[:, 4 + r:5 + r], rs.t[:, r:r + 1], sm.t[:, 5:6], ALU.mult), reads=[rs.b, sm.b], writes=[rs.b])
                    S.op("dve", lambda e, acc=acc, co=co, r=r: e.scalar_tensor_tensor(O1.t[:, r, :], acc.t[:, co:co + 128], rs.t[:, 4 + r:5 + r], O1.t[:, r, :], ALU.mult, ALU.add),
                         reads=[acc.b, rs.b, O1.b], writes=[O1.b])
            if c == 0:
                return
            ONs = [ONr.next() for r in range(4)]
            for r in range(4):
                S.op("act", lambda e, r=r: e.activation(ONs[r].t[:], O1.t[:, r, :], AF.Square, accum_out=rs.t[:, 8 + r:9 + r]), reads=[O1.b], writes=[ONs[r].b, rs.b])
            S.op("act", lambda e: e.activation(rs.t[:, 8:12], rs.t[:, 8:12], AF.Ln, bias=1e-5, scale=1.0 / 128.0), reads=[rs.b], writes=[rs.b])
            S.op("act", lambda e: e.activation(rs.t[:, 8:12], rs.t[:, 8:12], AF.Exp, scale=-0.5), reads=[rs.b], writes=[rs.b])
            for r in range(4):
                ON = ONs[r]
                S.op("dve", lambda e, r=r, ON=ON: e.scalar_tensor_tensor(ON.t[:], O1.t[:, r, :], rs.t[:, 8 + r:9 + r], gsc.t[:], ALU.mult, ALU.mult),
                     reads=[O1.b, rs.b, gsc.b, ON.b], writes=[ON.b])

            def transposes():
                ptv = ptb.t[:].bitcast(BF16)
                for r in range(4):
                    ON = ONs[r]
                    S.op("pe", lambda e, ON=ON, r=r: e.transpose(ptv[:, r * 128:(r + 1) * 128], ON.t[:], self.ident), reads=[ON.b, self.cst.b], writes=[ptb.b])
                S.op("act", lambda e: e.activation(OT[:, h, I * 512:(I + 1) * 512], ptv[:, 0:512], AF.Copy), reads=[ptb.b], writes=[OTb[h][I]])
            return transposes

        deferred = []

        def S3(n):
            I, c, kb = items[n]
            while deferred and deferred[0][0] <= n:
                deferred.pop(0)[1]()
            if kb == 4 * I + 3:
                fn = evac_all(I, c)
                if fn is not None:
                    deferred.append((n + 6, fn))
        pipeline(len(items), [S0, S1, S2, S3])
        while deferred:
            deferred.pop(0)[1]()
        self.rot_banks = list(range(8))

    def ssd(self, li, j):
        S = self.S
        self.load_ln(li, 0)
        XT = self.XT
        win = self.mb_w_in[j].rearrange("(k p) n -> p k n", p=128)
        with S.scope():
            cols = S.tile("mbcols", [128, 160], F32)
            S.dma("sp", cols.t[:], self.mb_cols[j], writes=[cols.b])
            hv = S.tile("mbhv", [128, 3, 32], F32)
            S.dma("sp", hv.t[:].rearrange("p a h -> p (a h)"), self.mb_hv[j].partition_broadcast(128), writes=[hv.b])
            DT = S.tile("DT", [128, NT, 32], F32)
            DTA = S.tile("DTA", [128, NT, 32], F32)
            ACU = S.tile("ACU", [128, NT, 32], F32)
            EA = S.tile("EA", [128, NT, 32], F32)
            W2 = S.tile("W2", [128, NT, 32], F32)
            ET = S.tile("ET", [128, NT, 32], F32)
            v3 = lambda ap: ap.rearrange("p (t h) -> p t h", t=NT)
            with S.scope():
                Wdt = S.tile("Wdt", [128, 8, 32], F32)
                S.dma("sp", Wdt.t[:], win[:, :, 6144:6176], writes=[Wdt.b])
                L1 = S.tile("L1", [128, NT, 32], F32)
                X32r = Rot([S.tile("X32_%d" % i, [128, 8, 128], F32) for i in range(2)])
                self.rot_banks = list(range(7))
                p = self.banks[7]
                for t in range(NT):
                    X32 = X32r.next()
                    for hf in range(2):
                        pq = self.ps()
                        for kk in range(4):
                            k = hf * 4 + kk
                            self.mm(pq.t[:, kk * 128:(kk + 1) * 128], self.X[:, t, k * 128:(k + 1) * 128], self.ident32, True, True,
                                    reads=[self.Xb[t], self.cst32.b], writes=[pq.b], sgc=True)
                        S.op("act" if hf == 0 else "dve", (lambda e, pq=pq, X32=X32, hf=hf: e.activation(X32.t[:, hf * 4:hf * 4 + 4, :], pq.t[:, :].rearrange("p (k n) -> p k n", k=4), AF.Copy)) if hf == 0 else
                             (lambda e, pq=pq, X32=X32, hf=hf: e.tensor_copy(X32.t[:, hf * 4:hf * 4 + 4, :], pq.t[:, :].rearrange("p (k n) -> p k n", k=4))),
                             reads=[pq.b], writes=[X32.b])
                    for k in range(8):
                        self.mm(p.t[:, t * 32:(t + 1) * 32], X32.t[:, k, :], Wdt.t[:, k, :], t == 0 and k == 0, k == 7,
                                reads=[Wdt.b, X32.b], writes=[p.b], sgc=True)
                self.rot_banks = list(range(8))
                S.op("dve", lambda e, p=p: e.tensor_tensor(DT.t[:], v3(p.t[:, :]), hv.t[:, 0:1, :].to_broadcast([128, NT, 32]), ALU.add), reads=[p.b, hv.b], writes=[DT.b])
                S.op("act", lambda e: e.activation(L1.t[:], DT.t[:], AF.Abs), reads=[DT.b], writes=[L1.b])
                S.op("act", lambda e: e.activation(L1.t[:], L1.t[:], AF.Exp, scale=-1.0), reads=[L1.b], writes=[L1.b])
                S.op("act", lambda e: e.activation(L1.t[:], L1.t[:], AF.Ln, bias=1.0, scale=1.0), reads=[L1.b], writes=[L1.b])
                S.op("dve", lambda e: e.scalar_tensor_tensor(DT.t[:], DT.t[:], 0.0, L1.t[:], ALU.max, ALU.add), reads=[DT.b, L1.b], writes=[DT.b])
                S.op("act", lambda e: e.activation(hv.t[:, 1, :], hv.t[:, 1, :], AF.Exp), reads=[hv.b], writes=[hv.b])
                S.op("dve", lambda e: e.scalar_tensor_tensor(DTA.t[:], DT.t[:], -1.0, hv.t[:, 1:2, :].to_broadcast([128, NT, 32]), ALU.mult, ALU.mult), reads=[DT.b, hv.b], writes=[DTA.b])
                pc, pt = self.ps(), self.ps()
                for ch in range(NT):
                    self.mm(pc.t[:, ch * 32:(ch + 1) * 32], self.tri32, DTA.t[:, ch, :], ch == 0, True, reads=[self.cst32.b, DTA.b], writes=[pc.b], sgc=True)
                for ch in range(NT):
                    self.mm(pt.t[:, ch * 32:(ch + 1) * 32], self.ones32, DTA.t[:, ch, :], ch == 0, True, reads=[self.cst32.b, DTA.b], writes=[pt.b], sgc=True)
                S.op("act", lambda e, pc=pc: e.activation(ACU.t[:], v3(pc.t[:, :]), AF.Copy), reads=[pc.b], writes=[ACU.b])
                S.op("act", lambda e: e.activation(EA.t[:], ACU.t[:], AF.Exp), reads=[ACU.b], writes=[EA.b])
                S.op("act", lambda e, pt=pt: e.activation(ET.t[:], v3(pt.t[:, :]), AF.Exp), reads=[pt.b], writes=[ET.b])
                S.op("dve", lambda e, pt=pt: e.tensor_tensor(W2.t[:], v3(pt.t[:, :]), ACU.t[:], ALU.subtract), reads=[pt.b, ACU.b], writes=[W2.b])
                S.op("act", lambda e: e.activation(W2.t[:], W2.t[:], AF.Exp), reads=[W2.b], writes=[W2.b])
                S.op("dve", lambda e: e.tensor_tensor(W2.t[:], W2.t[:], DT.t[:], ALU.mult), reads=[W2.b, DT.b], writes=[W2.b])
            if DEBUG == "ssd":
                for i_, T_ in enumerate((DT, DTA, ACU, EA, W2, ET)):
                    S.dma("sp", self.dbg[:, i_, 0:512], T_.t[:].rearrange("p t h -> p (t h)"), reads=[T_.b], out_final=True)
            WXr = Rot([S.tile("WX%d" % i, [128, 8, 128], BF16) for i in range(3)])
            WZs = [S.tile("WZ%d" % i, [128, 8, 256], BF16) for i in range(2)]
            WOs = [S.tile("WOg%d" % i, [128, 2, D], BF16) for i in range(2)]
            CIs = [S.tile("CI%d" % i, [128, 3 + SEQ], BF16) for i in range(2)]
            for CI in CIs:
                S.op("dve", lambda e, CI=CI: e.memset(CI.t[:, 0:3], 0.0), writes=[CI.b])
            self.wload(WZs[0].t[:], win[:, :, 0:256], WZs[0].b)
            self.wload(WOs[0].t[:], self.mb_w_out[j, 0:256, :].rearrange("(k p) n -> p k n", p=128), WOs[0].b)
            for g in range(8):
                self.ssd_group(j, g, win, cols, hv, DT, DTA, ACU, EA, W2, ET, WXr, WZs, WOs, CIs)

    def ssd_group(self, j, g, win, cols, hv, DT, DTA, ACU, EA, W2, ET, WXr, WZs, WOs, CIs):
        S = self.S
        XT = self.XT
        WZ, WOg = WZs[g % 2], WOs[g % 2]
        if g < 7:
            self.wload(WZs[(g + 1) % 2].t[:], win[:, :, (g + 1) * 256:(g + 2) * 256], WZs[(g + 1) % 2].b)
            self.wload(WOs[(g + 1) % 2].t[:], self.mb_w_out[j, (g + 1) * 256:(g + 2) * 256, :].rearrange("(k p) n -> p k n", p=128), WOs[(g + 1) % 2].b)
        wcol = [2048 + g * 256, 2048 + g * 256 + 128, 4096 + g * 128, 5120 + g * 128]
        fcs = [2 * g, 2 * g + 1, 16 + g, 24 + g]
        with S.scope():
            FBC = S.alloc("FBC", [128, 2, SEQ], BF16)
            FBCb = [[S.buf("FBC%d_%d" % (i, tb)) for tb in range(NTB)] for i in range(2)]
            XTK = S.tile("XTK", [128, NT, 256], BF16)
            BTK = S.tile("BTK", [128, NT, 128], BF16)
            NG = S.tile("NGc", [128, 2], F32)
            S.dma("sp", NG.t[:], self.mb_ngcol[j, :, 2 * g:2 * g + 2], writes=[NG.b])
            SS = S.tile("SS", [128, NT], F32)
            DGD = S.tile("DGD", [128, 4, 128], BF16)
            for h in range(4):
                hh = 4 * g + h
                S.op("dve", lambda e, h=h, hh=hh: e.tensor_scalar(DGD.t[:, h, :], self.ident, hv.t[:, 2, hh:hh + 1], None, ALU.mult),
                     reads=[self.cst.b, hv.b, DGD.b], writes=[DGD.b])
            DGr = Rot([S.tile("DGC%d" % i, [128, 4, 128], BF16) for i in range(2)])
            with S.scope():
                FX = S.alloc("FX", [128, 2, SEQ], BF16)
                FXb = [[S.buf("FX%d_%d" % (i, tb)) for tb in range(NTB)] for i in range(2)]
                dst_of = lambda i4: (FX, FXb, i4) if i4 < 2 else (FBC, FBCb, i4 - 2)
                st = {}

                def P0(i4):
                    CI = CIs[i4 % 2]
                    W = WXr.next()
                    self.wload(W.t[:], win[:, :, wcol[i4]:wcol[i4] + 128], W.b)
                    DG = DGr.next()
                    st[i4] = DG
                    for k in range(4):
                        col = k * 32 + fcs[i4]
                        S.op("dve", lambda e, k=k, col=col: e.tensor_scalar(DG.t[:, k, :], self.ident, cols.t[:, col:col + 1], None, ALU.mult),
                             reads=[self.cst.b, cols.b, DG.b], writes=[DG.b])
                    for tb in range(NTB):
                        p = self.ps()
                        for k in range(8):
                            self.mm(p.t[:, :], W.t[:, k, :], XT[:, k, tb * 512:(tb + 1) * 512], k == 0, k == 7,
                                    reads=[W.b] + self.xt_reads(tb), writes=[p.b])
                        S.op("act", lambda e, p=p, tb=tb: e.activation(CI.t[:, 3 + tb * 512:3 + (tb + 1) * 512], p.t[:, :], AF.Copy), reads=[p.b], writes=[CI.b])

                def P1(i4):
                    CI = CIs[i4 % 2]
                    DG = st.pop(i4)
                    fc = fcs[i4]
                    T, Tb, ti = dst_of(i4)
                    for tb in range(NTB):
                        p = self.ps()
                        for k in range(4):
                            self.mm(p.t[:, :], DG.t[:, k, :], CI.t[:, k + tb * 512:k + (tb + 1) * 512], k == 0, k == 3,
                                    reads=[DG.b, CI.b], writes=[p.b])
                        S.op("act", lambda e, p=p, tb=tb: e.activation(T[:, ti, tb * 512:(tb + 1) * 512], p.t[:, :], AF.Silu, bias=cols.t[:, 128 + fc:129 + fc], scale=1.0),
                             reads=[p.b, cols.b], writes=[Tb[ti][tb]])
                pipeline(4, [P0, P1])
                for t4 in range(4):
                    for T, Tb, ti, dst, w in ((FX, FXb, 0, XTK, 0), (FX, FXb, 1, XTK, 128), (FBC, FBCb, 0, BTK, 0)):
                        p = self.ps()
                        pv = p.t[:].bitcast(BF16)
                        for tt in range(4):
                            t = t4 * 4 + tt
                            S.op("pe", lambda e, pv=pv, tt=tt, t=t, T=T, ti=ti: e.transpose(pv[:, tt * 128:(tt + 1) * 128], T[:, ti, t * 128:(t + 1) * 128], self.ident),
                                 reads=[Tb[ti][t4], self.cst.b], writes=[p.b])
                        S.op("dve", lambda e, pv=pv, dst=dst, w=w, t4=t4: e.tensor_copy(dst.t[:, t4 * 4:t4 * 4 + 4, w:w + 128], pv[:, 0:512].rearrange("p (t n) -> p t n", t=4)),
                             reads=[p.b], writes=[dst.b])

            YT = S.alloc("YT", [128, 2, SEQ], BF16)
            YTb = [S.buf("YT%d" % tb) for tb in range(NTB)]
            HS = S.tile("HS", [128, 256], F32)
            HSbs = [S.tile("HSb%d" % i, [128, 256], BF16) for i in range(4)]
            CBr = Rot([S.tile("CBm%d" % i, [128, 128], F32) for i in range(3)])
            TMr = Rot([S.tile("TMs%d" % i, [128, 128], F32) for i in range(8)])
            WTr = Rot([S.tile("WTs%d" % i, [128, 128], BF16) for i in range(8)])
            YGr = Rot([S.tile("YG%d" % i, [128, 256], F32) for i in range(3)])
            Y2r = Rot([S.tile("Y2%d" % i, [128, 256], F32) for i in range(2)])
            ZSr = Rot([S.tile("ZSc%d" % i, [128, 256], BF16) for i in range(4)])
            THr = Rot([S.tile("THc%d" % i, [128, 256], BF16) for i in range(1)])
            XWr = Rot([S.tile("XW%d" % i, [128, 256], BF16) for i in range(3)])
            YNr = Rot([S.tile("YN%d" % i, [128, 256], BF16) for i in range(3)])
            rsR = Rot([S.tile("rss%d" % i, [128, 4], F32) for i in range(3)])
            bc4 = lambda T, ch: T.t[:, ch, 4 * g:4 * g + 4].unsqueeze(2).to_broadcast([128, 4, 64])
            v4 = lambda ap: ap.rearrange("p (h d) -> p h d", h=4)
            WTs = {}
            ZSs = {}

            PHs = {}
            YGs = {}
            YNs = {}
            CBs = {}
            TMs = {}
            XWs = {}
            RSs = {}
            phbanks = [self.banks[6], self.banks[7]]

            def T0(ch):
                tsl = slice(ch * 128, (ch + 1) * 128)
                pcb = self.ps()
                self.mm(pcb.t[:, 0:128], FBC[:, 0, tsl], FBC[:, 1, tsl], True, True, reads=[FBCb[0][ch // 4], FBCb[1][ch // 4]], writes=[pcb.b])
                CBm = CBr.next()
                CBs[ch] = CBm
                S.op("dve", lambda e: e.tensor_tensor(CBm.t[:], pcb.t[:, 0:128], self.tri32, ALU.mult), reads=[pcb.b, self.cst32.b], writes=[CBm.b])
                for h in range(4):
                    hh = 4 * g + h
                    pseg = self.ps()
                    self.mm(pseg.t[:, 0:128], DTA.t[:, ch, hh:hh + 1].to_broadcast([128, 128]), self.tri32, True, True, reads=[DTA.b, self.cst32.b], writes=[pseg.b])
                    TM = TMr.next()
                    TMs[(ch, h)] = TM
                    S.op("dve", lambda e, TM=TM, pseg=pseg, hh=hh: e.tensor_scalar(TM.t[:], pseg.t[:, 0:128], ACU.t[:, ch, hh:hh + 1], 0.0, ALU.subtract, ALU.min),
                         reads=[pseg.b, ACU.b], writes=[TM.b])
                    S.op("act", lambda e, TM=TM: e.activation(TM.t[:], TM.t[:], AF.Exp), reads=[TM.b], writes=[TM.b])
                pz = self.ps()
                for k in range(8):
                    self.mm(pz.t[:, 0:256], XT[:, k, tsl], WZ.t[:, k, :], k == 0, k == 7, reads=[WZ.b, self.XTb[ch]], writes=[pz.b])
                ZS = ZSr.next()
                TH = THr.next()
                S.op("act", lambda e: e.activation(TH.t[:], pz.t[:, 0:256], AF.Tanh, scale=0.5), reads=[pz.b], writes=[TH.b])
                S.op("dve", lambda e: e.scalar_tensor_tensor(ZS.t[:], TH.t[:], 1.0, pz.t[:, 0:256], ALU.add, ALU.mult), reads=[TH.b, pz.b], writes=[ZS.b])
                ZSs[ch] = ZS
                if ch < NT - 1:
                    XW = XWr.next()
                    XWs[ch] = XW
                    S.op("pool", lambda e: e.tensor_tensor(v4(XW.t[:]), v4(XTK.t[:, ch, :]), bc4(W2, ch), ALU.mult), reads=[XTK.b, W2.b], writes=[XW.b])

            def T1(ch):
                CBm = CBs.pop(ch)
                for h in range(4):
                    hh = 4 * g + h
                    TM = TMs.pop((ch, h))
                    WT = WTr.next()
                    S.op("dve", lambda e, TM=TM, WT=WT, hh=hh: e.scalar_tensor_tensor(WT.t[:], TM.t[:], DT.t[:, ch, hh:hh + 1], CBm.t[:], ALU.mult, ALU.mult),
                         reads=[TM.b, DT.b, CBm.b], writes=[WT.b])
                    WTs[(ch, h)] = WT
                if ch < NT - 1:
                    XW = XWs.pop(ch)
                    ph = phbanks[ch % 2]
                    self.mm(ph.t[:, 0:256], BTK.t[:, ch, :], XW.t[:], True, True, reads=[BTK.b, XW.b], writes=[ph.b])
                    PHs[ch] = ph

            def T2(ch):
                tsl = slice(ch * 128, (ch + 1) * 128)
                py = self.ps()
                for h in range(4):
                    WT = WTs.pop((ch, h))
                    self.mm(py.t[:, h * 64:(h + 1) * 64], WT.t[:], XTK.t[:, ch, h * 64:(h + 1) * 64], True, False, reads=[WT.b, XTK.b], writes=[py.b], sgc=True)
                    self.mm(py.t[:, h * 64:(h + 1) * 64], DGD.t[:, h, :], XTK.t[:, ch, h * 64:(h + 1) * 64], False, True, reads=[DGD.b, XTK.b], writes=[py.b], sgc=True)
                YG = YGr.next()
                YGs[ch] = YG
                if ch > 0:
                    HSb = HSbs[ch % 4]
                    po = self.ps()
                    self.mm(po.t[:, 0:256], FBC[:, 1, tsl], HSb.t[:], True, True, reads=[FBCb[1][ch // 4], HSb.b], writes=[po.b])
                    Y2 = Y2r.next()
                    S.op("dve", lambda e: e.tensor_tensor(v4(Y2.t[:]), v4(po.t[:, 0:256]), bc4(EA, ch), ALU.mult), reads=[po.b, EA.b], writes=[Y2.b])
                    S.op("dve", lambda e: e.tensor_tensor(YG.t[:], Y2.t[:], py.t[:, 0:256], ALU.add), reads=[Y2.b, py.b], writes=[YG.b])
                else:
                    S.op("act", lambda e: e.activation(YG.t[:], py.t[:, 0:256], AF.Copy), reads=[py.b], writes=[YG.b])

            def R(ch):
                if ch < NT - 1:
                    ph = PHs.pop(ch)
                    if ch == 0:
                        S.op("act", lambda e: e.activation(HS.t[:], ph.t[:, 0:256], AF.Copy), reads=[ph.b], writes=[HS.b])
                    else:
                        S.op("dve", lambda e: e.tensor_tensor(v4(HS.t[:]), v4(HS.t[:]), bc4(ET, ch), ALU.mult), reads=[HS.b, ET.b], writes=[HS.b])
                        S.op("dve", lambda e: e.tensor_tensor(HS.t[:], HS.t[:], ph.t[:, 0:256], ALU.add), reads=[HS.b, ph.b], writes=[HS.b])
                    HSn = HSbs[(ch + 1) % 4]
                    S.op("act", lambda e: e.activation(HSn.t[:], HS.t[:], AF.Copy), reads=[HS.b], writes=[HSn.b])

            def T3(ch):
                YG = YGs.pop(ch)
                ZS = ZSs.pop(ch)
                YN = YNr.next()
                YNs[ch] = YN
                S.op("pool", lambda e: e.tensor_tensor(YN.t[:], YG.t[:], ZS.t[:], ALU.mult), reads=[YG.b, ZS.b], writes=[YN.b])
                sqj = Y2r.next()
                S.op("act", lambda e: e.activation(sqj.t[:], YN.t[:], AF.Square, accum_out=SS.t[:, ch:ch + 1]), reads=[YN.b], writes=[sqj.b, SS.b])

            def T5(ch):
                tsl = slice(ch * 128, (ch + 1) * 128)
                YN = YNs.pop(ch)
                ptp = self.ps()
                ptv = ptp.t[:].bitcast(BF16)
                for c2 in range(2):
                    S.op("pe", lambda e, c2=c2: e.transpose(ptv[:, c2 * 128:(c2 + 1) * 128], YN.t[:, c2 * 128:(c2 + 1) * 128], self.ident),
                         reads=[YN.b, self.cst.b], writes=[ptp.b])
                for c2 in range(2):
                    S.op("act", lambda e, c2=c2: e.activation(YT[:, c2, tsl], ptv[:, c2 * 128:(c2 + 1) * 128], AF.Copy, scale=NG.t[:, c2:c2 + 1]),
                         reads=[ptp.b, NG.b], writes=[YTb[ch // 4]])

            self.rot_banks = list(range(6))
            def T2x(ch):
                if ch == 0:
                    R(0)
                if ch + 1 < NT:
                    R(ch + 1)
                T2(ch)
            pipeline(NT, [T0, T1, T2x, T3, T5])
            self.rot_banks = list(range(8))
            S.op("act", lambda e: e.activation(SS.t[:], SS.t[:], AF.Sqrt, bias=4e-5, scale=1.0 / 256.0), reads=[SS.b], writes=[SS.b])
            S.op("dve", lambda e: e.reciprocal(SS.t[:], SS.t[:]), reads=[SS.b], writes=[SS.b])
            self.outproj_norm(WOg.t, [WOg.b], 2, lambda k, t: (YT[:, k, t * 128:(t + 1) * 128], [YTb[t // 4]]),
                              first=(g == 0), last=(g == 7), bias=None, want_xt=True, store_b=None, rowscale=SS)

    def build(self):
        S = self.S
        nst = len(self.stages)
        for b in range(self.nseq):
            with S.scope():
                self.load_x(b)
                for t in range(NT):
                    self.make_xt(t)
                for si, (li, sub) in enumerate(self.stages):
                    lastst = si == nst - 1
                    if sub == "a":
                        if li % 3 == 0:
                            self.conformer(li, li // 3)
                        elif li % 3 == 2:
                            self.diffattn(li, li // 3, b)
                        else:
                            self.ssd(li, li // 3)
                        if lastst:
                            for t in range(NT):
                                self.store_x(b, t)
                    elif sub == "b":
                        self.xattn(li, b)
                        if lastst:
                            for t in range(NT):
                                self.store_x(b, t)
                    else:
                        self.ffn(li, want_xt=not lastst, store_b=b if lastst else None)
        S.finish()
        return self.nc


def host_consts():
    c = np.zeros((128, NCST * 128), np.float32)
    c[:, 0:128] = np.eye(128, dtype=np.float32)
    c[:, 128:256] = 1.0
    c[:, 256:384] = 1.0 / 1024.0
    prot = np.zeros((128, 128), np.float32)
    for base in (0, 64):
        for i in range(8):
            prot[base + i, base + i + 8] = 1.0
            prot[base + i + 8, base + i] = 1.0
    c[:, 384:512] = prot
    c[0:64, 512:640] = 1.0
    c[64:128, 640:768] = 1.0
    k = np.arange(128)[:, None]
    q = np.arange(128)[None, :]
    c[:, 768:896] = np.where(q < k, -30000.0, 0.0)
    return c


def host_consts_f():
    c = np.zeros((128, 16), np.float32)
    for p in range(128):
        d = p % 64
        if d < 16:
            i = d % 8
            c[p, 0] = ROPE_THETA ** (-(2.0 * i) / 16.0) / (2.0 * np.pi)
            sgn = -1.0 if d < 8 else 1.0
        else:
            sgn = 1.0
        c[p, 1] = sgn * 2.0 * np.pi
    return c


def colpack(v):
    return np.ascontiguousarray(v.reshape(-1, 128).T)


def host_layout(inputs):
    m = {}
    m["cst"] = host_consts()
    m["cstf"] = host_consts_f()
    c32 = np.zeros((128, 384), np.float32)
    c32[:, 256:384] = np.eye(128, dtype=np.float32)
    c32[:, 0:128] = np.triu(np.ones((128, 128), np.float32))
    c32[:, 128:256] = 1.0
    m["cst32"] = c32
    mbc = np.zeros((1, 128, 160), np.float32)
    for k in range(4):
        mbc[0, :, k * 32:(k + 1) * 32] = colpack(inputs["mb_w_conv"][0, k])
    mbc[0, :, 128:160] = colpack(inputs["mb_b_conv"][0])
    m["mb_cols"] = mbc
    m["mb_ngcol"] = np.ascontiguousarray(colpack(inputs["mb_norm_g"][0])[None])
    m["mb_hv"] = np.ascontiguousarray(np.concatenate([inputs["mb_dt_bias"], inputs["mb_a_log"], inputs["mb_d"]], 1))
    m["da_lqk"] = np.ascontiguousarray(np.concatenate([inputs[k] for k in ("da_lq1", "da_lk1", "da_lq2", "da_lk2")], 0))
    cvc = np.zeros((2, 128, CV_N), np.float32)
    for j in range(2):
        cvc[j, :, CV_BIN:CV_BIN + 16] = colpack(inputs["cv_b_in"][j])
        wd = inputs["cv_w_dw"][j]
        for k in range(CONVW):
            cvc[j, :, CV_WDW + k * 8:CV_WDW + (k + 1) * 8] = colpack(wd[k])
        cvc[j, :, CV_BDW:CV_BDW + 8] = colpack(inputs["cv_b_dw"][j])
        cvc[j, :, CV_LNG:CV_LNG + 8] = colpack(inputs["cv_ln_g"][j])
        cvc[j, :, CV_LNB:CV_LNB + 8] = colpack(inputs["cv_ln_b"][j])
    m["cv_cols"] = cvc
    for k in ("mb_w_in", "mb_w_out", "mb_norm_g", "da_w_qkv", "da_w_out", "da_subln_g", "cv_w_in", "cv_w_out", "cv_b_out", "xa_w_q", "xa_w_kv", "xa_w_out", "ff_w_in", "ff_w_out", "ln_g", "ln_b"):
        m[k] = np.ascontiguousarray(inputs[k])
    return m


ALL_STAGES = [(li, s) for li in range(DEPTH) for s in "abc"]


def run(inputs, nseq, stages, batch_ids, trace=False):
    nc = Builder(nseq, stages).build()
    shared = host_layout(inputs)
    in_maps = []
    for ids in batch_ids:
        m = dict(shared)
        m["x"] = np.ascontiguousarray(inputs["x"][ids])
        m["mem"] = np.ascontiguousarray(inputs["mem"][ids])
        m["pos"] = np.ascontiguousarray(inputs["positions"][ids])
        in_maps.append(m)
    res = run_bass_kernel_spmd(nc, in_maps, core_ids=list(range(len(batch_ids))), trace=trace)
    return res


def kernel(**inputs):
    ncore = 8
    nseq = inputs["x"].shape[0] // ncore
    ids = [list(range(c * nseq, (c + 1) * nseq)) for c in range(ncore)]
    res = run(inputs, nseq, ALL_STAGES, ids)
    out = np.empty(inputs["x"].shape, np.float32)
    for c in range(ncore):
        out[ids[c]] = res.results[c]["out"]
    return out
```
